# Optimizing a Trainium2 kernel written in Bass

```python
import math
import jax, jax.numpy as jnp
from jax import lax
import numpy as np

D_MODEL = 1024
BATCH = 4
SEQ = 4096
DEPTH = 1

ATTN_WIDTH = D_MODEL // 2
CONV_WIDTH = D_MODEL - ATTN_WIDTH
N_ATTN_HEADS = 4
ATTN_HEAD_DIM = ATTN_WIDTH // N_ATTN_HEADS // 2
CONV_GROUPS = 8
CONV_K = 3
D_FF = 2816
FFN_CONV_K = 3
Q_BLOCK = 128
LN_EPS = 1e-5
RMS_EPS = 1e-5
DEEPNORM_ALPHA = (2.0 * DEPTH) ** 0.25
DEEPNORM_BETA = (8.0 * DEPTH) ** -0.25
QK_COLS = N_ATTN_HEADS * 2 * ATTN_HEAD_DIM
IN_COLS = 2 * QK_COLS + ATTN_WIDTH + 3 * CONV_WIDTH

kernel_name = "hymba_diffattn_shortconv_convglu_deepnorm"


def lambda_init_fn(layer_idx):
    return 0.8 - 0.6 * math.exp(-0.3 * layer_idx)


def layer_norm(x, g, b):
    xf = x.astype(jnp.float32)
    mu = jnp.mean(xf, axis=-1, keepdims=True)
    var = jnp.mean(jnp.square(xf - mu), axis=-1, keepdims=True)
    y = (xf - mu) * lax.rsqrt(var + LN_EPS) * g.astype(jnp.float32) + b.astype(jnp.float32)
    return y.astype(x.dtype)


def causal_dwconv(u, w):
    k, c = w.shape
    return lax.conv_general_dilated(
        u, w[:, None, :].astype(u.dtype), window_strides=(1,), padding=[(k - 1, 0)],
        dimension_numbers=("NWC", "WIO", "NWC"), feature_group_count=c)


def diff_attention(q, k, v, lam):
    bsz, s, h, _, dh = q.shape
    nb = s // Q_BLOCK
    qb = q.reshape(bsz, nb, Q_BLOCK, h, 2, dh).transpose(1, 0, 3, 4, 2, 5)
    kt = k.transpose(0, 2, 3, 1, 4)
    vt = v.transpose(0, 2, 1, 3)
    kpos = jnp.arange(s)
    scale = dh ** -0.5

    def block(args):
        q_blk, i = args
        sc = jnp.einsum('bhmqd,bhmkd->bhmqk', q_blk, kt).astype(jnp.float32) * scale
        qpos = i * Q_BLOCK + jnp.arange(Q_BLOCK)
        mask = kpos[None, :] <= qpos[:, None]
        sc = jnp.where(mask, sc, -jnp.inf)
        p = jax.nn.softmax(sc, axis=-1)
        a = p[:, :, 0] - lam * p[:, :, 1]
        return jnp.einsum('bhqk,bhkd->bhqd', a.astype(vt.dtype), vt)

    o = lax.map(block, (qb, jnp.arange(nb)))
    return o.transpose(1, 0, 3, 2, 4).reshape(bsz, s, h, 2 * dh)


def hybrid_mixer(x, w_in, lq1, lk1, lq2, lk2, attn_norm_g, conv_w, w_out, layer_idx):
    bsz, s, _ = x.shape
    proj = jnp.einsum('bsd,dc->bsc', x, w_in)
    o1 = QK_COLS
    o2 = o1 + QK_COLS
    o3 = o2 + ATTN_WIDTH
    o4 = o3 + CONV_WIDTH
    o5 = o4 + CONV_WIDTH
    q = proj[..., :o1].reshape(bsz, s, N_ATTN_HEADS, 2, ATTN_HEAD_DIM)
    k = proj[..., o1:o2].reshape(bsz, s, N_ATTN_HEADS, 2, ATTN_HEAD_DIM)
    v = proj[..., o2:o3].reshape(bsz, s, N_ATTN_HEADS, 2 * ATTN_HEAD_DIM)
    gate_b = proj[..., o3:o4]
    gate_c = proj[..., o4:o5]
    hc = proj[..., o5:]

    lam_init = lambda_init_fn(layer_idx)
    lam = (jnp.exp(jnp.sum(lq1.astype(jnp.float32) * lk1.astype(jnp.float32)))
           - jnp.exp(jnp.sum(lq2.astype(jnp.float32) * lk2.astype(jnp.float32))) + lam_init)
    o = diff_attention(q, k, v, lam).astype(jnp.float32)
    o = o * lax.rsqrt(jnp.mean(jnp.square(o), axis=-1, keepdims=True) + RMS_EPS)
    o = o * attn_norm_g.astype(jnp.float32) * (1.0 - lam_init)
    attn_out = o.reshape(bsz, s, ATTN_WIDTH).astype(x.dtype)

    conv_out = gate_b * causal_dwconv(gate_c * hc, conv_w)

    cat = jnp.concatenate([attn_out, conv_out], axis=-1)
    return jnp.einsum('bsc,cd->bsd', cat, w_out)


def conv_glu(x, w_up, conv_w, conv_b, w_down):
    up = jnp.einsum('bsd,df->bsf', x, w_up)
    g, val = up[..., :D_FF], up[..., D_FF:]
    g = causal_dwconv(g, conv_w) + conv_b
    return jnp.einsum('bsf,fd->bsd', jax.nn.silu(g) * val, w_down)


def setup_inputs(seed: int = 0) -> dict:
    key = jax.random.key(seed)
    ks = jax.random.split(key, 20)
    L = DEPTH
    x = jax.random.normal(ks[0], (BATCH, SEQ, D_MODEL), jnp.float32)
    col_scale = jnp.concatenate([
        jnp.ones((2 * QK_COLS,), jnp.float32),
        jnp.full((ATTN_WIDTH,), DEEPNORM_BETA, jnp.float32),
        jnp.ones((2 * CONV_WIDTH,), jnp.float32),
        jnp.full((CONV_WIDTH,), DEEPNORM_BETA, jnp.float32)])
    w_in = jax.random.normal(ks[1], (L, D_MODEL, IN_COLS), jnp.float32) * D_MODEL ** -0.5 * col_scale
    lambda_q1 = 0.1 * jax.random.normal(ks[2], (L, ATTN_HEAD_DIM), jnp.float32)
    lambda_k1 = 0.1 * jax.random.normal(ks[3], (L, ATTN_HEAD_DIM), jnp.float32)
    lambda_q2 = 0.1 * jax.random.normal(ks[4], (L, ATTN_HEAD_DIM), jnp.float32)
    lambda_k2 = 0.1 * jax.random.normal(ks[5], (L, ATTN_HEAD_DIM), jnp.float32)
    attn_norm_g = 1.0 + 0.02 * jax.random.normal(ks[6], (L, 2 * ATTN_HEAD_DIM), jnp.float32)
    conv_w = jax.random.normal(ks[7], (L, CONV_K, CONV_WIDTH), jnp.float32) * CONV_K ** -0.5
    w_out = jax.random.normal(ks[8], (L, D_MODEL, D_MODEL), jnp.float32) * D_MODEL ** -0.5 * DEEPNORM_BETA
    ln1_g = 1.0 + 0.02 * jax.random.normal(ks[9], (L, D_MODEL), jnp.float32)
    ln1_b = 0.02 * jax.random.normal(ks[10], (L, D_MODEL), jnp.float32)
    ffn_w_up = jax.random.normal(ks[11], (L, D_MODEL, 2 * D_FF), jnp.float32) * D_MODEL ** -0.5 * DEEPNORM_BETA
    ffn_conv_w = jax.random.normal(ks[12], (L, FFN_CONV_K, D_FF), jnp.float32) * FFN_CONV_K ** -0.5
    ffn_conv_b = 0.02 * jax.random.normal(ks[13], (L, D_FF), jnp.float32)
    ffn_w_down = jax.random.normal(ks[14], (L, D_FF, D_MODEL), jnp.float32) * D_FF ** -0.5 * DEEPNORM_BETA
    ln2_g = 1.0 + 0.02 * jax.random.normal(ks[15], (L, D_MODEL), jnp.float32)
    ln2_b = 0.02 * jax.random.normal(ks[16], (L, D_MODEL), jnp.float32)
    return {"x": x, "w_in": w_in, "lambda_q1": lambda_q1, "lambda_k1": lambda_k1,
            "lambda_q2": lambda_q2, "lambda_k2": lambda_k2, "attn_norm_g": attn_norm_g,
            "conv_w": conv_w, "w_out": w_out, "ln1_g": ln1_g, "ln1_b": ln1_b,
            "ffn_w_up": ffn_w_up, "ffn_conv_w": ffn_conv_w, "ffn_conv_b": ffn_conv_b,
            "ffn_w_down": ffn_w_down, "ln2_g": ln2_g, "ln2_b": ln2_b}


def reference(x, w_in, lambda_q1, lambda_k1, lambda_q2, lambda_k2, attn_norm_g, conv_w, w_out,
              ln1_g, ln1_b, ffn_w_up, ffn_conv_w, ffn_conv_b, ffn_w_down, ln2_g, ln2_b):
    h = x
    for l in range(DEPTH):
        mix = hybrid_mixer(h, w_in[l], lambda_q1[l], lambda_k1[l], lambda_q2[l], lambda_k2[l],
                           attn_norm_g[l], conv_w[l], w_out[l], l)
        h = layer_norm(DEEPNORM_ALPHA * h + mix, ln1_g[l], ln1_b[l])
        f = conv_glu(h, ffn_w_up[l], ffn_conv_w[l], ffn_conv_b[l], ffn_w_down[l])
        h = layer_norm(DEEPNORM_ALPHA * h + f, ln2_g[l], ln2_b[l])
    return h
```

```python
import math
import numpy as np
import concourse.bass as bass
import concourse.mybir as mybir
from concourse.bass_utils import run_bass_kernel_spmd

F32 = mybir.dt.float32
BF16 = mybir.dt.bfloat16
AF = mybir.ActivationFunctionType
ALU = mybir.AluOpType
AX = mybir.AxisListType

D = 1024
SEQ = 4096
NB = 4
KC = 8
NJ = 8
CW = 256
QW = 258
XW = 260
DFF = 2816
NHC = 22
GRP = 4
GROUPS = [(0, 2), (2, 4), (6, 4), (10, 4), (14, 4), (18, 4)]
ALPHA = (2.0 * 1) ** 0.25
LAM_INIT = 0.8 - 0.6 * math.exp(0.0)
LN_EPS = 1e-5
RMS_EPS = 1e-5
NEG = -30000.0

C_LN1G, C_LN1B, C_LN2G, C_LN2B = 0, 8, 16, 24
C_ANG = 32
C_CW = 33
C_FCW = 45
C_FCB = 111
C_LAM = 133
C_HFLAG = 389
NP_ = 397
V_AG1, V_AB1, V_GSC, V_NEGLAM, V_T0, V_T1, V_E0, V_E1 = 0, 8, 16, 17, 18, 19, 20, 21
NV_ = 24

SEM_ROLL = 1000


class DSem:
    def __init__(self, sem):
        self.sem = sem
        self.count = 0


class Op:
    __slots__ = ("eng", "fn", "deps", "dsem", "ticket", "signal", "clock", "idx", "ndma")


class Sched:
    ENGS = ("pe", "act", "dve", "pool", "sp")

    def __init__(self, nc, same_engine_sync=True):
        self.nc = nc
        self.ops = []
        self.lastw = {}
        self.rd_eng = {}
        self.rd_dma = {}
        self.same_engine_sync = same_engine_sync

    def add(self, eng, fn, reads=(), writes=(), dsem=None, ndma=1):
        op = Op()
        op.eng, op.fn, op.dsem, op.ndma = eng, fn, dsem, ndma
        op.idx = len(self.ops)
        op.ticket = None
        op.signal = dsem is not None
        op.clock = None
        deps = set()
        for r in reads:
            w = self.lastw.get(r)
            if w is not None:
                deps.add(w)
        for w_ in writes:
            w = self.lastw.get(w_)
            if w is not None:
                deps.add(w)
            for i in self.rd_eng.get(w_, {}).values():
                deps.add(i)
            for i in self.rd_dma.get(w_, ()):
                deps.add(i)
        for w_ in writes:
            self.lastw[w_] = op.idx
            self.rd_eng[w_] = {}
            self.rd_dma[w_] = []
        for r in reads:
            if r in writes:
                continue
            if dsem is not None:
                self.rd_dma.setdefault(r, []).append(op.idx)
            else:
                self.rd_eng.setdefault(r, {})[eng] = op.idx
        keep = set()
        best = {}
        bestd = {}
        for d in deps:
            p = self.ops[d]
            if p.dsem is not None:
                k_ = id(p.dsem)
                if k_ not in bestd or bestd[k_] < d:
                    bestd[k_] = d
                continue
            if p.eng == eng and dsem is None and (eng == "pe" or not self.same_engine_sync):
                continue
            if p.eng not in best or best[p.eng] < d:
                best[p.eng] = d
        keep.update(best.values())
        keep.update(bestd.values())
        op.deps = keep
        for d in keep:
            self.ops[d].signal = True
        self.ops.append(op)
        return op

    def emit(self, block, sems):
        cnt = {e: 0 for e in self.ENGS}
        semlist = {e: [] for e in self.ENGS}
        for op in self.ops:
            if op.dsem is not None:
                op.dsem.count += 16 * op.ndma
                op.ticket = ("d", op.dsem.sem, op.dsem.count)
            elif op.signal:
                c = cnt[op.eng]
                si, v = divmod(c, SEM_ROLL)
                if si >= len(semlist[op.eng]):
                    semlist[op.eng].append(next(sems))
                op.ticket = ("e", semlist[op.eng][si], v + 1)
                cnt[op.eng] = c + 1
        streams = {e: [] for e in self.ENGS}
        clock = {e: {} for e in self.ENGS}
        nwaits = 0
        for op in self.ops:
            ck = clock[op.eng]
            st = streams[op.eng]
            for d in sorted(op.deps):
                p = self.ops[d]
                t = p.ticket
                key = id(t[1])
                if ck.get(key, 0) >= t[2]:
                    continue
                st.append(("w", t[1], t[2]))
                nwaits += 1
                ck[key] = t[2]
                if p.clock is not None:
                    for k, v in p.clock.items():
                        if ck.get(k, 0) < v:
                            ck[k] = v
            st.append(("o", op))
            if op.signal:
                op.clock = dict(ck)
        self.nwaits = nwaits
        print("signal counts", cnt, "dma max", max([op.ticket[2] for op in self.ops if op.dsem is not None] + [0]))
        self.streams = streams

        def run(e):
            def body(eng):
                for item in streams[e]:
                    if item[0] == "w":
                        eng.wait_ge(item[1], item[2])
                    else:
                        op = item[1]
                        ins = op.fn(eng)
                        if op.dsem is not None:
                            if not isinstance(ins, (list, tuple)):
                                ins = [ins]
                            assert len(ins) == op.ndma
                            for i_ in ins:
                                i_.then_inc(op.dsem.sem, 16)
                        elif op.signal:
                            ins.then_inc(op.ticket[1], 1)
            return body

        block.tensor(run("pe"))
        block.scalar(run("act"))
        block.vector(run("dve"))
        block.gpsimd(run("pool"))
        block.sync(run("sp"))


def MM(out, lhsT, rhs, start=True, stop=True):
    return lambda e: e.matmul(out, lhsT=lhsT, rhs=rhs, start=start, stop=stop)


def ACT(out, in_, func, scale=None, bias=None):
    kw = {}
    if scale is not None:
        kw["scale"] = scale
    if bias is not None:
        kw["bias"] = bias
    return lambda e: e.activation(out=out, in_=in_, func=func, **kw)


def TT(out, a, b, op):
    return lambda e: e.tensor_tensor(out=out, in0=a, in1=b, op=op)


def TS(out, a, s1, op0, s2=None, op1=None):
    if op1 is None:
        return lambda e: e.tensor_scalar(out=out, in0=a, scalar1=s1, scalar2=None, op0=op0)
    return lambda e: e.tensor_scalar(out=out, in0=a, scalar1=s1, scalar2=s2, op0=op0, op1=op1)


def STT(out, a, s, b, op0, op1):
    return lambda e: e.scalar_tensor_tensor(out=out, in0=a, scalar=s, in1=b, op0=op0, op1=op1)


def CP(out, in_):
    return lambda e: e.tensor_copy(out=out, in_=in_)


def RECIP(out, in_):
    return lambda e: e.reciprocal(out=out, in_=in_)


def DMA(out, in_):
    return lambda e: e.dma_start(out=out, in_=in_)


def DMAS(pairs):
    return lambda e: [e.dma_start(out=o, in_=i) for (o, i) in pairs]


class Arena:
    def __init__(self, nc, base, top):
        self.nc, self.cur, self.top = nc, base, top
        self.n = 0

    def alloc(self, name, shape, dtype):
        esz = 4 if dtype == F32 else 2
        size = esz
        for s in shape[1:]:
            size *= s
        off = (self.cur + 31) // 32 * 32
        self.cur = off + size
        self.last_off = off
        assert self.cur <= self.top, (name, self.cur, self.top)
        self.n += 1
        return self.nc.alloc_sbuf_tensor_at("%s_%d" % (name, self.n), list(shape), dtype, offset=off)


def build_program(nj=NJ, debug=False, limit=99):
    import os
    KX = set(os.environ.get('KX', '').split(','))
    nc = bass.Bass("TRN2", target_bir_lowering=False)

    def dram(name, shape, dtype, kind):
        return nc.dram_tensor(name, list(shape), dtype, kind=kind).ap()

    xo = dram("xo", [NJ, 128, KC * XW], F32, "ExternalInput")
    xt = dram("xt", [NJ, 128, KC * CW], F32, "ExternalInput")
    pvd = dram("pv", [128, NP_], F32, "ExternalInput")
    cstd = dram("cst", [128, 128 + 9 * QW], F32, "ExternalInput")
    w_in = dram("w_in", [D, 3072], F32, "ExternalInput")
    w_out = dram("w_out", [D, D], F32, "ExternalInput")
    w_up = dram("w_up", [D, 2 * DFF], F32, "ExternalInput")
    w_down = dram("w_down", [DFF, D], F32, "ExternalInput")
    outd = [dram("out%d" % j, [D, CW], F32, "ExternalOutput") for j in range(NJ)]
    scr_b = [dram("scr_b%d" % j, [128, KC * QW], BF16, "ExternalOutput") for j in range(NJ)]
    scr_y = [dram("scr_y%d" % j, [128, KC * CW], F32, "ExternalOutput") for j in range(NJ)]
    if debug:
        dbg_cat = dram("dbg_cat", [NJ, 128, KC * QW], F32, "ExternalOutput")
        dbg_h1 = dram("dbg_h1", [NJ, 128, KC * CW], F32, "ExternalOutput")

    nsem = [0]

    def semgen():
        while True:
            nsem[0] += 1
            yield nc.alloc_semaphore("sm%d" % nsem[0])

    sg = semgen()

    def dsem():
        return DSem(next(sg))

    S = Sched(nc)
    base0 = (nc.sbuf_base + 63) // 64 * 64
    top0 = nc.sbuf_top - 64
    arena_t = nc.alloc_sbuf_tensor("arena", [128, top0 - base0], mybir.dt.uint8)
    A = Arena(nc, base0, top0)
    ps = nc.alloc_psum_tensor("ps", [128, 8, 512], F32)

    pv = A.alloc("pv", [128, NP_], F32)
    dv = A.alloc("dv", [128, NV_], F32)
    lamt = A.alloc("lamt", [128, 128], F32)
    ones_f = A.alloc("ones_f", [128, 128], F32)
    ones_b = A.alloc("ones_b", [128, 128], BF16)
    cstb = A.alloc("cstb", [128, 128 + 9 * QW], BF16)
    ident_b = cstb[:, 0:128]

    def maskb(ms, r):
        o = 128 + (ms * 4 + r) * QW
        return cstb[:, o:o + QW]

    mark_persist = A.cur

    winb = A.alloc("winb", [128, KC, 3072], BF16)
    winb_off = A.last_off
    woutb = A.alloc("woutb", [128, KC, D], BF16)
    KT = A.alloc("KT", [128, 4, SEQ], BF16)
    Vt = A.alloc("Vt", [128, 32, 512], BF16)
    xof = A.alloc("xof", [128, KC, XW], F32)
    xob = [A.alloc("xob", [128, KC, XW], BF16) for _ in range(2)]
    xtb = [A.alloc("xtb", [128, KC, CW], BF16) for _ in range(2)]
    qTz = A.alloc("qTz", [128, 4, 2, QW], BF16)
    catT = A.alloc("catT", [128, KC, QW], BF16)
    hc_sb = [A.alloc("hc_sb", [128, XW], F32) for _ in range(2)]
    u_sb = [A.alloc("u_sb", [128, XW], F32) for _ in range(2)]
    acc_sb = [A.alloc("acc_sb", [128, QW], F32) for _ in range(2)]
    pbuf = [A.alloc("pbuf", [128, 2, QW], BF16) for _ in range(3)]
    tmpW = [A.alloc("tmpW", [128, 2, QW], F32) for _ in range(6)]
    tmpW_off = A.last_off - 5 * 2 * QW * 4 - 5 * 32

    def TH(k, h):
        return (tmpW[k][:, h, :], ("tmp", k, h))
    y1 = A.alloc("y1", [128, KC, QW], F32)
    lnt = [TH(0, 0), TH(0, 1), TH(1, 0), TH(1, 1), TH(2, 0)]
    tt_ = [TH(3, 0), TH(3, 1)]
    sqr = [TH(4, 0), TH(4, 1)]
    h1bs = A.alloc("h1bs", [128, KC, QW], BF16)
    print("phase1 sbuf end", A.cur, "of", top0)
    p1_end = A.cur
    A.cur = winb_off
    wg, wv, wd = [], [], []
    for _ in range(2):
        wg.append(A.alloc("wg", [128, KC, 512], BF16))
        wv.append(A.alloc("wv", [128, KC, 512], BF16))
        wd.append(A.alloc("wd", [128, GRP, D], BF16))
    wff_end = A.cur
    assert wff_end <= winb_off + KC * 3072 * 2
    A.cur = wff_end
    h1T = A.alloc("h1T", [128, NJ, KC, QW], BF16)
    y2 = A.alloc("y2", [128, NJ, KC, CW], F32)
    actT = [A.alloc("actT", [128, GRP, CW], BF16) for _ in range(2)]
    ca = [A.alloc("ca", [128, CW], F32) for _ in range(3)]
    cs = [A.alloc("cs", [128, CW], F32) for _ in range(3)]
    dcnt = [0]
    sqr2 = [A.alloc("sqr2", [128, CW], F32) for _ in range(2)]
    lnt2 = [A.alloc("lnt2", [128, CW], F32) for _ in range(5)]
    tt2 = [A.alloc("ttmp2", [128, CW], F32) for _ in range(2)]
    print("phase2 sbuf end", A.cur, "of", top0)
    p2_end = A.cur
    ds_h1T = [dsem() for _ in range(nj)]
    ds_y2 = [dsem() for _ in range(nj)]
    ds_out = dsem()
    p1_done = [("scr_b", nj - 1), ("scr_y", nj - 1)]
    assert A.cur <= tmpW_off, (A.cur, tmpW_off)
    dead_keys = ["wout", "xof"] + [("cat", k) for k in range(KC)] + [("qT", h) for h in range(4)] + [("xob", 0), ("xob", 1), ("xtb", 0), ("xtb", 1)]
    A.cur = p1_end

    ds_pv, ds_cst = dsem(), dsem()
    ds_win = {"k": dsem(), "v": dsem(), "q": dsem(), 1: dsem()}
    ds_wout = dsem()
    ds_xof = dsem()
    ds_xob = [dsem(), dsem()]
    ds_xtb = [dsem(), dsem()]
    ds_sb, ds_sy = dsem(), dsem()
    ds_dbg = dsem()

    ds_wff = [dsem(), dsem()]

    def load_group(g):
        b = g % 2
        h0, nh = GROUPS[g]
        pairs = []
        for kc in range(KC):
            pairs.append((wg[b][:, kc, 0:128 * nh], w_up[128 * kc:128 * kc + 128, 128 * h0:128 * (h0 + nh)]))
            pairs.append((wv[b][:, kc, 0:128 * nh], w_up[128 * kc:128 * kc + 128, DFF + 128 * h0:DFF + 128 * (h0 + nh)]))
        for i in range(nh):
            pairs.append((wd[b][:, i, :], w_down[128 * (h0 + i):128 * (h0 + i) + 128, :]))
        S.add("pool", DMAS(pairs), writes=[("wff", b), ("win", "k"), ("win", "v"), ("win", "q"), ("win", 1)],
              dsem=ds_wff[b], ndma=len(pairs))

    def issue_reload(j):
        S.add("sp", DMA(h1T[:, j, :, :], scr_b[j].rearrange("p (k w) -> p k w", k=KC)),
              reads=[("scr_b", j)], writes=[("h1T", j)] + dead_keys, dsem=ds_h1T[j])
        S.add("sp", DMA(y2[:, j, :, :], scr_y[j].rearrange("p (k w) -> p k w", k=KC)),
              reads=[("scr_y", j)], writes=[("y2", j, m) for m in range(KC)] + dead_keys, dsem=ds_y2[j])


    rb = [0]

    def bank():
        b = rb[0]
        rb[0] = (b + 1) % 8
        return b

    S.add("sp", DMA(pv[:], pvd[:, :]), writes=["pv"], dsem=ds_pv)
    S.add("pool", DMAS([(cstb[:, 0:1024], cstd[:, 0:1024]), (cstb[:, 1024:128 + 9 * QW], cstd[:, 1024:128 + 9 * QW])]),
          writes=["cst"], dsem=ds_cst, ndma=2)
    S.add("dve", lambda e: e.memset(ones_f[:], 1.0), writes=["ones_f"])
    S.add("dve", lambda e: e.memset(ones_b[:], 1.0), writes=["ones_b"])
    S.add("dve", lambda e: e.memset(qTz[:].rearrange("p a b w -> p (a b w)"), 0.0), writes=[("qT", h) for h in range(4)])

    def xo3(j):
        return xo[j].rearrange("p (k w) -> p k w", k=KC)

    def xt3(j):
        return xt[j].rearrange("p (k w) -> p k w", k=KC)

    def issue_loads(j):
        b = j % 2
        S.add("pool", DMA(xob[b][:], xo3(j)), writes=[("xob", b)], dsem=ds_xob[b])
        S.add("pool", DMA(xtb[b][:], xt3(j)), writes=[("xtb", b)], dsem=ds_xtb[b])

    issue_loads(0)
    for grp, c0, cn in (("k", 512, 512), ("v", 1024, 512), ("q", 0, 512), (1, 1536, 1536)):
        S.add("pool", DMAS([(winb[:, kc, c0:c0 + cn], w_in[128 * kc:128 * kc + 128, c0:c0 + cn]) for kc in range(KC)]),
              writes=[("win", grp)], dsem=ds_win[grp], ndma=KC)
    S.add("sp", DMA(xof[:], xo3(0)), writes=["xof"], dsem=ds_xof)
    S.add("pool", DMAS([(woutb[:, kc, :], w_out[128 * kc:128 * kc + 128, :]) for kc in range(KC)]),
          writes=["wout"], dsem=ds_wout, ndma=KC)

    S.add("dve", TT(lamt[:, 0:64], pv[:, C_LAM:C_LAM + 64], pv[:, C_LAM + 64:C_LAM + 128], ALU.mult), reads=["pv"], writes=["lamt0"])
    S.add("dve", TT(lamt[:, 64:128], pv[:, C_LAM + 128:C_LAM + 192], pv[:, C_LAM + 192:C_LAM + 256], ALU.mult), reads=["pv"], writes=["lamt1"])
    S.add("dve", lambda e: e.reduce_sum(out=dv[:, V_T0:V_T0 + 1], in_=lamt[:, 0:64], axis=AX.X), reads=["lamt0"], writes=["dvt0"])
    S.add("dve", lambda e: e.reduce_sum(out=dv[:, V_T1:V_T1 + 1], in_=lamt[:, 64:128], axis=AX.X), reads=["lamt1"], writes=["dvt1"])
    S.add("act", ACT(dv[:, V_E0:V_E0 + 2], dv[:, V_T0:V_T0 + 2], AF.Exp), reads=["dvt0", "dvt1"], writes=["dve01"])
    S.add("dve", TT(dv[:, V_NEGLAM:V_NEGLAM + 1], dv[:, V_E1:V_E1 + 1], dv[:, V_E0:V_E0 + 1], ALU.subtract), reads=["dve01"], writes=["neglam0"])
    S.add("dve", TS(dv[:, V_NEGLAM:V_NEGLAM + 1], dv[:, V_NEGLAM:V_NEGLAM + 1], -LAM_INIT, ALU.add), reads=["neglam0"], writes=["neglam"])
    S.add("dve", TS(dv[:, V_GSC:V_GSC + 1], pv[:, C_ANG:C_ANG + 1], 1.0 - LAM_INIT, ALU.mult), reads=["pv"], writes=["gsc"])
    S.add("dve", TS(dv[:, V_AG1:V_AG1 + 8], pv[:, C_LN1G:C_LN1G + 8], ALPHA, ALU.mult), reads=["pv"], writes=["ag1"])
    S.add("dve", TS(dv[:, V_AB1:V_AB1 + 8], pv[:, C_LN1B:C_LN1B + 8], ALPHA, ALU.mult), reads=["pv"], writes=["ab1"])
    neglam = dv[:, V_NEGLAM:V_NEGLAM + 1]
    gsc = dv[:, V_GSC:V_GSC + 1]

    evac_flip = [0]

    def evac(out_ap, in_ap, reads, writes):
        evac_flip[0] ^= 1
        if evac_flip[0]:
            S.add("dve", CP(out_ap, in_ap), reads=reads, writes=writes)
        else:
            S.add("act", ACT(out_ap, in_ap, AF.Identity), reads=reads, writes=writes)

    todo = []

    def drain(n):
        for _ in range(min(n, len(todo))):
            todo.pop(0)()

    def flush():
        drain(len(todo))

    def layer_norm(src, src_keys, width, gcol, bcol, emit_out):
        b1, b2 = bank(), bank()
        s1 = ps[:, b1, 0:width]
        s2 = ps[:, b2, 0:width]
        for m in range(KC):
            sq, sqk = sqr[m % 2]
            S.add("act", ACT(sq[:, 0:width], src(m), AF.Square), reads=[src_keys(m)], writes=[sqk])
            S.add("pe", MM(s1, ones_f[:], src(m), start=(m == 0), stop=(m == KC - 1)), reads=["ones_f", src_keys(m)], writes=[("ps", b1)])
            S.add("pe", MM(s2, ones_f[:], sq[:, 0:width], start=(m == 0), stop=(m == KC - 1)), reads=["ones_f", sqk], writes=[("ps", b2)])
        (mean, kmean), (msq, kmsq), (var, kvar), (rstd, krstd), (nmr, knmr) = [(t[:, 0:width], k) for (t, k) in lnt]
        S.add("dve", TS(mean, s1, 1.0 / D, ALU.mult), reads=[("ps", b1)], writes=[kmean])
        S.add("dve", TT(msq, mean, mean, ALU.mult), reads=[kmean], writes=[kmsq])
        S.add("dve", STT(var, s2, 1.0 / D, msq, ALU.mult, ALU.subtract), reads=[("ps", b2), kmsq], writes=[kvar])
        S.add("dve", TS(var, var, LN_EPS, ALU.add), reads=[kvar], writes=[kvar])
        S.add("act", ACT(var, var, AF.Ln), reads=[kvar], writes=[kvar])
        S.add("act", ACT(rstd, var, AF.Exp, scale=-0.5), reads=[kvar], writes=[krstd])
        S.add("dve", STT(nmr, mean, -1.0, rstd, ALU.mult, ALU.mult), reads=[kmean, krstd], writes=[knmr])
        def norm_step(m, t, tk):
            def go():
                S.add("dve", TT(t, src(m), rstd, ALU.mult), reads=[src_keys(m), krstd], writes=[tk])
                S.add("dve", TT(t, t, nmr, ALU.add), reads=[tk, knmr], writes=[tk])
                emit_out(m, t, tk)
            return go

        for m in range(KC):
            t, tk = tt_[m % 2]
            todo.append(norm_step(m, t[:, 0:width], tk))

    for j in range(nj if limit >= 1 else 0):
        b = j % 2
        xb, xtb_ = xob[b], xtb[b]
        if j + 1 < nj:
            issue_loads(j + 1)
        for h in range(4):
            bk = bank()
            col = 512 + 128 * h
            for half, (src, skey) in enumerate([(lambda kc: xb[:, kc, 4:XW], ("xob", b)), (lambda kc: xtb_[:, kc, :], ("xtb", b))]):
                for kc in range(KC):
                    S.add("pe", MM(ps[:, bk, 256 * half:256 * half + 256], winb[:, kc, col:col + 128], src(kc),
                                   start=(kc == 0), stop=(kc == KC - 1)),
                          reads=[("win", "k"), skey], writes=[("ps", bk)])
            evac(KT[:, h, 512 * j:512 * j + 512], ps[:, bk, :], [("ps", bk)], [("KT", h, j)])
            drain(1)
        if limit < 1.2:
            continue
        for blk in range(4):
            bk = bank()
            for kc in range(KC):
                if blk < 2:
                    lt = xb[:, kc, 4 + 128 * blk:4 + 128 * blk + 128]
                    skey = ("xob", b)
                else:
                    lt = xtb_[:, kc, 128 * (blk - 2):128 * (blk - 2) + 128]
                    skey = ("xtb", b)
                S.add("pe", MM(ps[:, bk, :], lt, winb[:, kc, 1024:1536], start=(kc == 0), stop=(kc == KC - 1)),
                      reads=[("win", "v"), skey], writes=[("ps", bk)])
            evac(Vt[:, 4 * j + blk, :], ps[:, bk, :], [("ps", bk)], [("V", 4 * j + blk)])
            drain(1)
        if limit < 1.4:
            continue
        for h in range(4):
            bk = bank()
            for kc in range(KC):
                S.add("pe", MM(ps[:, bk, 0:QW], winb[:, kc, 128 * h:128 * h + 128], xb[:, kc, 2:XW],
                               start=(kc == 0), stop=(kc == KC - 1)),
                      reads=[("win", "q"), ("xob", b)], writes=[("ps", bk)])
            S.add("dve", CP(qTz[0:64, h, 0, :], ps[0:64, bk, 0:QW]), reads=[("ps", bk)], writes=[("qT", h)])
            S.add("act", ACT(qTz[64:128, h, 1, :], ps[64:128, bk, 0:QW], AF.Identity), reads=[("ps", bk)], writes=[("qT", h)])
            drain(1)
        if limit < 1.6:
            continue
        def conv_branch(xb=xb, b=b):
            for fc in range(4):
                bB, bC, bH = bank(), bank(), bank()
                for (bk, c0) in [(bH, 2560), (bC, 2048), (bB, 1536)]:
                    col = c0 + 128 * fc
                    for kc in range(KC):
                        S.add("pe", MM(ps[:, bk, 0:XW], winb[:, kc, col:col + 128], xb[:, kc, :],
                                       start=(kc == 0), stop=(kc == KC - 1)),
                              reads=[("win", 1), ("xob", b)], writes=[("ps", bk)])
                t = fc % 2
                hs, us, ac = hc_sb[t], u_sb[t], acc_sb[t]
                S.add("act", ACT(hs[:], ps[:, bH, 0:XW], AF.Identity), reads=[("ps", bH)], writes=[("hc_sb", t)])
                S.add("dve", TT(us[:], ps[:, bC, 0:XW], hs[:], ALU.mult), reads=[("ps", bC), ("hc_sb", t)], writes=[("u_sb", t)])
                cw = C_CW + 3 * fc
                S.add("dve", TS(ac[:], us[:, 2:XW], pv[:, cw + 2:cw + 3], ALU.mult), reads=[("u_sb", t), "pv"], writes=[("acc", t)])
                S.add("dve", STT(ac[:], us[:, 1:XW - 1], pv[:, cw + 1:cw + 2], ac[:], ALU.mult, ALU.add), reads=[("u_sb", t), "pv", ("acc", t)], writes=[("acc", t)])
                S.add("dve", STT(ac[:], us[:, 0:XW - 2], pv[:, cw:cw + 1], ac[:], ALU.mult, ALU.add), reads=[("u_sb", t), "pv", ("acc", t)], writes=[("acc", t)])
                S.add("dve", TT(catT[:, 4 + fc, :], ps[:, bB, 2:XW], ac[:], ALU.mult), reads=[("ps", bB), ("acc", t)], writes=[("cat", 4 + fc)])


        if j > 0:
            conv_branch()
        if j == nj - 1 and limit >= 4:
            load_group(0)
            load_group(1)
        if limit < 2:
            continue
        flush()
        nkb = 4 * j + 4
        if 'noattn' in KX:
            nkb = 0
        ms = 0 if j == 0 else 1
        pcount = [0]
        for h in range(4 if nkb else 0):
            accb = [4, 5, 6, 7]
            pend = None

            def do_pv(kb, pi, first, last):
                pb = pbuf[pi]
                for mp in range(2):
                    S.add("pe", MM(ps[:, accb[mp], 0:QW], Vt[:, kb, 128 * h:128 * h + 128], pb[:, mp, :], start=first, stop=last),
                          reads=[("V", kb), ("pbuf", pi)], writes=[("ps", accb[mp])])
                    S.add("pe", MM(ps[:, accb[2 + mp], 0:QW], ones_b[:], pb[:, mp, :], start=first, stop=last),
                          reads=["ones_b", ("pbuf", pi)], writes=[("ps", accb[2 + mp])])

            for kb in range(nkb):
                sp_ = (kb % 2) * 2
                masked = (kb >= 4 * j) or ('allmask' in KX)
                kj = kb // 4
                for mp in range(2):
                    p0 = 64 * mp
                    sdst = ps[:, sp_ + mp, 0:QW]
                    if masked:
                        S.add("pe", MM(sdst, ident_b, (maskb(ms, kb - 4 * j) if kb >= 4 * j else maskb(2, 0)), start=True, stop=False),
                              reads=["cst"], writes=[("ps", sp_ + mp)])
                    S.add("pe", MM(sdst, KT[:, h, 128 * kb:128 * kb + 128], qTz[:, h, mp, :],
                                   start=(not masked), stop=True),
                          reads=[("KT", h, kj), ("qT", h)], writes=[("ps", sp_ + mp)])
                pi = pcount[0] % 3
                pcount[0] += 1
                S.add("act", ACT(pbuf[pi][:], ps[:, sp_:sp_ + 2, 0:QW], AF.Exp, scale=0.125),
                      reads=[("ps", sp_), ("ps", sp_ + 1)], writes=[("pbuf", pi)])
                if pend is not None:
                    do_pv(*pend)
                pend = (kb, pi, kb == 0, kb == nkb - 1)
            do_pv(*pend)
            osb, lsb = tmpW[0], tmpW[1]
            ko = [("tmp", 0, 0), ("tmp", 0, 1)]
            kl = [("tmp", 1, 0), ("tmp", 1, 1)]
            o_, ok_ = TH(2 + h // 2, h % 2)
            sq_, sqk_ = TH(4 + h // 2, h % 2)
            S.add("dve", CP(osb[:], ps[:, 4:6, 0:QW]), reads=[("ps", 4), ("ps", 5)], writes=ko)
            S.add("act", ACT(lsb[:], ps[:, 6:8, 0:QW], AF.Identity), reads=[("ps", 6), ("ps", 7)], writes=kl)
            S.add("dve", RECIP(lsb[:], lsb[:]), reads=kl, writes=kl)
            S.add("dve", TT(osb[:], osb[:], lsb[:], ALU.mult), reads=ko + kl, writes=ko)
            S.add("dve", STT(o_, osb[:, 1, :], neglam, osb[:, 0, :], ALU.mult, ALU.add), reads=ko + ["neglam"], writes=[ok_])
            S.add("dve", TT(sq_, o_, o_, ALU.mult), reads=[ok_], writes=[sqk_])

        if j == 0:
            conv_branch()
        nh_ = 4 if nkb else 0

        def rms_stats():
            for h in range(nh_):
                sq_, sqk_ = TH(4 + h // 2, h % 2)
                S.add("pe", MM(ps[:, h, 0:QW], ones_f[:], sq_, start=True, stop=True), reads=["ones_f", sqk_], writes=[("ps", h)])

        def rms_chain():
            for h in range(nh_):
                lnv, lk_ = TH(4 + h // 2, h % 2)
                S.add("dve", TS(lnv, ps[:, h, 0:QW], 1.0 / 128.0, ALU.mult, RMS_EPS, ALU.add), reads=[("ps", h)], writes=[lk_])
                S.add("act", ACT(lnv, lnv, AF.Ln), reads=[lk_], writes=[lk_])
                S.add("act", ACT(lnv, lnv, AF.Exp, scale=-0.5), reads=[lk_], writes=[lk_])
            for h in range(nh_):
                o_, ok_ = TH(2 + h // 2, h % 2)
                lnv, lk_ = TH(4 + h // 2, h % 2)
                S.add("dve", STT(catT[:, h, :], o_, gsc, lnv, ALU.mult, ALU.mult), reads=[ok_, "gsc", lk_], writes=[("cat", h)])

        if limit < 3:
            rms_stats()
            rms_chain()
            continue
        ob = [4, 5, 6, 7, 0, 1, 2, 3]

        def op_half(m, kcs, first, last):
            for ki, kc in enumerate(kcs):
                S.add("pe", MM(ps[:, ob[m], 0:QW], woutb[:, kc, 128 * m:128 * m + 128], catT[:, kc, :],
                               start=(first and ki == 0), stop=(last and ki == len(kcs) - 1)),
                      reads=["wout", ("cat", kc)], writes=[("ps", ob[m])])

        def op_evac(m):
            S.add("dve", STT(y1[:, m, :], xof[:, m, 2:XW], ALPHA, ps[:, ob[m], 0:QW], ALU.mult, ALU.add),
                  reads=["xof", ("ps", ob[m])], writes=[("y1", m)])

        for m in range(0, 4):
            op_half(m, [4, 5, 6, 7], True, False)
        rms_stats()
        rms_chain()
        for m in range(4, 8):
            op_half(m, [4, 5, 6, 7], True, False)
        for m in range(0, 4):
            op_half(m, [0, 1, 2, 3], False, True)
            op_evac(m)
        for m in range(4, 8):
            op_half(m, [0, 1, 2, 3], False, True)
            op_evac(m)

        if j + 1 < nj:
            S.add("sp", DMA(xof[:], xo3(j + 1)), writes=["xof"], dsem=ds_xof)
        elif limit >= 4:
            for jj in range(nj - 1):
                issue_reload(jj)

        def ln1_out(m, t, tkey, j=j):
            if True:
                S.add("act", ACT(h1bs[:, m, :], t, AF.Identity, scale=pv[:, C_LN1G + m:C_LN1G + m + 1], bias=pv[:, C_LN1B + m:C_LN1B + m + 1]),
                      reads=[tkey, "pv"], writes=[("h1bs", m)])
            else:
                S.add("pool", TS(h1bs[:, m, :], t, pv[:, C_LN1G + m:C_LN1G + m + 1], ALU.mult, pv[:, C_LN1B + m:C_LN1B + m + 1], ALU.add),
                      reads=[tkey, "pv"], writes=[("h1bs", m)])
            S.add("act", ACT(y1[:, m, 2:QW], t[:, 2:QW], AF.Identity, scale=dv[:, V_AG1 + m:V_AG1 + m + 1], bias=dv[:, V_AB1 + m:V_AB1 + m + 1]),
                  reads=[tkey, "ag1", "ab1"], writes=[("y1", m)])

        layer_norm(lambda m: y1[:, m, :], lambda m: ("y1", m), QW, C_LN1G, C_LN1B, ln1_out)
        def ln1_tail(j=j):
            S.add("dve", TS(h1bs[:, :, 0:2], h1bs[:, :, 0:2], pv[:, C_HFLAG + j:C_HFLAG + j + 1], ALU.mult),
                  reads=[("h1bs", m) for m in range(KC)] + ["pv"], writes=[("h1bs", m) for m in range(KC)])
            S.add("sp", DMA(scr_b[j].rearrange("p (k w) -> p k w", k=KC), h1bs[:]), reads=[("h1bs", m) for m in range(KC)],
                  writes=[("scr_b", j)], dsem=ds_sb)
            S.add("sp", DMA(scr_y[j].rearrange("p (k w) -> p k w", k=KC), y1[:, :, 2:QW]), reads=[("y1", m) for m in range(KC)],
                  writes=[("scr_y", j)], dsem=ds_sy)
        todo.append(ln1_tail)
        flush()

    if limit < 4:
        fin = ["pv", "cst", ("win", "k"), ("win", "v"), ("win", "q"), ("win", 1), "wout", "xof", ("xob", 0), ("xob", 1), ("xtb", 0), ("xtb", 1)]
        fin += [("scr_b", j) for j in range(nj)] + [("scr_y", j) for j in range(nj)]
        if debug:
            fin += [("dbgc", j) for j in range(nj)]
        S.add("dve", lambda e: e.memset(dv[:, NV_ - 1:NV_], 0.0), reads=fin, writes=["fin"])
        with nc.Block() as block:
            S.emit(block, sg)
        print("ops", len(S.ops), "waits", S.nwaits, "sems", nsem[0])
        return nc

    sqr[:] = [(t[:], ("sqr2", i)) for i, t in enumerate(sqr2)]
    lnt[:] = [(t[:], ("lnt2", i)) for i, t in enumerate(lnt2)]
    tt_[:] = [(t[:], ("tt2", i)) for i, t in enumerate(tt2)]
    ngrp = len(GROUPS)
    issue_reload(nj - 1)

    cidx = [0]

    def stage_up(g, j, t):
        b = g % 2
        h0, nh = GROUPS[g]
        at = actT[t % 2]
        slots = []
        for i in range(nh + 1):
            if i < nh:
                hc = h0 + i
                bg, bv = bank(), bank()
                for kc in range(KC):
                    S.add("pe", MM(ps[:, bg, 0:QW], wg[b][:, kc, 128 * i:128 * i + 128], h1T[:, j, kc, :],
                                   start=(kc == 0), stop=(kc == KC - 1)),
                          reads=[("wff", b), ("h1T", j)], writes=[("ps", bg)])
                for kc in range(KC):
                    S.add("pe", MM(ps[:, bv, 0:CW], wv[b][:, kc, 128 * i:128 * i + 128], h1T[:, j, kc, 2:QW],
                                   start=(kc == 0), stop=(kc == KC - 1)),
                          reads=[("wff", b), ("h1T", j)], writes=[("ps", bv)])
                ci = cidx[0] % 3
                cidx[0] += 1
                a_ = ca[ci]
                fw_ = C_FCW + 3 * hc
                S.add("act", ACT(a_[:], ps[:, bg, 0:CW], AF.Identity, scale=pv[:, fw_:fw_ + 1], bias=pv[:, C_FCB + hc:C_FCB + hc + 1]),
                      reads=[("ps", bg), "pv"], writes=[("ca", ci)])
                S.add("dve", STT(a_[:], ps[:, bg, 1:CW + 1], pv[:, fw_ + 1:fw_ + 2], a_[:], ALU.mult, ALU.add),
                      reads=[("ps", bg), "pv", ("ca", ci)], writes=[("ca", ci)])
                S.add("dve", STT(a_[:], ps[:, bg, 2:CW + 2], pv[:, fw_ + 2:fw_ + 3], a_[:], ALU.mult, ALU.add),
                      reads=[("ps", bg), "pv", ("ca", ci)], writes=[("ca", ci)])
                slots.append((ci, bv))
            if i >= 1:
                ci, bv = slots[i - 1]
                S.add("act", ACT(cs[ci][:], ca[ci][:], AF.Silu), reads=[("ca", ci)], writes=[("cs", ci)])
                S.add("dve", TT(at[:, i - 1, :], ps[:, bv, 0:CW], cs[ci][:], ALU.mult), reads=[("ps", bv), ("cs", ci)], writes=[("actT", t % 2, i - 1)])

    def stage_down(g, j, t):
        b = g % 2
        h0, nh = GROUPS[g]
        at = actT[t % 2]
        for mp in range(4):
            bd = bank()
            for half in range(2):
                m = 2 * mp + half
                for i in range(nh):
                    S.add("pe", MM(ps[:, bd, 256 * half:256 * half + 256], wd[b][:, i, 128 * m:128 * m + 128], at[:, i, :],
                                   start=(i == 0), stop=(i == nh - 1)),
                          reads=[("wff", b), ("actT", t % 2, i)], writes=[("ps", bd)])
            if 'pooladd' not in KX:
                S.add("dve", TT(y2[:, j, 2 * mp:2 * mp + 2, :], y2[:, j, 2 * mp:2 * mp + 2, :],
                                ps[:, bd, :].rearrange("p (a w) -> p a w", a=2), ALU.add),
                      reads=[("ps", bd), ("y2", j, 2 * mp), ("y2", j, 2 * mp + 1)], writes=[("y2", j, 2 * mp), ("y2", j, 2 * mp + 1)])
            else:
                di = dcnt[0] % 3
                dcnt[0] += 1
                S.add("act", ACT(dsb[di][:], ps[:, bd, :], AF.Identity), reads=[("ps", bd)], writes=[("dsb", di)])
                S.add("pool", TT(y2[:, j, 2 * mp:2 * mp + 2, :], y2[:, j, 2 * mp:2 * mp + 2, :],
                                 dsb[di][:].rearrange("p (a w) -> p a w", a=2), ALU.add),
                      reads=[("dsb", di), ("y2", j, 2 * mp), ("y2", j, 2 * mp + 1)], writes=[("y2", j, 2 * mp), ("y2", j, 2 * mp + 1)])

    def stage_ln2(j):
        def ln2_out(m, t, tkey, j=j):
            S.add("act", ACT(y2[:, j, m, :], t, AF.Identity, scale=pv[:, C_LN2G + m:C_LN2G + m + 1], bias=pv[:, C_LN2B + m:C_LN2B + m + 1]),
                  reads=[tkey, "pv"], writes=[("y2", j, m)])
        flush()
        layer_norm(lambda m, j=j: y2[:, j, m, :], lambda m, j=j: ("y2", j, m), CW, C_LN2G, C_LN2B, ln2_out)
        todo.append(lambda j=j: S.add("sp", DMA(outd[j].rearrange("(m p) w -> p m w", p=128), y2[:, j, :, :]),
                                      reads=[("y2", j, m) for m in range(KC)], writes=[("out", j)], dsem=ds_out))

    seq = [(g, j) for g in range(ngrp) for j in range(nj)]
    pend_ln = []
    for t, (g, j) in enumerate(seq):
        stage_up(g, j, t)
        if pend_ln:
            stage_ln2(pend_ln.pop(0))
        if t >= 1:
            gp, jp = seq[t - 1]
            stage_down(gp, jp, t - 1)
            if jp == nj - 1 and gp + 2 < ngrp:
                load_group(gp + 2)
            if gp == ngrp - 1:
                pend_ln.append(jp)
        flush()
    gp, jp = seq[-1]
    stage_down(gp, jp, len(seq) - 1)
    pend_ln.append(jp)
    for jj in pend_ln:
        stage_ln2(jj)
        flush()
    fin = [("out", j) for j in range(nj)]
    if debug:
        fin += [("dbgc", j) for j in range(nj)]
    S.add("dve", lambda e: e.memset(dv[:, NV_ - 1:NV_], 0.0), reads=fin, writes=["fin"])

    with nc.Block() as block:
        S.emit(block, sg)
    print("ops", len(S.ops), "waits", S.nwaits, "sems", nsem[0])
    return nc


def _host_layout(x, lambda_q1, lambda_k1, lambda_q2, lambda_k2, attn_norm_g, conv_w, ln1_g, ln1_b,
                 ffn_conv_w, ffn_conv_b, ln2_g, ln2_b):
    f = np.float32
    pv_base = np.zeros((128, NP_), f)

    def pcols(v):
        return np.ascontiguousarray(np.asarray(v, f).reshape(-1, 128).T)

    pv_base[:, C_LN1G:C_LN1G + 8] = pcols(ln1_g[0])
    pv_base[:, C_LN1B:C_LN1B + 8] = pcols(ln1_b[0])
    pv_base[:, C_LN2G:C_LN2G + 8] = pcols(ln2_g[0])
    pv_base[:, C_LN2B:C_LN2B + 8] = pcols(ln2_b[0])
    pv_base[:, C_ANG] = np.asarray(attn_norm_g[0], f)
    cw = np.asarray(conv_w[0], f)
    for fc in range(4):
        for tap in range(3):
            pv_base[:, C_CW + 3 * fc + tap] = cw[tap, 128 * fc:128 * fc + 128]
    fw = np.asarray(ffn_conv_w[0], f)
    for hc in range(NHC):
        for tap in range(3):
            pv_base[:, C_FCW + 3 * hc + tap] = fw[tap, 128 * hc:128 * hc + 128]
    pv_base[:, C_FCB:C_FCB + NHC] = pcols(ffn_conv_b[0])
    lam = np.concatenate([np.asarray(v[0], f) for v in (lambda_q1, lambda_k1, lambda_q2, lambda_k2)])
    pv_base[:, C_LAM:C_LAM + 256] = lam[None, :]

    in_maps_part = []
    for core in range(8):
        bi, c = core // 2, core % 2
        xT = np.zeros((D, SEQ + 4), f)
        xT[:, 4:] = np.asarray(x[bi], f).T
        xo = np.zeros((NJ, 128, KC * XW), f)
        xt = np.zeros((NJ, 128, KC * CW), f)
        for j in range(NJ):
            t0 = 256 * (2 * j + c)
            t1 = 256 * (2 * j + 1 - c)
            w = xT[:, t0:t0 + XW].reshape(KC, 128, XW).transpose(1, 0, 2)
            xo[j] = w.reshape(128, KC * XW)
            w = xT[:, 4 + t1:4 + t1 + CW].reshape(KC, 128, CW).transpose(1, 0, 2)
            xt[j] = w.reshape(128, KC * CW)
        pvc = pv_base.copy()
        pvc[:, C_HFLAG:C_HFLAG + NJ] = 1.0
        if c == 0:
            pvc[:, C_HFLAG] = 0.0
        cst = np.zeros((128, 128 + 9 * QW), f)
        cst[:, 0:128] = np.eye(128, dtype=f)
        p = np.arange(128)[:, None]
        q = np.arange(QW)[None, :]
        for ms in range(2):
            for r in range(4):
                if r < 2:
                    key_rel = 256 * c + 128 * r + p
                else:
                    key_rel = 256 * (1 - c) + 128 * (r - 2) + p
                qry_rel = 256 * c - 2 + q
                allowed = key_rel <= qry_rel
                if ms == 0 and c == 0:
                    allowed = allowed | (q < 2)
                o = 128 + (ms * 4 + r) * QW
                cst[:, o:o + QW] = np.where(allowed, 0.0, NEG).astype(f)
        in_maps_part.append({"xo": xo, "xt": xt, "pv": pvc, "cst": cst})
    return in_maps_part


_CACHE = {}


def kernel(x, w_in, lambda_q1, lambda_k1, lambda_q2, lambda_k2, attn_norm_g, conv_w, w_out,
           ln1_g, ln1_b, ffn_w_up, ffn_conv_w, ffn_conv_b, ffn_w_down, ln2_g, ln2_b, _debug=False):
    x = np.asarray(x)
    parts = _host_layout(x, lambda_q1, lambda_k1, lambda_q2, lambda_k2, attn_norm_g, conv_w, ln1_g, ln1_b,
                         ffn_conv_w, ffn_conv_b, ln2_g, ln2_b)
    shared = {
        "w_in": np.ascontiguousarray(np.asarray(w_in, np.float32)[0]),
        "w_out": np.ascontiguousarray(np.asarray(w_out, np.float32)[0]),
        "w_up": np.ascontiguousarray(np.asarray(ffn_w_up, np.float32)[0]),
        "w_down": np.ascontiguousarray(np.asarray(ffn_w_down, np.float32)[0]),
    }
    in_maps = [dict(p, **shared) for p in parts]
    key = ("nc", bool(_debug))
    if key not in _CACHE:
        _CACHE[key] = build_program(debug=_debug)
    nc = _CACHE[key]
    res = run_bass_kernel_spmd(nc, in_maps, core_ids=list(range(8)))
    out = np.zeros((NB, SEQ, D), np.float32)
    for core in range(8):
        bi, c = core // 2, core % 2
        for j in range(NJ):
            t0 = 256 * (2 * j + c)
            out[bi, t0:t0 + CW, :] = res.results[core]["out%d" % j].T
    if _debug:
        return out, res
    return out
```

```python
import math
import numpy as np
import concourse.bass as bass
import concourse.mybir as mybir
from concourse.bass_utils import run_bass_kernel_spmd

F32 = mybir.dt.float32
BF16 = mybir.dt.bfloat16
AF = mybir.ActivationFunctionType
ALU = mybir.AluOpType
AX = mybir.AxisListType

D = 1024
SEQ = 4096
NB = 4
KC = 8
NJ = 8
CW = 256
QW = 258
XW = 260
DFF = 2816
NHC = 22
GRP = 4
GROUPS = [(0, 2), (2, 4), (6, 4), (10, 4), (14, 4), (18, 4)]
ALPHA = (2.0 * 1) ** 0.25
LAM_INIT = 0.8 - 0.6 * math.exp(0.0)
LN_EPS = 1e-5
RMS_EPS = 1e-5
NEG = -30000.0

C_LN1G, C_LN1B, C_LN2G, C_LN2B = 0, 8, 16, 24
C_ANG = 32
C_CW = 33
C_FCW = 45
C_FCB = 111
C_LAM = 133
C_HFLAG = 389
NP_ = 397
V_AG1, V_AB1, V_GSC, V_NEGLAM, V_T0, V_T1, V_E0, V_E1 = 0, 8, 16, 17, 18, 19, 20, 21
NV_ = 24

SEM_ROLL = 1000


class DSem:
    def __init__(self, sem):
        self.sem = sem
        self.count = 0


class Op:
    __slots__ = ("eng", "fn", "deps", "dsem", "ticket", "signal", "clock", "idx", "ndma")


class Sched:
    ENGS = ("pe", "act", "dve", "pool", "sp")

    def __init__(self, nc, same_engine_sync=True):
        self.nc = nc
        self.ops = []
        self.lastw = {}
        self.rd_eng = {}
        self.rd_dma = {}
        self.same_engine_sync = same_engine_sync

    def add(self, eng, fn, reads=(), writes=(), dsem=None, ndma=1):
        op = Op()
        op.eng, op.fn, op.dsem, op.ndma = eng, fn, dsem, ndma
        op.idx = len(self.ops)
        op.ticket = None
        op.signal = dsem is not None
        op.clock = None
        deps = set()
        for r in reads:
            w = self.lastw.get(r)
            if w is not None:
                deps.add(w)
        for w_ in writes:
            w = self.lastw.get(w_)
            if w is not None:
                deps.add(w)
            for i in self.rd_eng.get(w_, {}).values():
                deps.add(i)
            for i in self.rd_dma.get(w_, ()):
                deps.add(i)
        for w_ in writes:
            self.lastw[w_] = op.idx
            self.rd_eng[w_] = {}
            self.rd_dma[w_] = []
        for r in reads:
            if r in writes:
                continue
            if dsem is not None:
                self.rd_dma.setdefault(r, []).append(op.idx)
            else:
                self.rd_eng.setdefault(r, {})[eng] = op.idx
        keep = set()
        best = {}
        bestd = {}
        for d in deps:
            p = self.ops[d]
            if p.dsem is not None:
                k_ = id(p.dsem)
                if k_ not in bestd or bestd[k_] < d:
                    bestd[k_] = d
                continue
            if p.eng == eng and dsem is None and (eng == "pe" or not self.same_engine_sync):
                continue
            if p.eng not in best or best[p.eng] < d:
                best[p.eng] = d
        keep.update(best.values())
        keep.update(bestd.values())
        op.deps = keep
        for d in keep:
            self.ops[d].signal = True
        self.ops.append(op)
        return op

    def emit(self, block, sems):
        cnt = {e: 0 for e in self.ENGS}
        semlist = {e: [] for e in self.ENGS}
        for op in self.ops:
            if op.dsem is not None:
                op.dsem.count += 16 * op.ndma
                op.ticket = ("d", op.dsem.sem, op.dsem.count)
            elif op.signal:
                c = cnt[op.eng]
                si, v = divmod(c, SEM_ROLL)
                if si >= len(semlist[op.eng]):
                    semlist[op.eng].append(next(sems))
                op.ticket = ("e", semlist[op.eng][si], v + 1)
                cnt[op.eng] = c + 1
        streams = {e: [] for e in self.ENGS}
        clock = {e: {} for e in self.ENGS}
        nwaits = 0
        for op in self.ops:
            ck = clock[op.eng]
            st = streams[op.eng]
            for d in sorted(op.deps):
                p = self.ops[d]
                t = p.ticket
                key = id(t[1])
                if ck.get(key, 0) >= t[2]:
                    continue
                st.append(("w", t[1], t[2]))
                nwaits += 1
                ck[key] = t[2]
                if p.clock is not None:
                    for k, v in p.clock.items():
                        if ck.get(k, 0) < v:
                            ck[k] = v
            st.append(("o", op))
            if op.signal:
                op.clock = dict(ck)
        self.nwaits = nwaits
        print("signal counts", cnt, "dma max", max([op.ticket[2] for op in self.ops if op.dsem is not None] + [0]))
        self.streams = streams

        def run(e):
            def body(eng):
                for item in streams[e]:
                    if item[0] == "w":
                        eng.wait_ge(item[1], item[2])
                    else:
                        op = item[1]
                        ins = op.fn(eng)
                        if op.dsem is not None:
                            if not isinstance(ins, (list, tuple)):
                                ins = [ins]
                            assert len(ins) == op.ndma
                            for i_ in ins:
                                i_.then_inc(op.dsem.sem, 16)
                        elif op.signal:
                            ins.then_inc(op.ticket[1], 1)
            return body

        block.tensor(run("pe"))
        block.scalar(run("act"))
        block.vector(run("dve"))
        block.gpsimd(run("pool"))
        block.sync(run("sp"))


def MM(out, lhsT, rhs, start=True, stop=True):
    return lambda e: e.matmul(out, lhsT=lhsT, rhs=rhs, start=start, stop=stop)


def ACT(out, in_, func, scale=None, bias=None):
    kw = {}
    if scale is not None:
        kw["scale"] = scale
    if bias is not None:
        kw["bias"] = bias
    return lambda e: e.activation(out=out, in_=in_, func=func, **kw)


def TT(out, a, b, op):
    return lambda e: e.tensor_tensor(out=out, in0=a, in1=b, op=op)


def TS(out, a, s1, op0, s2=None, op1=None):
    if op1 is None:
        return lambda e: e.tensor_scalar(out=out, in0=a, scalar1=s1, scalar2=None, op0=op0)
    return lambda e: e.tensor_scalar(out=out, in0=a, scalar1=s1, scalar2=s2, op0=op0, op1=op1)


def STT(out, a, s, b, op0, op1):
    return lambda e: e.scalar_tensor_tensor(out=out, in0=a, scalar=s, in1=b, op0=op0, op1=op1)


def CP(out, in_):
    return lambda e: e.tensor_copy(out=out, in_=in_)


def RECIP(out, in_):
    return lambda e: e.reciprocal(out=out, in_=in_)


def DMA(out, in_):
    return lambda e: e.dma_start(out=out, in_=in_)


def DMAS(pairs):
    return lambda e: [e.dma_start(out=o, in_=i) for (o, i) in pairs]


class Arena:
    def __init__(self, nc, base, top):
        self.nc, self.cur, self.top = nc, base, top
        self.n = 0

    def alloc(self, name, shape, dtype):
        esz = 4 if dtype == F32 else 2
        size = esz
        for s in shape[1:]:
            size *= s
        off = (self.cur + 31) // 32 * 32
        self.cur = off + size
        self.last_off = off
        assert self.cur <= self.top, (name, self.cur, self.top)
        self.n += 1
        return self.nc.alloc_sbuf_tensor_at("%s_%d" % (name, self.n), list(shape), dtype, offset=off)


def build_program(nj=NJ, debug=False, limit=99):
    import os
    KX = set(os.environ.get('KX', '').split(','))
    nc = bass.Bass("TRN2", target_bir_lowering=False)

    def dram(name, shape, dtype, kind):
        return nc.dram_tensor(name, list(shape), dtype, kind=kind).ap()

    xo = dram("xo", [NJ, 128, KC * XW], F32, "ExternalInput")
    xt = dram("xt", [NJ, 128, KC * CW], F32, "ExternalInput")
    pvd = dram("pv", [128, NP_], F32, "ExternalInput")
    cstd = dram("cst", [128, 128 + 9 * QW], F32, "ExternalInput")
    w_in = dram("w_in", [D, 3072], F32, "ExternalInput")
    w_out = dram("w_out", [D, D], F32, "ExternalInput")
    w_up = dram("w_up", [D, 2 * DFF], F32, "ExternalInput")
    w_down = dram("w_down", [DFF, D], F32, "ExternalInput")
    outd = [dram("out%d" % j, [128, KC * CW], F32, "ExternalOutput") for j in range(NJ)]
    scr_b = [dram("scr_b%d" % j, [128, KC * QW], BF16, "ExternalOutput") for j in range(NJ)]
    scr_y = [dram("scr_y%d" % j, [128, KC * CW], F32, "ExternalOutput") for j in range(NJ)]
    if debug:
        dbg_cat = dram("dbg_cat", [NJ, 128, KC * QW], F32, "ExternalOutput")
        dbg_h1 = dram("dbg_h1", [NJ, 128, KC * CW], F32, "ExternalOutput")

    nsem = [0]

    def semgen():
        while True:
            nsem[0] += 1
            yield nc.alloc_semaphore("sm%d" % nsem[0])

    sg = semgen()

    def dsem():
        return DSem(next(sg))

    S = Sched(nc)
    base0 = (nc.sbuf_base + 63) // 64 * 64
    top0 = nc.sbuf_top - 64
    arena_t = nc.alloc_sbuf_tensor("arena", [128, top0 - base0], mybir.dt.uint8)
    A = Arena(nc, base0, top0)
    ps = nc.alloc_psum_tensor("ps", [128, 8, 512], F32)

    pv = A.alloc("pv", [128, NP_], F32)
    dv = A.alloc("dv", [128, NV_], F32)
    lamt = A.alloc("lamt", [128, 128], F32)
    ones_f = A.alloc("ones_f", [128, 128], F32)
    ones_b = A.alloc("ones_b", [128, 128], BF16)
    cstb = A.alloc("cstb", [128, 128 + 9 * QW], BF16)
    ident_b = cstb[:, 0:128]

    def maskb(ms, r):
        o = 128 + (ms * 4 + r) * QW
        return cstb[:, o:o + QW]

    mark_persist = A.cur

    winb = A.alloc("winb", [128, KC, 3072], BF16)
    winb_off = A.last_off
    woutb = A.alloc("woutb", [128, KC, D], BF16)
    KT = A.alloc("KT", [128, 4, SEQ], BF16)
    Vt = A.alloc("Vt", [128, 32, 512], BF16)
    xof = A.alloc("xof", [128, KC, XW], F32)
    xob = [A.alloc("xob", [128, KC, XW], BF16) for _ in range(2)]
    xtb = [A.alloc("xtb", [128, KC, CW], BF16) for _ in range(2)]
    qTz = A.alloc("qTz", [128, 4, 2, QW], BF16)
    catT = A.alloc("catT", [128, KC, QW], BF16)
    hc_sb = [A.alloc("hc_sb", [128, XW], F32) for _ in range(2)]
    u_sb = [A.alloc("u_sb", [128, XW], F32) for _ in range(2)]
    acc_sb = [A.alloc("acc_sb", [128, QW], F32) for _ in range(2)]
    pbuf = [A.alloc("pbuf", [128, 2, QW], BF16) for _ in range(3)]
    tmpW = [A.alloc("tmpW", [128, 2, QW], F32) for _ in range(6)]
    tmpW_off = A.last_off - 5 * 2 * QW * 4 - 5 * 32

    def TH(k, h):
        return (tmpW[k][:, h, :], ("tmp", k, h))
    y1 = A.alloc("y1", [128, KC, QW], F32)
    lnt = [TH(0, 0), TH(0, 1), TH(1, 0), TH(1, 1), TH(2, 0)]
    tt_ = [TH(3, 0), TH(3, 1)]
    sqr = [TH(4, 0), TH(4, 1)]
    h1bs = A.alloc("h1bs", [128, KC, QW], BF16)
    print("phase1 sbuf end", A.cur, "of", top0)
    p1_end = A.cur
    A.cur = winb_off
    wg, wv, wd = [], [], []
    for _ in range(2):
        wg.append(A.alloc("wg", [128, KC, 512], BF16))
        wv.append(A.alloc("wv", [128, KC, 512], BF16))
        wd.append(A.alloc("wd", [128, GRP, D], BF16))
    wff_end = A.cur
    assert wff_end <= winb_off + KC * 3072 * 2
    A.cur = wff_end
    h1T = A.alloc("h1T", [128, NJ, KC, QW], BF16)
    y2 = A.alloc("y2", [128, NJ, KC, CW], F32)
    actT = [A.alloc("actT", [128, GRP, CW], BF16) for _ in range(2)]
    ca = [A.alloc("ca", [128, CW], F32) for _ in range(3)]
    cs = [A.alloc("cs", [128, CW], F32) for _ in range(3)]
    dcnt = [0]
    sqr2 = [A.alloc("sqr2", [128, CW], F32) for _ in range(2)]
    lnt2 = [A.alloc("lnt2", [128, CW], F32) for _ in range(5)]
    tt2 = [A.alloc("ttmp2", [128, CW], F32) for _ in range(2)]
    print("phase2 sbuf end", A.cur, "of", top0)
    p2_end = A.cur
    ds_h1T = [dsem() for _ in range(nj)]
    ds_y2 = [dsem() for _ in range(nj)]
    ds_out = dsem()
    p1_done = [("scr_b", nj - 1), ("scr_y", nj - 1)]
    assert A.cur <= tmpW_off, (A.cur, tmpW_off)
    dead_keys = ["wout", "xof"] + [("cat", k) for k in range(KC)] + [("qT", h) for h in range(4)] + [("xob", 0), ("xob", 1), ("xtb", 0), ("xtb", 1)]
    A.cur = p1_end

    ds_pv, ds_cst = dsem(), dsem()
    ds_win = {"k": dsem(), "v": dsem(), "q": dsem(), 1: dsem()}
    ds_wout = dsem()
    ds_xof = dsem()
    ds_xob = [dsem(), dsem()]
    ds_xtb = [dsem(), dsem()]
    ds_sb, ds_sy = dsem(), dsem()
    ds_dbg = dsem()

    ds_wff = [dsem(), dsem()]

    def load_group(g):
        b = g % 2
        h0, nh = GROUPS[g]
        pairs = []
        for kc in range(KC):
            pairs.append((wg[b][:, kc, 0:128 * nh], w_up[128 * kc:128 * kc + 128, 128 * h0:128 * (h0 + nh)]))
            pairs.append((wv[b][:, kc, 0:128 * nh], w_up[128 * kc:128 * kc + 128, DFF + 128 * h0:DFF + 128 * (h0 + nh)]))
        for i in range(nh):
            pairs.append((wd[b][:, i, :], w_down[128 * (h0 + i):128 * (h0 + i) + 128, :]))
        S.add("pool", DMAS(pairs), writes=[("wff", b), ("win", "k"), ("win", "v"), ("win", "q"), ("win", 1)],
              dsem=ds_wff[b], ndma=len(pairs))

    def issue_reload(j):
        S.add("sp", DMA(h1T[:, j, :, :].rearrange("p k w -> p (k w)"), scr_b[j]),
              reads=[("scr_b", j)], writes=[("h1T", j)] + dead_keys, dsem=ds_h1T[j])
        S.add("sp", DMA(y2[:, j, :, :].rearrange("p k w -> p (k w)"), scr_y[j]),
              reads=[("scr_y", j)], writes=[("y2", j, m) for m in range(KC)] + dead_keys, dsem=ds_y2[j])


    rb = [0]

    def bank():
        b = rb[0]
        rb[0] = (b + 1) % 8
        return b

    S.add("sp", DMA(pv[:], pvd[:, :]), writes=["pv"], dsem=ds_pv)
    S.add("pool", DMAS([(cstb[:, 0:1024], cstd[:, 0:1024]), (cstb[:, 1024:128 + 9 * QW], cstd[:, 1024:128 + 9 * QW])]),
          writes=["cst"], dsem=ds_cst, ndma=2)
    S.add("dve", lambda e: e.memset(ones_f[:], 1.0), writes=["ones_f"])
    S.add("dve", lambda e: e.memset(ones_b[:], 1.0), writes=["ones_b"])
    S.add("dve", lambda e: e.memset(qTz[:].rearrange("p a b w -> p (a b w)"), 0.0), writes=[("qT", h) for h in range(4)])

    def xo3(j):
        return xo[j].rearrange("p (k w) -> p k w", k=KC)

    def xt3(j):
        return xt[j].rearrange("p (k w) -> p k w", k=KC)

    def issue_loads(j):
        b = j % 2
        xf = xob[b][:].rearrange("p k w -> p (k w)")
        hx = KC * XW // 2
        S.add("pool", DMAS([(xf[:, 0:hx], xo[j][:, 0:hx]), (xf[:, hx:2 * hx], xo[j][:, hx:2 * hx])]),
              writes=[("xob", b)], dsem=ds_xob[b], ndma=2)
        tf = xtb[b][:].rearrange("p k w -> p (k w)")
        ht = KC * CW // 2
        S.add("pool", DMAS([(tf[:, 0:ht], xt[j][:, 0:ht]), (tf[:, ht:2 * ht], xt[j][:, ht:2 * ht])]),
              writes=[("xtb", b)], dsem=ds_xtb[b], ndma=2)

    issue_loads(0)
    for grp, c0, cn in (("k", 512, 512), ("v", 1024, 512), ("q", 0, 512), (1, 1536, 1536)):
        S.add("pool", DMAS([(winb[:, kc, c0:c0 + cn], w_in[128 * kc:128 * kc + 128, c0:c0 + cn]) for kc in range(KC)]),
              writes=[("win", grp)], dsem=ds_win[grp], ndma=KC)
    S.add("sp", DMA(xof[:].rearrange("p k w -> p (k w)"), xo[0]), writes=["xof"], dsem=ds_xof)
    S.add("pool", DMAS([(woutb[:, kc, :], w_out[128 * kc:128 * kc + 128, :]) for kc in range(KC)]),
          writes=["wout"], dsem=ds_wout, ndma=KC)

    S.add("dve", TT(lamt[:, 0:64], pv[:, C_LAM:C_LAM + 64], pv[:, C_LAM + 64:C_LAM + 128], ALU.mult), reads=["pv"], writes=["lamt0"])
    S.add("dve", TT(lamt[:, 64:128], pv[:, C_LAM + 128:C_LAM + 192], pv[:, C_LAM + 192:C_LAM + 256], ALU.mult), reads=["pv"], writes=["lamt1"])
    S.add("dve", lambda e: e.reduce_sum(out=dv[:, V_T0:V_T0 + 1], in_=lamt[:, 0:64], axis=AX.X), reads=["lamt0"], writes=["dvt0"])
    S.add("dve", lambda e: e.reduce_sum(out=dv[:, V_T1:V_T1 + 1], in_=lamt[:, 64:128], axis=AX.X), reads=["lamt1"], writes=["dvt1"])
    S.add("act", ACT(dv[:, V_E0:V_E0 + 2], dv[:, V_T0:V_T0 + 2], AF.Exp), reads=["dvt0", "dvt1"], writes=["dve01"])
    S.add("dve", TT(dv[:, V_NEGLAM:V_NEGLAM + 1], dv[:, V_E1:V_E1 + 1], dv[:, V_E0:V_E0 + 1], ALU.subtract), reads=["dve01"], writes=["neglam0"])
    S.add("dve", TS(dv[:, V_NEGLAM:V_NEGLAM + 1], dv[:, V_NEGLAM:V_NEGLAM + 1], -LAM_INIT, ALU.add), reads=["neglam0"], writes=["neglam"])
    S.add("dve", TS(dv[:, V_GSC:V_GSC + 1], pv[:, C_ANG:C_ANG + 1], 1.0 - LAM_INIT, ALU.mult), reads=["pv"], writes=["gsc"])
    S.add("dve", TS(dv[:, V_AG1:V_AG1 + 8], pv[:, C_LN1G:C_LN1G + 8], ALPHA, ALU.mult), reads=["pv"], writes=["ag1"])
    S.add("dve", TS(dv[:, V_AB1:V_AB1 + 8], pv[:, C_LN1B:C_LN1B + 8], ALPHA, ALU.mult), reads=["pv"], writes=["ab1"])
    neglam = dv[:, V_NEGLAM:V_NEGLAM + 1]
    gsc = dv[:, V_GSC:V_GSC + 1]

    evac_flip = [0]

    def evac(out_ap, in_ap, reads, writes):
        evac_flip[0] ^= 1
        if evac_flip[0]:
            S.add("dve", CP(out_ap, in_ap), reads=reads, writes=writes)
        else:
            S.add("act", ACT(out_ap, in_ap, AF.Identity), reads=reads, writes=writes)

    todo = []

    def drain(n):
        for _ in range(min(n, len(todo))):
            todo.pop(0)()

    def flush():
        drain(len(todo))

    def layer_norm(src, src_keys, width, gcol, bcol, emit_out):
        b1, b2 = bank(), bank()
        s1 = ps[:, b1, 0:width]
        s2 = ps[:, b2, 0:width]
        for m in range(KC):
            sq, sqk = sqr[m % 2]
            S.add("act", ACT(sq[:, 0:width], src(m), AF.Square), reads=[src_keys(m)], writes=[sqk])
            S.add("pe", MM(s1, ones_f[:], src(m), start=(m == 0), stop=(m == KC - 1)), reads=["ones_f", src_keys(m)], writes=[("ps", b1)])
            S.add("pe", MM(s2, ones_f[:], sq[:, 0:width], start=(m == 0), stop=(m == KC - 1)), reads=["ones_f", sqk], writes=[("ps", b2)])
        (mean, kmean), (msq, kmsq), (var, kvar), (rstd, krstd), (nmr, knmr) = [(t[:, 0:width], k) for (t, k) in lnt]
        S.add("dve", TS(mean, s1, 1.0 / D, ALU.mult), reads=[("ps", b1)], writes=[kmean])
        S.add("dve", TT(msq, mean, mean, ALU.mult), reads=[kmean], writes=[kmsq])
        S.add("dve", STT(var, s2, 1.0 / D, msq, ALU.mult, ALU.subtract), reads=[("ps", b2), kmsq], writes=[kvar])
        S.add("dve", TS(var, var, LN_EPS, ALU.add), reads=[kvar], writes=[kvar])
        S.add("act", ACT(var, var, AF.Ln), reads=[kvar], writes=[kvar])
        S.add("act", ACT(rstd, var, AF.Exp, scale=-0.5), reads=[kvar], writes=[krstd])
        S.add("dve", STT(nmr, mean, -1.0, rstd, ALU.mult, ALU.mult), reads=[kmean, krstd], writes=[knmr])
        def norm_step(m, t, tk):
            def go():
                S.add("dve", TT(t, src(m), rstd, ALU.mult), reads=[src_keys(m), krstd], writes=[tk])
                S.add("dve", TT(t, t, nmr, ALU.add), reads=[tk, knmr], writes=[tk])
                emit_out(m, t, tk)
            return go

        for m in range(KC):
            t, tk = tt_[m % 2]
            todo.append(norm_step(m, t[:, 0:width], tk))

    for j in range(nj if limit >= 1 else 0):
        b = j % 2
        xb, xtb_ = xob[b], xtb[b]
        if j + 1 < nj:
            issue_loads(j + 1)
        for h in range(4):
            bk = bank()
            col = 512 + 128 * h
            for half, (src, skey) in enumerate([(lambda kc: xb[:, kc, 4:XW], ("xob", b)), (lambda kc: xtb_[:, kc, :], ("xtb", b))]):
                for kc in range(KC):
                    S.add("pe", MM(ps[:, bk, 256 * half:256 * half + 256], winb[:, kc, col:col + 128], src(kc),
                                   start=(kc == 0), stop=(kc == KC - 1)),
                          reads=[("win", "k"), skey], writes=[("ps", bk)])
            evac(KT[:, h, 512 * j:512 * j + 512], ps[:, bk, :], [("ps", bk)], [("KT", h, j)])
            drain(1)
        if limit < 1.2:
            continue
        for blk in range(4):
            bk = bank()
            for kc in range(KC):
                if blk < 2:
                    lt = xb[:, kc, 4 + 128 * blk:4 + 128 * blk + 128]
                    skey = ("xob", b)
                else:
                    lt = xtb_[:, kc, 128 * (blk - 2):128 * (blk - 2) + 128]
                    skey = ("xtb", b)
                S.add("pe", MM(ps[:, bk, :], lt, winb[:, kc, 1024:1536], start=(kc == 0), stop=(kc == KC - 1)),
                      reads=[("win", "v"), skey], writes=[("ps", bk)])
            evac(Vt[:, 4 * j + blk, :], ps[:, bk, :], [("ps", bk)], [("V", 4 * j + blk)])
            drain(1)
        if limit < 1.4:
            continue
        for h in range(4):
            bk = bank()
            for kc in range(KC):
                S.add("pe", MM(ps[:, bk, 0:QW], winb[:, kc, 128 * h:128 * h + 128], xb[:, kc, 2:XW],
                               start=(kc == 0), stop=(kc == KC - 1)),
                      reads=[("win", "q"), ("xob", b)], writes=[("ps", bk)])
            S.add("dve", CP(qTz[0:64, h, 0, :], ps[0:64, bk, 0:QW]), reads=[("ps", bk)], writes=[("qT", h)])
            S.add("act", ACT(qTz[64:128, h, 1, :], ps[64:128, bk, 0:QW], AF.Identity), reads=[("ps", bk)], writes=[("qT", h)])
            drain(1)
        if limit < 1.6:
            continue
        def conv_branch(xb=xb, b=b):
            for fc in range(4):
                bB, bC, bH = bank(), bank(), bank()
                for (bk, c0) in [(bH, 2560), (bC, 2048), (bB, 1536)]:
                    col = c0 + 128 * fc
                    for kc in range(KC):
                        S.add("pe", MM(ps[:, bk, 0:XW], winb[:, kc, col:col + 128], xb[:, kc, :],
                                       start=(kc == 0), stop=(kc == KC - 1)),
                              reads=[("win", 1), ("xob", b)], writes=[("ps", bk)])
                t = fc % 2
                hs, us, ac = hc_sb[t], u_sb[t], acc_sb[t]
                S.add("act", ACT(hs[:], ps[:, bH, 0:XW], AF.Identity), reads=[("ps", bH)], writes=[("hc_sb", t)])
                S.add("dve", TT(us[:], ps[:, bC, 0:XW], hs[:], ALU.mult), reads=[("ps", bC), ("hc_sb", t)], writes=[("u_sb", t)])
                cw = C_CW + 3 * fc
                S.add("dve", TS(ac[:], us[:, 2:XW], pv[:, cw + 2:cw + 3], ALU.mult), reads=[("u_sb", t), "pv"], writes=[("acc", t)])
                S.add("dve", STT(ac[:], us[:, 1:XW - 1], pv[:, cw + 1:cw + 2], ac[:], ALU.mult, ALU.add), reads=[("u_sb", t), "pv", ("acc", t)], writes=[("acc", t)])
                S.add("dve", STT(ac[:], us[:, 0:XW - 2], pv[:, cw:cw + 1], ac[:], ALU.mult, ALU.add), reads=[("u_sb", t), "pv", ("acc", t)], writes=[("acc", t)])
                S.add("dve", TT(catT[:, 4 + fc, :], ps[:, bB, 2:XW], ac[:], ALU.mult), reads=[("ps", bB), ("acc", t)], writes=[("cat", 4 + fc)])


        if j > 0:
            conv_branch()
        if j == nj - 1 and limit >= 4:
            load_group(0)
            load_group(1)
        if limit < 2:
            continue
        flush()
        nkb = 4 * j + 4
        if 'noattn' in KX:
            nkb = 0
        ms = 0 if j == 0 else 1
        pcount = [0]
        for h in range(4 if nkb else 0):
            accb = [4, 5, 6, 7]
            pend = None

            def do_pv(kb, pi, first, last):
                pb = pbuf[pi]
                for mp in range(2):
                    S.add("pe", MM(ps[:, accb[mp], 0:QW], Vt[:, kb, 128 * h:128 * h + 128], pb[:, mp, :], start=first, stop=last),
                          reads=[("V", kb), ("pbuf", pi)], writes=[("ps", accb[mp])])
                    S.add("pe", MM(ps[:, accb[2 + mp], 0:QW], ones_b[:], pb[:, mp, :], start=first, stop=last),
                          reads=["ones_b", ("pbuf", pi)], writes=[("ps", accb[2 + mp])])

            for kb in range(nkb):
                sp_ = (kb % 2) * 2
                masked = (kb >= 4 * j) or ('allmask' in KX)
                kj = kb // 4
                for mp in range(2):
                    p0 = 64 * mp
                    sdst = ps[:, sp_ + mp, 0:QW]
                    if masked:
                        S.add("pe", MM(sdst, ident_b, (maskb(ms, kb - 4 * j) if kb >= 4 * j else maskb(2, 0)), start=True, stop=False),
                              reads=["cst"], writes=[("ps", sp_ + mp)])
                    S.add("pe", MM(sdst, KT[:, h, 128 * kb:128 * kb + 128], qTz[:, h, mp, :],
                                   start=(not masked), stop=True),
                          reads=[("KT", h, kj), ("qT", h)], writes=[("ps", sp_ + mp)])
                pi = pcount[0] % 3
                pcount[0] += 1
                S.add("act", ACT(pbuf[pi][:], ps[:, sp_:sp_ + 2, 0:QW], AF.Exp, scale=0.125),
                      reads=[("ps", sp_), ("ps", sp_ + 1)], writes=[("pbuf", pi)])
                if pend is not None:
                    do_pv(*pend)
                pend = (kb, pi, kb == 0, kb == nkb - 1)
            do_pv(*pend)
            osb, lsb = tmpW[0], tmpW[1]
            ko = [("tmp", 0, 0), ("tmp", 0, 1)]
            kl = [("tmp", 1, 0), ("tmp", 1, 1)]
            o_, ok_ = TH(2 + h // 2, h % 2)
            sq_, sqk_ = TH(4 + h // 2, h % 2)
            S.add("dve", CP(osb[:], ps[:, 4:6, 0:QW]), reads=[("ps", 4), ("ps", 5)], writes=ko)
            S.add("act", ACT(lsb[:], ps[:, 6:8, 0:QW], AF.Identity), reads=[("ps", 6), ("ps", 7)], writes=kl)
            S.add("dve", RECIP(lsb[:], lsb[:]), reads=kl, writes=kl)
            S.add("dve", TT(osb[:], osb[:], lsb[:], ALU.mult), reads=ko + kl, writes=ko)
            S.add("dve", STT(o_, osb[:, 1, :], neglam, osb[:, 0, :], ALU.mult, ALU.add), reads=ko + ["neglam"], writes=[ok_])
            S.add("dve", TT(sq_, o_, o_, ALU.mult), reads=[ok_], writes=[sqk_])

        if j == 0:
            conv_branch()
        nh_ = 4 if nkb else 0

        def rms_stats():
            for h in range(nh_):
                sq_, sqk_ = TH(4 + h // 2, h % 2)
                S.add("pe", MM(ps[:, h, 0:QW], ones_f[:], sq_, start=True, stop=True), reads=["ones_f", sqk_], writes=[("ps", h)])

        def rms_chain():
            for h in range(nh_):
                lnv, lk_ = TH(4 + h // 2, h % 2)
                S.add("dve", TS(lnv, ps[:, h, 0:QW], 1.0 / 128.0, ALU.mult, RMS_EPS, ALU.add), reads=[("ps", h)], writes=[lk_])
                S.add("act", ACT(lnv, lnv, AF.Ln), reads=[lk_], writes=[lk_])
                S.add("act", ACT(lnv, lnv, AF.Exp, scale=-0.5), reads=[lk_], writes=[lk_])
            for h in range(nh_):
                o_, ok_ = TH(2 + h // 2, h % 2)
                lnv, lk_ = TH(4 + h // 2, h % 2)
                S.add("dve", STT(catT[:, h, :], o_, gsc, lnv, ALU.mult, ALU.mult), reads=[ok_, "gsc", lk_], writes=[("cat", h)])

        if limit < 3:
            rms_stats()
            rms_chain()
            continue
        ob = [4, 5, 6, 7, 0, 1, 2, 3]

        def op_half(m, kcs, first, last):
            for ki, kc in enumerate(kcs):
                S.add("pe", MM(ps[:, ob[m], 0:QW], woutb[:, kc, 128 * m:128 * m + 128], catT[:, kc, :],
                               start=(first and ki == 0), stop=(last and ki == len(kcs) - 1)),
                      reads=["wout", ("cat", kc)], writes=[("ps", ob[m])])

        def op_evac(m):
            S.add("dve", STT(y1[:, m, :], xof[:, m, 2:XW], ALPHA, ps[:, ob[m], 0:QW], ALU.mult, ALU.add),
                  reads=["xof", ("ps", ob[m])], writes=[("y1", m)])

        for m in range(0, 4):
            op_half(m, [4, 5, 6, 7], True, False)
        rms_stats()
        rms_chain()
        for m in range(4, 8):
            op_half(m, [4, 5, 6, 7], True, False)
        for m in range(0, 4):
            op_half(m, [0, 1, 2, 3], False, True)
            op_evac(m)
        for m in range(4, 8):
            op_half(m, [0, 1, 2, 3], False, True)
            op_evac(m)

        if j + 1 < nj:
            S.add("sp", DMA(xof[:].rearrange("p k w -> p (k w)"), xo[j + 1]), writes=["xof"], dsem=ds_xof)
        elif limit >= 4:
            for jj in range(nj - 1):
                issue_reload(jj)

        def ln1_out(m, t, tkey, j=j):
            if True:
                S.add("act", ACT(h1bs[:, m, :], t, AF.Identity, scale=pv[:, C_LN1G + m:C_LN1G + m + 1], bias=pv[:, C_LN1B + m:C_LN1B + m + 1]),
                      reads=[tkey, "pv"], writes=[("h1bs", m)])
            else:
                S.add("pool", TS(h1bs[:, m, :], t, pv[:, C_LN1G + m:C_LN1G + m + 1], ALU.mult, pv[:, C_LN1B + m:C_LN1B + m + 1], ALU.add),
                      reads=[tkey, "pv"], writes=[("h1bs", m)])
            S.add("act", ACT(y1[:, m, 2:QW], t[:, 2:QW], AF.Identity, scale=dv[:, V_AG1 + m:V_AG1 + m + 1], bias=dv[:, V_AB1 + m:V_AB1 + m + 1]),
                  reads=[tkey, "ag1", "ab1"], writes=[("y1", m)])

        layer_norm(lambda m: y1[:, m, :], lambda m: ("y1", m), QW, C_LN1G, C_LN1B, ln1_out)
        def ln1_tail(j=j):
            S.add("dve", TS(h1bs[:, :, 0:2], h1bs[:, :, 0:2], pv[:, C_HFLAG + j:C_HFLAG + j + 1], ALU.mult),
                  reads=[("h1bs", m) for m in range(KC)] + ["pv"], writes=[("h1bs", m) for m in range(KC)])
            S.add("sp", DMA(scr_b[j], h1bs[:].rearrange("p k w -> p (k w)")), reads=[("h1bs", m) for m in range(KC)],
                  writes=[("scr_b", j)], dsem=ds_sb)
            S.add("sp", DMA(scr_y[j].rearrange("p (k w) -> p k w", k=KC), y1[:, :, 2:QW]), reads=[("y1", m) for m in range(KC)],
                  writes=[("scr_y", j)], dsem=ds_sy)
        todo.append(ln1_tail)
        flush()

    if limit < 4:
        fin = ["pv", "cst", ("win", "k"), ("win", "v"), ("win", "q"), ("win", 1), "wout", "xof", ("xob", 0), ("xob", 1), ("xtb", 0), ("xtb", 1)]
        fin += [("scr_b", j) for j in range(nj)] + [("scr_y", j) for j in range(nj)]
        if debug:
            fin += [("dbgc", j) for j in range(nj)]
        S.add("dve", lambda e: e.memset(dv[:, NV_ - 1:NV_], 0.0), reads=fin, writes=["fin"])
        with nc.Block() as block:
            S.emit(block, sg)
        print("ops", len(S.ops), "waits", S.nwaits, "sems", nsem[0])
        return nc

    sqr[:] = [(t[:], ("sqr2", i)) for i, t in enumerate(sqr2)]
    lnt[:] = [(t[:], ("lnt2", i)) for i, t in enumerate(lnt2)]
    tt_[:] = [(t[:], ("tt2", i)) for i, t in enumerate(tt2)]
    ngrp = len(GROUPS)
    issue_reload(nj - 1)

    cidx = [0]

    def stage_up(g, j, t):
        b = g % 2
        h0, nh = GROUPS[g]
        at = actT[t % 2]
        slots = []
        for i in range(nh + 1):
            if i < nh:
                hc = h0 + i
                bg, bv = bank(), bank()
                for kc in range(KC):
                    S.add("pe", MM(ps[:, bg, 0:QW], wg[b][:, kc, 128 * i:128 * i + 128], h1T[:, j, kc, :],
                                   start=(kc == 0), stop=(kc == KC - 1)),
                          reads=[("wff", b), ("h1T", j)], writes=[("ps", bg)])
                for kc in range(KC):
                    S.add("pe", MM(ps[:, bv, 0:CW], wv[b][:, kc, 128 * i:128 * i + 128], h1T[:, j, kc, 2:QW],
                                   start=(kc == 0), stop=(kc == KC - 1)),
                          reads=[("wff", b), ("h1T", j)], writes=[("ps", bv)])
                ci = cidx[0] % 3
                cidx[0] += 1
                a_ = ca[ci]
                fw_ = C_FCW + 3 * hc
                S.add("act", ACT(a_[:], ps[:, bg, 0:CW], AF.Identity, scale=pv[:, fw_:fw_ + 1], bias=pv[:, C_FCB + hc:C_FCB + hc + 1]),
                      reads=[("ps", bg), "pv"], writes=[("ca", ci)])
                S.add("dve", STT(a_[:], ps[:, bg, 1:CW + 1], pv[:, fw_ + 1:fw_ + 2], a_[:], ALU.mult, ALU.add),
                      reads=[("ps", bg), "pv", ("ca", ci)], writes=[("ca", ci)])
                S.add("dve", STT(a_[:], ps[:, bg, 2:CW + 2], pv[:, fw_ + 2:fw_ + 3], a_[:], ALU.mult, ALU.add),
                      reads=[("ps", bg), "pv", ("ca", ci)], writes=[("ca", ci)])
                slots.append((ci, bv))
            if i >= 1:
                ci, bv = slots[i - 1]
                S.add("act", ACT(cs[ci][:], ca[ci][:], AF.Silu), reads=[("ca", ci)], writes=[("cs", ci)])
                S.add("dve", TT(at[:, i - 1, :], ps[:, bv, 0:CW], cs[ci][:], ALU.mult), reads=[("ps", bv), ("cs", ci)], writes=[("actT", t % 2, i - 1)])

    def stage_down(g, j, t):
        b = g % 2
        h0, nh = GROUPS[g]
        at = actT[t % 2]
        for mp in range(4):
            bd = bank()
            for half in range(2):
                m = 2 * mp + half
                for i in range(nh):
                    S.add("pe", MM(ps[:, bd, 256 * half:256 * half + 256], wd[b][:, i, 128 * m:128 * m + 128], at[:, i, :],
                                   start=(i == 0), stop=(i == nh - 1)),
                          reads=[("wff", b), ("actT", t % 2, i)], writes=[("ps", bd)])
            if 'pooladd' not in KX:
                S.add("dve", TT(y2[:, j, 2 * mp:2 * mp + 2, :], y2[:, j, 2 * mp:2 * mp + 2, :],
                                ps[:, bd, :].rearrange("p (a w) -> p a w", a=2), ALU.add),
                      reads=[("ps", bd), ("y2", j, 2 * mp), ("y2", j, 2 * mp + 1)], writes=[("y2", j, 2 * mp), ("y2", j, 2 * mp + 1)])
            else:
                di = dcnt[0] % 3
                dcnt[0] += 1
                S.add("act", ACT(dsb[di][:], ps[:, bd, :], AF.Identity), reads=[("ps", bd)], writes=[("dsb", di)])
                S.add("pool", TT(y2[:, j, 2 * mp:2 * mp + 2, :], y2[:, j, 2 * mp:2 * mp + 2, :],
                                 dsb[di][:].rearrange("p (a w) -> p a w", a=2), ALU.add),
                      reads=[("dsb", di), ("y2", j, 2 * mp), ("y2", j, 2 * mp + 1)], writes=[("y2", j, 2 * mp), ("y2", j, 2 * mp + 1)])

    def stage_ln2(j):
        def ln2_out(m, t, tkey, j=j):
            S.add("act", ACT(y2[:, j, m, :], t, AF.Identity, scale=pv[:, C_LN2G + m:C_LN2G + m + 1], bias=pv[:, C_LN2B + m:C_LN2B + m + 1]),
                  reads=[tkey, "pv"], writes=[("y2", j, m)])
        flush()
        layer_norm(lambda m, j=j: y2[:, j, m, :], lambda m, j=j: ("y2", j, m), CW, C_LN2G, C_LN2B, ln2_out)
        todo.append(lambda j=j: S.add("sp", DMA(outd[j], y2[:, j, :, :].rearrange("p k w -> p (k w)")),
                                      reads=[("y2", j, m) for m in range(KC)], writes=[("out", j)], dsem=ds_out))

    seq = [(g, j) for g in range(ngrp) for j in range(nj)]
    pend_ln = []
    for t, (g, j) in enumerate(seq):
        stage_up(g, j, t)
        flush()
        if t >= 1:
            gp, jp = seq[t - 1]
            stage_down(gp, jp, t - 1)
            if jp == nj - 1 and gp + 2 < ngrp:
                load_group(gp + 2)
            if gp == ngrp - 1:
                pend_ln.append(jp)
        if len(pend_ln) >= 2:
            stage_ln2(pend_ln.pop(0))
    gp, jp = seq[-1]
    stage_down(gp, jp, len(seq) - 1)
    pend_ln.append(jp)
    for jj in pend_ln:
        stage_ln2(jj)
    flush()
    fin = [("out", j) for j in range(nj)]
    if debug:
        fin += [("dbgc", j) for j in range(nj)]
    S.add("dve", lambda e: e.memset(dv[:, NV_ - 1:NV_], 0.0), reads=fin, writes=["fin"])

    with nc.Block() as block:
        S.emit(block, sg)
    print("ops", len(S.ops), "waits", S.nwaits, "sems", nsem[0])
    return nc


def _host_layout(x, lambda_q1, lambda_k1, lambda_q2, lambda_k2, attn_norm_g, conv_w, ln1_g, ln1_b,
                 ffn_conv_w, ffn_conv_b, ln2_g, ln2_b):
    f = np.float32
    pv_base = np.zeros((128, NP_), f)

    def pcols(v):
        return np.ascontiguousarray(np.asarray(v, f).reshape(-1, 128).T)

    pv_base[:, C_LN1G:C_LN1G + 8] = pcols(ln1_g[0])
    pv_base[:, C_LN1B:C_LN1B + 8] = pcols(ln1_b[0])
    pv_base[:, C_LN2G:C_LN2G + 8] = pcols(ln2_g[0])
    pv_base[:, C_LN2B:C_LN2B + 8] = pcols(ln2_b[0])
    pv_base[:, C_ANG] = np.asarray(attn_norm_g[0], f)
    cw = np.asarray(conv_w[0], f)
    for fc in range(4):
        for tap in range(3):
            pv_base[:, C_CW + 3 * fc + tap] = cw[tap, 128 * fc:128 * fc + 128]
    fw = np.asarray(ffn_conv_w[0], f)
    for hc in range(NHC):
        for tap in range(3):
            pv_base[:, C_FCW + 3 * hc + tap] = fw[tap, 128 * hc:128 * hc + 128]
    pv_base[:, C_FCB:C_FCB + NHC] = pcols(ffn_conv_b[0])
    lam = np.concatenate([np.asarray(v[0], f) for v in (lambda_q1, lambda_k1, lambda_q2, lambda_k2)])
    pv_base[:, C_LAM:C_LAM + 256] = lam[None, :]

    in_maps_part = []
    for core in range(8):
        bi, c = core // 2, core % 2
        xT = np.zeros((D, SEQ + 4), f)
        xT[:, 4:] = np.asarray(x[bi], f).T
        xo = np.zeros((NJ, 128, KC * XW), f)
        xt = np.zeros((NJ, 128, KC * CW), f)
        for j in range(NJ):
            t0 = 256 * (2 * j + c)
            t1 = 256 * (2 * j + 1 - c)
            w = xT[:, t0:t0 + XW].reshape(KC, 128, XW).transpose(1, 0, 2)
            xo[j] = w.reshape(128, KC * XW)
            w = xT[:, 4 + t1:4 + t1 + CW].reshape(KC, 128, CW).transpose(1, 0, 2)
            xt[j] = w.reshape(128, KC * CW)
        pvc = pv_base.copy()
        pvc[:, C_HFLAG:C_HFLAG + NJ] = 1.0
        if c == 0:
            pvc[:, C_HFLAG] = 0.0
        cst = np.zeros((128, 128 + 9 * QW), f)
        cst[:, 0:128] = np.eye(128, dtype=f)
        p = np.arange(128)[:, None]
        q = np.arange(QW)[None, :]
        for ms in range(2):
            for r in range(4):
                if r < 2:
                    key_rel = 256 * c + 128 * r + p
                else:
                    key_rel = 256 * (1 - c) + 128 * (r - 2) + p
                qry_rel = 256 * c - 2 + q
                allowed = key_rel <= qry_rel
                if ms == 0 and c == 0:
                    allowed = allowed | (q < 2)
                o = 128 + (ms * 4 + r) * QW
                cst[:, o:o + QW] = np.where(allowed, 0.0, NEG).astype(f)
        in_maps_part.append({"xo": xo, "xt": xt, "pv": pvc, "cst": cst})
    return in_maps_part


_CACHE = {}


def kernel(x, w_in, lambda_q1, lambda_k1, lambda_q2, lambda_k2, attn_norm_g, conv_w, w_out,
           ln1_g, ln1_b, ffn_w_up, ffn_conv_w, ffn_conv_b, ffn_w_down, ln2_g, ln2_b, _debug=False):
    x = np.asarray(x)
    parts = _host_layout(x, lambda_q1, lambda_k1, lambda_q2, lambda_k2, attn_norm_g, conv_w, ln1_g, ln1_b,
                         ffn_conv_w, ffn_conv_b, ln2_g, ln2_b)
    shared = {
        "w_in": np.ascontiguousarray(np.asarray(w_in, np.float32)[0]),
        "w_out": np.ascontiguousarray(np.asarray(w_out, np.float32)[0]),
        "w_up": np.ascontiguousarray(np.asarray(ffn_w_up, np.float32)[0]),
        "w_down": np.ascontiguousarray(np.asarray(ffn_w_down, np.float32)[0]),
    }
    in_maps = [dict(p, **shared) for p in parts]
    key = ("nc", bool(_debug))
    if key not in _CACHE:
        _CACHE[key] = build_program(debug=_debug)
    nc = _CACHE[key]
    res = run_bass_kernel_spmd(nc, in_maps, core_ids=list(range(8)))
    out = np.zeros((NB, SEQ, D), np.float32)
    for core in range(8):
        bi, c = core // 2, core % 2
        for j in range(NJ):
            t0 = 256 * (2 * j + c)
            o = res.results[core]["out%d" % j].reshape(128, KC, CW)
            out[bi, t0:t0 + CW, :] = o.transpose(2, 1, 0).reshape(CW, D)
    if _debug:
        return out, res
    return out
```

```python
import math
import numpy as np
import concourse.bass as bass
import concourse.mybir as mybir
from concourse.bass_utils import run_bass_kernel_spmd

F32 = mybir.dt.float32
BF16 = mybir.dt.bfloat16
AF = mybir.ActivationFunctionType
ALU = mybir.AluOpType
AX = mybir.AxisListType

D = 1024
SEQ = 4096
NB = 4
KC = 8
NJ = 8
CW = 256
QW = 258
XW = 260
DFF = 2816
NHC = 22
GRP = 4
GROUPS = [(0, 2), (2, 4), (6, 4), (10, 4), (14, 4), (18, 4)]
ALPHA = (2.0 * 1) ** 0.25
LAM_INIT = 0.8 - 0.6 * math.exp(0.0)
LN_EPS = 1e-5
RMS_EPS = 1e-5
NEG = -30000.0

C_LN1G, C_LN1B, C_LN2G, C_LN2B = 0, 8, 16, 24
C_ANG = 32
C_CW = 33
C_FCW = 45
C_FCB = 111
C_LAM = 133
C_HFLAG = 389
NP_ = 397
V_AG1, V_AB1, V_GSC, V_NEGLAM, V_T0, V_T1, V_E0, V_E1 = 0, 8, 16, 17, 18, 19, 20, 21
NV_ = 24

SEM_ROLL = 1000


class DSem:
    def __init__(self, sem):
        self.sem = sem
        self.count = 0


class Op:
    __slots__ = ("eng", "fn", "deps", "dsem", "ticket", "signal", "clock", "idx", "ndma")


class Sched:
    ENGS = ("pe", "act", "dve", "pool", "sp")

    def __init__(self, nc, same_engine_sync=True):
        self.nc = nc
        self.ops = []
        self.lastw = {}
        self.rd_eng = {}
        self.rd_dma = {}
        self.same_engine_sync = same_engine_sync

    def add(self, eng, fn, reads=(), writes=(), dsem=None, ndma=1):
        op = Op()
        op.eng, op.fn, op.dsem, op.ndma = eng, fn, dsem, ndma
        op.idx = len(self.ops)
        op.ticket = None
        op.signal = dsem is not None
        op.clock = None
        deps = set()
        for r in reads:
            w = self.lastw.get(r)
            if w is not None:
                deps.add(w)
        for w_ in writes:
            w = self.lastw.get(w_)
            if w is not None:
                deps.add(w)
            for i in self.rd_eng.get(w_, {}).values():
                deps.add(i)
            for i in self.rd_dma.get(w_, ()):
                deps.add(i)
        for w_ in writes:
            self.lastw[w_] = op.idx
            self.rd_eng[w_] = {}
            self.rd_dma[w_] = []
        for r in reads:
            if r in writes:
                continue
            if dsem is not None:
                self.rd_dma.setdefault(r, []).append(op.idx)
            else:
                self.rd_eng.setdefault(r, {})[eng] = op.idx
        keep = set()
        best = {}
        bestd = {}
        for d in deps:
            p = self.ops[d]
            if p.dsem is not None:
                k_ = id(p.dsem)
                if k_ not in bestd or bestd[k_] < d:
                    bestd[k_] = d
                continue
            if p.eng == eng and dsem is None and (eng == "pe" or not self.same_engine_sync):
                continue
            if p.eng not in best or best[p.eng] < d:
                best[p.eng] = d
        keep.update(best.values())
        keep.update(bestd.values())
        op.deps = keep
        for d in keep:
            self.ops[d].signal = True
        self.ops.append(op)
        return op

    def emit(self, block, sems):
        cnt = {e: 0 for e in self.ENGS}
        semlist = {e: [] for e in self.ENGS}
        for op in self.ops:
            if op.dsem is not None:
                op.dsem.count += 16 * op.ndma
                op.ticket = ("d", op.dsem.sem, op.dsem.count)
            elif op.signal:
                c = cnt[op.eng]
                si, v = divmod(c, SEM_ROLL)
                if si >= len(semlist[op.eng]):
                    semlist[op.eng].append(next(sems))
                op.ticket = ("e", semlist[op.eng][si], v + 1)
                cnt[op.eng] = c + 1
        streams = {e: [] for e in self.ENGS}
        clock = {e: {} for e in self.ENGS}
        nwaits = 0
        for op in self.ops:
            ck = clock[op.eng]
            st = streams[op.eng]
            for d in sorted(op.deps):
                p = self.ops[d]
                t = p.ticket
                key = id(t[1])
                if ck.get(key, 0) >= t[2]:
                    continue
                st.append(("w", t[1], t[2]))
                nwaits += 1
                ck[key] = t[2]
                if p.clock is not None:
                    for k, v in p.clock.items():
                        if ck.get(k, 0) < v:
                            ck[k] = v
            st.append(("o", op))
            if op.signal:
                op.clock = dict(ck)
        self.nwaits = nwaits
        print("signal counts", cnt, "dma max", max([op.ticket[2] for op in self.ops if op.dsem is not None] + [0]))
        self.streams = streams

        def run(e):
            def body(eng):
                for item in streams[e]:
                    if item[0] == "w":
                        eng.wait_ge(item[1], item[2])
                    else:
                        op = item[1]
                        ins = op.fn(eng)
                        if op.dsem is not None:
                            if not isinstance(ins, (list, tuple)):
                                ins = [ins]
                            assert len(ins) == op.ndma
                            for i_ in ins:
                                i_.then_inc(op.dsem.sem, 16)
                        elif op.signal:
                            ins.then_inc(op.ticket[1], 1)
            return body

        block.tensor(run("pe"))
        block.scalar(run("act"))
        block.vector(run("dve"))
        block.gpsimd(run("pool"))
        block.sync(run("sp"))


def MM(out, lhsT, rhs, start=True, stop=True):
    return lambda e: e.matmul(out, lhsT=lhsT, rhs=rhs, start=start, stop=stop)


def ACT(out, in_, func, scale=None, bias=None):
    kw = {}
    if scale is not None:
        kw["scale"] = scale
    if bias is not None:
        kw["bias"] = bias
    return lambda e: e.activation(out=out, in_=in_, func=func, **kw)


def TT(out, a, b, op):
    return lambda e: e.tensor_tensor(out=out, in0=a, in1=b, op=op)


def TS(out, a, s1, op0, s2=None, op1=None):
    if op1 is None:
        return lambda e: e.tensor_scalar(out=out, in0=a, scalar1=s1, scalar2=None, op0=op0)
    return lambda e: e.tensor_scalar(out=out, in0=a, scalar1=s1, scalar2=s2, op0=op0, op1=op1)


def STT(out, a, s, b, op0, op1):
    return lambda e: e.scalar_tensor_tensor(out=out, in0=a, scalar=s, in1=b, op0=op0, op1=op1)


def CP(out, in_):
    return lambda e: e.tensor_copy(out=out, in_=in_)


def RECIP(out, in_):
    return lambda e: e.reciprocal(out=out, in_=in_)


def DMA(out, in_):
    return lambda e: e.dma_start(out=out, in_=in_)


def DMAS(pairs):
    return lambda e: [e.dma_start(out=o, in_=i) for (o, i) in pairs]


class Arena:
    def __init__(self, nc, base, top):
        self.nc, self.cur, self.top = nc, base, top
        self.n = 0

    def alloc(self, name, shape, dtype):
        esz = 4 if dtype == F32 else 2
        size = esz
        for s in shape[1:]:
            size *= s
        off = (self.cur + 31) // 32 * 32
        self.cur = off + size
        self.last_off = off
        assert self.cur <= self.top, (name, self.cur, self.top)
        self.n += 1
        return self.nc.alloc_sbuf_tensor_at("%s_%d" % (name, self.n), list(shape), dtype, offset=off)


def build_program(nj=NJ, debug=False, limit=99):
    import os
    KX = set(os.environ.get('KX', '').split(','))
    nc = bass.Bass("TRN2", target_bir_lowering=False)

    def dram(name, shape, dtype, kind):
        return nc.dram_tensor(name, list(shape), dtype, kind=kind).ap()

    xo = dram("xo", [NJ, 128, KC * XW], F32, "ExternalInput")
    xt = dram("xt", [NJ, 128, KC * CW], F32, "ExternalInput")
    pvd = dram("pv", [128, NP_], F32, "ExternalInput")
    cstd = dram("cst", [128, 128 + 9 * QW], F32, "ExternalInput")
    w_in = dram("w_in", [D, 3072], F32, "ExternalInput")
    w_out = dram("w_out", [D, D], F32, "ExternalInput")
    w_up = dram("w_up", [D, 2 * DFF], F32, "ExternalInput")
    w_down = dram("w_down", [DFF, D], F32, "ExternalInput")
    outd = [dram("out%d" % j, [D, CW], F32, "ExternalOutput") for j in range(NJ)]
    scr_b = [dram("scr_b%d" % j, [128, KC * QW], BF16, "ExternalOutput") for j in range(NJ)]
    scr_y = [dram("scr_y%d" % j, [128, KC * CW], F32, "ExternalOutput") for j in range(NJ)]
    if debug:
        dbg_cat = dram("dbg_cat", [NJ, 128, KC * QW], F32, "ExternalOutput")
        dbg_h1 = dram("dbg_h1", [NJ, 128, KC * CW], F32, "ExternalOutput")

    nsem = [0]

    def semgen():
        while True:
            nsem[0] += 1
            yield nc.alloc_semaphore("sm%d" % nsem[0])

    sg = semgen()

    def dsem():
        return DSem(next(sg))

    S = Sched(nc)
    base0 = (nc.sbuf_base + 63) // 64 * 64
    top0 = nc.sbuf_top - 64
    arena_t = nc.alloc_sbuf_tensor("arena", [128, top0 - base0], mybir.dt.uint8)
    A = Arena(nc, base0, top0)
    ps = nc.alloc_psum_tensor("ps", [128, 8, 512], F32)

    pv = A.alloc("pv", [128, NP_], F32)
    dv = A.alloc("dv", [128, NV_], F32)
    lamt = A.alloc("lamt", [128, 128], F32)
    ones_f = A.alloc("ones_f", [128, 128], F32)
    ones_b = A.alloc("ones_b", [128, 128], BF16)
    cstb = A.alloc("cstb", [128, 128 + 9 * QW], BF16)
    ident_b = cstb[:, 0:128]

    def maskb(ms, r):
        o = 128 + (ms * 4 + r) * QW
        return cstb[:, o:o + QW]

    mark_persist = A.cur

    winb = A.alloc("winb", [128, KC, 3072], BF16)
    winb_off = A.last_off
    woutb = A.alloc("woutb", [128, KC, D], BF16)
    KT = A.alloc("KT", [128, 4, SEQ], BF16)
    Vt = A.alloc("Vt", [128, 32, 512], BF16)
    xof = A.alloc("xof", [128, KC, XW], F32)
    xob = [A.alloc("xob", [128, KC, XW], BF16) for _ in range(2)]
    xtb = [A.alloc("xtb", [128, KC, CW], BF16) for _ in range(2)]
    qTz = A.alloc("qTz", [128, 4, 2, QW], BF16)
    catT = A.alloc("catT", [128, KC, QW], BF16)
    hc_sb = [A.alloc("hc_sb", [128, XW], F32) for _ in range(2)]
    u_sb = [A.alloc("u_sb", [128, XW], F32) for _ in range(2)]
    acc_sb = [A.alloc("acc_sb", [128, QW], F32) for _ in range(2)]
    pbuf = [A.alloc("pbuf", [128, 2, QW], BF16) for _ in range(4)]
    tmpW = [A.alloc("tmpW", [128, 2, QW], F32) for _ in range(6)]
    tmpW_off = A.last_off - 5 * 2 * QW * 4 - 5 * 32

    def TH(k, h):
        return (tmpW[k][:, h, :], ("tmp", k, h))
    y1 = A.alloc("y1", [128, KC, QW], F32)
    lnt = [TH(0, 0), TH(0, 1), TH(1, 0), TH(1, 1), TH(2, 0)]
    tt_ = [TH(3, 0), TH(3, 1)]
    sqr = [TH(4, 0), TH(4, 1)]
    h1bs = A.alloc("h1bs", [128, KC, QW], BF16)
    print("phase1 sbuf end", A.cur, "of", top0)
    p1_end = A.cur
    A.cur = winb_off
    wg, wv, wd = [], [], []
    for _ in range(2):
        wg.append(A.alloc("wg", [128, KC, 512], BF16))
        wv.append(A.alloc("wv", [128, KC, 512], BF16))
        wd.append(A.alloc("wd", [128, GRP, D], BF16))
    wff_end = A.cur
    assert wff_end <= winb_off + KC * 3072 * 2
    A.cur = wff_end
    h1T = A.alloc("h1T", [128, NJ, KC, QW], BF16)
    y2 = A.alloc("y2", [128, NJ, KC, CW], F32)
    actT = [A.alloc("actT", [128, GRP, CW], BF16) for _ in range(2)]
    ca = [A.alloc("ca", [128, CW], F32) for _ in range(3)]
    cs = [A.alloc("cs", [128, CW], F32) for _ in range(3)]
    dcnt = [0]
    sqr2 = [A.alloc("sqr2", [128, CW], F32) for _ in range(2)]
    lnt2 = [A.alloc("lnt2", [128, CW], F32) for _ in range(5)]
    tt2 = [A.alloc("ttmp2", [128, CW], F32) for _ in range(2)]
    print("phase2 sbuf end", A.cur, "of", top0)
    p2_end = A.cur
    ds_h1T = [dsem() for _ in range(nj)]
    ds_y2 = [dsem() for _ in range(nj)]
    ds_out = dsem()
    p1_done = [("scr_b", nj - 1), ("scr_y", nj - 1)]
    assert A.cur <= tmpW_off, (A.cur, tmpW_off)
    dead_keys = ["wout", "xof"] + [("cat", k) for k in range(KC)] + [("qT", h) for h in range(4)] + [("xob", 0), ("xob", 1), ("xtb", 0), ("xtb", 1)]
    A.cur = p1_end

    ds_pv, ds_cst = dsem(), dsem()
    ds_win = {"k": dsem(), "v": dsem(), "q": dsem(), 1: dsem()}
    ds_wout = dsem()
    ds_xof = dsem()
    ds_xob = [dsem(), dsem()]
    ds_xtb = [dsem(), dsem()]
    ds_sb, ds_sy = dsem(), dsem()
    ds_dbg = dsem()

    ds_wff = [dsem(), dsem()]

    def load_group(g):
        b = g % 2
        h0, nh = GROUPS[g]
        pairs = []
        for kc in range(KC):
            pairs.append((wg[b][:, kc, 0:128 * nh], w_up[128 * kc:128 * kc + 128, 128 * h0:128 * (h0 + nh)]))
            pairs.append((wv[b][:, kc, 0:128 * nh], w_up[128 * kc:128 * kc + 128, DFF + 128 * h0:DFF + 128 * (h0 + nh)]))
        for i in range(nh):
            pairs.append((wd[b][:, i, :], w_down[128 * (h0 + i):128 * (h0 + i) + 128, :]))
        S.add("pool", DMAS(pairs), writes=[("wff", b), ("win", "k"), ("win", "v"), ("win", "q"), ("win", 1)],
              dsem=ds_wff[b], ndma=len(pairs))

    def issue_reload(j):
        S.add("sp", DMA(h1T[:, j, :, :], scr_b[j].rearrange("p (k w) -> p k w", k=KC)),
              reads=[("scr_b", j)], writes=[("h1T", j)] + dead_keys, dsem=ds_h1T[j])
        S.add("sp", DMA(y2[:, j, :, :], scr_y[j].rearrange("p (k w) -> p k w", k=KC)),
              reads=[("scr_y", j)], writes=[("y2", j, m) for m in range(KC)] + dead_keys, dsem=ds_y2[j])


    rb = [0]

    def bank():
        b = rb[0]
        rb[0] = (b + 1) % 8
        return b

    S.add("sp", DMA(pv[:], pvd[:, :]), writes=["pv"], dsem=ds_pv)
    S.add("pool", DMAS([(cstb[:, 0:1024], cstd[:, 0:1024]), (cstb[:, 1024:128 + 9 * QW], cstd[:, 1024:128 + 9 * QW])]),
          writes=["cst"], dsem=ds_cst, ndma=2)
    S.add("dve", lambda e: e.memset(ones_f[:], 1.0), writes=["ones_f"])
    S.add("dve", lambda e: e.memset(ones_b[:], 1.0), writes=["ones_b"])
    S.add("dve", lambda e: e.memset(qTz[:].rearrange("p a b w -> p (a b w)"), 0.0), writes=[("qT", h) for h in range(4)])

    def xo3(j):
        return xo[j].rearrange("p (k w) -> p k w", k=KC)

    def xt3(j):
        return xt[j].rearrange("p (k w) -> p k w", k=KC)

    def issue_loads(j):
        b = j % 2
        S.add("pool", DMA(xob[b][:], xo3(j)), writes=[("xob", b)], dsem=ds_xob[b])
        S.add("pool", DMA(xtb[b][:], xt3(j)), writes=[("xtb", b)], dsem=ds_xtb[b])

    issue_loads(0)
    for grp, c0, cn in (("k", 512, 512), ("v", 1024, 512), ("q", 0, 512), (1, 1536, 1536)):
        S.add("pool", DMAS([(winb[:, kc, c0:c0 + cn], w_in[128 * kc:128 * kc + 128, c0:c0 + cn]) for kc in range(KC)]),
              writes=[("win", grp)], dsem=ds_win[grp], ndma=KC)
    S.add("sp", DMA(xof[:], xo3(0)), writes=["xof"], dsem=ds_xof)
    S.add("pool", DMAS([(woutb[:, kc, :], w_out[128 * kc:128 * kc + 128, :]) for kc in range(KC)]),
          writes=["wout"], dsem=ds_wout, ndma=KC)

    S.add("dve", TT(lamt[:, 0:64], pv[:, C_LAM:C_LAM + 64], pv[:, C_LAM + 64:C_LAM + 128], ALU.mult), reads=["pv"], writes=["lamt0"])
    S.add("dve", TT(lamt[:, 64:128], pv[:, C_LAM + 128:C_LAM + 192], pv[:, C_LAM + 192:C_LAM + 256], ALU.mult), reads=["pv"], writes=["lamt1"])
    S.add("dve", lambda e: e.reduce_sum(out=dv[:, V_T0:V_T0 + 1], in_=lamt[:, 0:64], axis=AX.X), reads=["lamt0"], writes=["dvt0"])
    S.add("dve", lambda e: e.reduce_sum(out=dv[:, V_T1:V_T1 + 1], in_=lamt[:, 64:128], axis=AX.X), reads=["lamt1"], writes=["dvt1"])
    S.add("act", ACT(dv[:, V_E0:V_E0 + 2], dv[:, V_T0:V_T0 + 2], AF.Exp), reads=["dvt0", "dvt1"], writes=["dve01"])
    S.add("dve", TT(dv[:, V_NEGLAM:V_NEGLAM + 1], dv[:, V_E1:V_E1 + 1], dv[:, V_E0:V_E0 + 1], ALU.subtract), reads=["dve01"], writes=["neglam0"])
    S.add("dve", TS(dv[:, V_NEGLAM:V_NEGLAM + 1], dv[:, V_NEGLAM:V_NEGLAM + 1], -LAM_INIT, ALU.add), reads=["neglam0"], writes=["neglam"])
    S.add("dve", TS(dv[:, V_GSC:V_GSC + 1], pv[:, C_ANG:C_ANG + 1], 1.0 - LAM_INIT, ALU.mult), reads=["pv"], writes=["gsc"])
    S.add("dve", TS(dv[:, V_AG1:V_AG1 + 8], pv[:, C_LN1G:C_LN1G + 8], ALPHA, ALU.mult), reads=["pv"], writes=["ag1"])
    S.add("dve", TS(dv[:, V_AB1:V_AB1 + 8], pv[:, C_LN1B:C_LN1B + 8], ALPHA, ALU.mult), reads=["pv"], writes=["ab1"])
    neglam = dv[:, V_NEGLAM:V_NEGLAM + 1]
    gsc = dv[:, V_GSC:V_GSC + 1]

    evac_flip = [0]

    def evac(out_ap, in_ap, reads, writes):
        evac_flip[0] ^= 1
        if evac_flip[0]:
            S.add("dve", CP(out_ap, in_ap), reads=reads, writes=writes)
        else:
            S.add("act", ACT(out_ap, in_ap, AF.Identity), reads=reads, writes=writes)

    todo = []

    def drain(n):
        for _ in range(min(n, len(todo))):
            todo.pop(0)()

    def flush():
        drain(len(todo))

    def layer_norm(src, src_keys, width, gcol, bcol, emit_out):
        b1, b2 = bank(), bank()
        s1 = ps[:, b1, 0:width]
        s2 = ps[:, b2, 0:width]
        for m in range(KC):
            sq, sqk = sqr[m % 2]
            S.add("act", ACT(sq[:, 0:width], src(m), AF.Square), reads=[src_keys(m)], writes=[sqk])
            S.add("pe", MM(s1, ones_f[:], src(m), start=(m == 0), stop=(m == KC - 1)), reads=["ones_f", src_keys(m)], writes=[("ps", b1)])
            S.add("pe", MM(s2, ones_f[:], sq[:, 0:width], start=(m == 0), stop=(m == KC - 1)), reads=["ones_f", sqk], writes=[("ps", b2)])
        (mean, kmean), (msq, kmsq), (var, kvar), (rstd, krstd), (nmr, knmr) = [(t[:, 0:width], k) for (t, k) in lnt]
        S.add("dve", TS(mean, s1, 1.0 / D, ALU.mult), reads=[("ps", b1)], writes=[kmean])
        S.add("dve", TT(msq, mean, mean, ALU.mult), reads=[kmean], writes=[kmsq])
        S.add("dve", STT(var, s2, 1.0 / D, msq, ALU.mult, ALU.subtract), reads=[("ps", b2), kmsq], writes=[kvar])
        S.add("dve", TS(var, var, LN_EPS, ALU.add), reads=[kvar], writes=[kvar])
        S.add("act", ACT(var, var, AF.Ln), reads=[kvar], writes=[kvar])
        S.add("act", ACT(rstd, var, AF.Exp, scale=-0.5), reads=[kvar], writes=[krstd])
        S.add("dve", STT(nmr, mean, -1.0, rstd, ALU.mult, ALU.mult), reads=[kmean, krstd], writes=[knmr])
        def norm_step(m, t, tk):
            def go():
                S.add("dve", TT(t, src(m), rstd, ALU.mult), reads=[src_keys(m), krstd], writes=[tk])
                S.add("dve", TT(t, t, nmr, ALU.add), reads=[tk, knmr], writes=[tk])
                emit_out(m, t, tk)
            return go

        for m in range(KC):
            t, tk = tt_[m % 2]
            todo.append(norm_step(m, t[:, 0:width], tk))

    for j in range(nj if limit >= 1 else 0):
        b = j % 2
        xb, xtb_ = xob[b], xtb[b]
        if j + 1 < nj:
            issue_loads(j + 1)
        for h in range(4):
            bk = bank()
            col = 512 + 128 * h
            for half, (src, skey) in enumerate([(lambda kc: xb[:, kc, 4:XW], ("xob", b)), (lambda kc: xtb_[:, kc, :], ("xtb", b))]):
                for kc in range(KC):
                    S.add("pe", MM(ps[:, bk, 256 * half:256 * half + 256], winb[:, kc, col:col + 128], src(kc),
                                   start=(kc == 0), stop=(kc == KC - 1)),
                          reads=[("win", "k"), skey], writes=[("ps", bk)])
            evac(KT[:, h, 512 * j:512 * j + 512], ps[:, bk, :], [("ps", bk)], [("KT", h, j)])
            drain(1)
        if limit < 1.2:
            continue
        for blk in range(4):
            bk = bank()
            for kc in range(KC):
                if blk < 2:
                    lt = xb[:, kc, 4 + 128 * blk:4 + 128 * blk + 128]
                    skey = ("xob", b)
                else:
                    lt = xtb_[:, kc, 128 * (blk - 2):128 * (blk - 2) + 128]
                    skey = ("xtb", b)
                S.add("pe", MM(ps[:, bk, :], lt, winb[:, kc, 1024:1536], start=(kc == 0), stop=(kc == KC - 1)),
                      reads=[("win", "v"), skey], writes=[("ps", bk)])
            evac(Vt[:, 4 * j + blk, :], ps[:, bk, :], [("ps", bk)], [("V", 4 * j + blk)])
            drain(1)
        if limit < 1.4:
            continue
        for h in range(4):
            bk = bank()
            for kc in range(KC):
                S.add("pe", MM(ps[:, bk, 0:QW], winb[:, kc, 128 * h:128 * h + 128], xb[:, kc, 2:XW],
                               start=(kc == 0), stop=(kc == KC - 1)),
                      reads=[("win", "q"), ("xob", b)], writes=[("ps", bk)])
            S.add("dve", CP(qTz[0:64, h, 0, :], ps[0:64, bk, 0:QW]), reads=[("ps", bk)], writes=[("qT", h)])
            S.add("act", ACT(qTz[64:128, h, 1, :], ps[64:128, bk, 0:QW], AF.Identity), reads=[("ps", bk)], writes=[("qT", h)])
            drain(1)
        if limit < 1.6:
            continue
        def conv_branch(xb=xb, b=b):
            for fc in range(4):
                bB, bC, bH = bank(), bank(), bank()
                for (bk, c0) in [(bH, 2560), (bC, 2048), (bB, 1536)]:
                    col = c0 + 128 * fc
                    for kc in range(KC):
                        S.add("pe", MM(ps[:, bk, 0:XW], winb[:, kc, col:col + 128], xb[:, kc, :],
                                       start=(kc == 0), stop=(kc == KC - 1)),
                              reads=[("win", 1), ("xob", b)], writes=[("ps", bk)])
                t = fc % 2
                hs, us, ac = hc_sb[t], u_sb[t], acc_sb[t]
                S.add("act", ACT(hs[:], ps[:, bH, 0:XW], AF.Identity), reads=[("ps", bH)], writes=[("hc_sb", t)])
                S.add("dve", TT(us[:], ps[:, bC, 0:XW], hs[:], ALU.mult), reads=[("ps", bC), ("hc_sb", t)], writes=[("u_sb", t)])
                cw = C_CW + 3 * fc
                S.add("dve", TS(ac[:], us[:, 2:XW], pv[:, cw + 2:cw + 3], ALU.mult), reads=[("u_sb", t), "pv"], writes=[("acc", t)])
                S.add("dve", STT(ac[:], us[:, 1:XW - 1], pv[:, cw + 1:cw + 2], ac[:], ALU.mult, ALU.add), reads=[("u_sb", t), "pv", ("acc", t)], writes=[("acc", t)])
                S.add("dve", STT(ac[:], us[:, 0:XW - 2], pv[:, cw:cw + 1], ac[:], ALU.mult, ALU.add), reads=[("u_sb", t), "pv", ("acc", t)], writes=[("acc", t)])
                S.add("dve", TT(catT[:, 4 + fc, :], ps[:, bB, 2:XW], ac[:], ALU.mult), reads=[("ps", bB), ("acc", t)], writes=[("cat", 4 + fc)])


        if j > 0:
            conv_branch()
        if j == nj - 1 and limit >= 4:
            load_group(0)
            load_group(1)
        if limit < 2:
            continue
        flush()
        nkb = 4 * j + 4
        if 'noattn' in KX:
            nkb = 0
        ms = 0 if j == 0 else 1
        pcount = [0]
        accb = [4, 5, 6, 7]
        LAG = 2
        NPB = len(pbuf)

        def epilogue_a(h):
            osb, lsb = tmpW[0], tmpW[1]
            ko = [("tmp", 0, 0), ("tmp", 0, 1)]
            kl = [("tmp", 1, 0), ("tmp", 1, 1)]
            o_, ok_ = TH(2 + h // 2, h % 2)
            sq_, sqk_ = TH(4 + h // 2, h % 2)
            S.add("dve", CP(osb[:], ps[:, 4:6, 0:QW]), reads=[("ps", 4), ("ps", 5)], writes=ko)
            S.add("act", ACT(lsb[:], ps[:, 6:8, 0:QW], AF.Identity), reads=[("ps", 6), ("ps", 7)], writes=kl)
            S.add("dve", RECIP(lsb[:], lsb[:]), reads=kl, writes=kl)
            S.add("dve", TT(osb[:], osb[:], lsb[:], ALU.mult), reads=ko + kl, writes=ko)
            S.add("dve", STT(o_, osb[:, 1, :], neglam, osb[:, 0, :], ALU.mult, ALU.add), reads=ko + ["neglam"], writes=[ok_])
            S.add("dve", TT(sq_, o_, o_, ALU.mult), reads=[ok_], writes=[sqk_])

        def do_pv(h, kb, pi):
            pb = pbuf[pi]
            first, last = (kb == 0), (kb == nkb - 1)
            for mp in range(2):
                S.add("pe", MM(ps[:, accb[mp], 0:QW], Vt[:, kb, 128 * h:128 * h + 128], pb[:, mp, :], start=first, stop=last),
                      reads=[("V", kb), ("pbuf", pi)], writes=[("ps", accb[mp])])
                S.add("pe", MM(ps[:, accb[2 + mp], 0:QW], ones_b[:], pb[:, mp, :], start=first, stop=last),
                      reads=["ones_b", ("pbuf", pi)], writes=[("ps", accb[2 + mp])])
            if last:
                epilogue_a(h)

        pendq = []
        items = [(h, kb) for h in range(4 if nkb else 0) for kb in range(nkb)]
        for idx, (h, kb) in enumerate(items):
            sp_ = (idx % 2) * 2
            masked = (kb >= 4 * j) or ('allmask' in KX)
            kj = kb // 4
            for mp in range(2):
                sdst = ps[:, sp_ + mp, 0:QW]
                if masked:
                    S.add("pe", MM(sdst, ident_b, (maskb(ms, kb - 4 * j) if kb >= 4 * j else maskb(2, 0)), start=True, stop=False),
                          reads=["cst"], writes=[("ps", sp_ + mp)])
                S.add("pe", MM(sdst, KT[:, h, 128 * kb:128 * kb + 128], qTz[:, h, mp, :],
                               start=(not masked), stop=True),
                      reads=[("KT", h, kj), ("qT", h)], writes=[("ps", sp_ + mp)])
            pi = pcount[0] % NPB
            pcount[0] += 1
            S.add("act", ACT(pbuf[pi][:], ps[:, sp_:sp_ + 2, 0:QW], AF.Exp, scale=0.125),
                  reads=[("ps", sp_), ("ps", sp_ + 1)], writes=[("pbuf", pi)])
            pendq.append((h, kb, pi))
            if len(pendq) > LAG:
                do_pv(*pendq.pop(0))
        while pendq:
            do_pv(*pendq.pop(0))

        if j == 0:
            conv_branch()
        nh_ = 4 if nkb else 0

        def rms_stats():
            for h in range(nh_):
                sq_, sqk_ = TH(4 + h // 2, h % 2)
                S.add("pe", MM(ps[:, h, 0:QW], ones_f[:], sq_, start=True, stop=True), reads=["ones_f", sqk_], writes=[("ps", h)])

        def rms_chain():
            for h in range(nh_):
                lnv, lk_ = TH(4 + h // 2, h % 2)
                S.add("dve", TS(lnv, ps[:, h, 0:QW], 1.0 / 128.0, ALU.mult, RMS_EPS, ALU.add), reads=[("ps", h)], writes=[lk_])
                S.add("act", ACT(lnv, lnv, AF.Ln), reads=[lk_], writes=[lk_])
                S.add("act", ACT(lnv, lnv, AF.Exp, scale=-0.5), reads=[lk_], writes=[lk_])
            for h in range(nh_):
                o_, ok_ = TH(2 + h // 2, h % 2)
                lnv, lk_ = TH(4 + h // 2, h % 2)
                S.add("dve", STT(catT[:, h, :], o_, gsc, lnv, ALU.mult, ALU.mult), reads=[ok_, "gsc", lk_], writes=[("cat", h)])

        if limit < 3:
            rms_stats()
            rms_chain()
            continue
        ob = [4, 5, 6, 7, 0, 1, 2, 3]

        def op_half(m, kcs, first, last):
            for ki, kc in enumerate(kcs):
                S.add("pe", MM(ps[:, ob[m], 0:QW], woutb[:, kc, 128 * m:128 * m + 128], catT[:, kc, :],
                               start=(first and ki == 0), stop=(last and ki == len(kcs) - 1)),
                      reads=["wout", ("cat", kc)], writes=[("ps", ob[m])])

        def op_evac(m):
            S.add("dve", STT(y1[:, m, :], xof[:, m, 2:XW], ALPHA, ps[:, ob[m], 0:QW], ALU.mult, ALU.add),
                  reads=["xof", ("ps", ob[m])], writes=[("y1", m)])

        for m in range(0, 4):
            op_half(m, [4, 5, 6, 7], True, False)
        rms_stats()
        rms_chain()
        for m in range(4, 8):
            op_half(m, [4, 5, 6, 7], True, False)
        for m in range(0, 4):
            op_half(m, [0, 1, 2, 3], False, True)
            op_evac(m)
        for m in range(4, 8):
            op_half(m, [0, 1, 2, 3], False, True)
            op_evac(m)

        if j + 1 < nj:
            S.add("sp", DMA(xof[:], xo3(j + 1)), writes=["xof"], dsem=ds_xof)
        elif limit >= 4:
            for jj in range(nj - 1):
                issue_reload(jj)

        def ln1_out(m, t, tkey, j=j):
            if True:
                S.add("act", ACT(h1bs[:, m, :], t, AF.Identity, scale=pv[:, C_LN1G + m:C_LN1G + m + 1], bias=pv[:, C_LN1B + m:C_LN1B + m + 1]),
                      reads=[tkey, "pv"], writes=[("h1bs", m)])
            else:
                S.add("pool", TS(h1bs[:, m, :], t, pv[:, C_LN1G + m:C_LN1G + m + 1], ALU.mult, pv[:, C_LN1B + m:C_LN1B + m + 1], ALU.add),
                      reads=[tkey, "pv"], writes=[("h1bs", m)])
            S.add("act", ACT(y1[:, m, 2:QW], t[:, 2:QW], AF.Identity, scale=dv[:, V_AG1 + m:V_AG1 + m + 1], bias=dv[:, V_AB1 + m:V_AB1 + m + 1]),
                  reads=[tkey, "ag1", "ab1"], writes=[("y1", m)])

        layer_norm(lambda m: y1[:, m, :], lambda m: ("y1", m), QW, C_LN1G, C_LN1B, ln1_out)
        def ln1_tail(j=j):
            S.add("dve", TS(h1bs[:, :, 0:2], h1bs[:, :, 0:2], pv[:, C_HFLAG + j:C_HFLAG + j + 1], ALU.mult),
                  reads=[("h1bs", m) for m in range(KC)] + ["pv"], writes=[("h1bs", m) for m in range(KC)])
            S.add("sp", DMA(scr_b[j].rearrange("p (k w) -> p k w", k=KC), h1bs[:]), reads=[("h1bs", m) for m in range(KC)],
                  writes=[("scr_b", j)], dsem=ds_sb)
            S.add("sp", DMA(scr_y[j].rearrange("p (k w) -> p k w", k=KC), y1[:, :, 2:QW]), reads=[("y1", m) for m in range(KC)],
                  writes=[("scr_y", j)], dsem=ds_sy)
        todo.append(ln1_tail)
        flush()

    if limit < 4:
        fin = ["pv", "cst", ("win", "k"), ("win", "v"), ("win", "q"), ("win", 1), "wout", "xof", ("xob", 0), ("xob", 1), ("xtb", 0), ("xtb", 1)]
        fin += [("scr_b", j) for j in range(nj)] + [("scr_y", j) for j in range(nj)]
        if debug:
            fin += [("dbgc", j) for j in range(nj)]
        S.add("dve", lambda e: e.memset(dv[:, NV_ - 1:NV_], 0.0), reads=fin, writes=["fin"])
        with nc.Block() as block:
            S.emit(block, sg)
        print("ops", len(S.ops), "waits", S.nwaits, "sems", nsem[0])
        return nc

    sqr[:] = [(t[:], ("sqr2", i)) for i, t in enumerate(sqr2)]
    lnt[:] = [(t[:], ("lnt2", i)) for i, t in enumerate(lnt2)]
    tt_[:] = [(t[:], ("tt2", i)) for i, t in enumerate(tt2)]
    ngrp = len(GROUPS)
    issue_reload(nj - 1)

    cidx = [0]

    def stage_up(g, j, t):
        b = g % 2
        h0, nh = GROUPS[g]
        at = actT[t % 2]
        slots = []
        for i in range(nh + 1):
            if i < nh:
                hc = h0 + i
                bg, bv = bank(), bank()
                for kc in range(KC):
                    S.add("pe", MM(ps[:, bg, 0:QW], wg[b][:, kc, 128 * i:128 * i + 128], h1T[:, j, kc, :],
                                   start=(kc == 0), stop=(kc == KC - 1)),
                          reads=[("wff", b), ("h1T", j)], writes=[("ps", bg)])
                for kc in range(KC):
                    S.add("pe", MM(ps[:, bv, 0:CW], wv[b][:, kc, 128 * i:128 * i + 128], h1T[:, j, kc, 2:QW],
                                   start=(kc == 0), stop=(kc == KC - 1)),
                          reads=[("wff", b), ("h1T", j)], writes=[("ps", bv)])
                ci = cidx[0] % 3
                cidx[0] += 1
                a_ = ca[ci]
                fw_ = C_FCW + 3 * hc
                S.add("act", ACT(a_[:], ps[:, bg, 0:CW], AF.Identity, scale=pv[:, fw_:fw_ + 1], bias=pv[:, C_FCB + hc:C_FCB + hc + 1]),
                      reads=[("ps", bg), "pv"], writes=[("ca", ci)])
                S.add("dve", STT(a_[:], ps[:, bg, 1:CW + 1], pv[:, fw_ + 1:fw_ + 2], a_[:], ALU.mult, ALU.add),
                      reads=[("ps", bg), "pv", ("ca", ci)], writes=[("ca", ci)])
                S.add("dve", STT(a_[:], ps[:, bg, 2:CW + 2], pv[:, fw_ + 2:fw_ + 3], a_[:], ALU.mult, ALU.add),
                      reads=[("ps", bg), "pv", ("ca", ci)], writes=[("ca", ci)])
                slots.append((ci, bv))
            if i >= 1:
                ci, bv = slots[i - 1]
                S.add("act", ACT(cs[ci][:], ca[ci][:], AF.Silu), reads=[("ca", ci)], writes=[("cs", ci)])
                S.add("dve", TT(at[:, i - 1, :], ps[:, bv, 0:CW], cs[ci][:], ALU.mult), reads=[("ps", bv), ("cs", ci)], writes=[("actT", t % 2, i - 1)])

    def stage_down(g, j, t):
        b = g % 2
        h0, nh = GROUPS[g]
        at = actT[t % 2]
        for mp in range(4):
            bd = bank()
            for half in range(2):
                m = 2 * mp + half
                for i in range(nh):
                    S.add("pe", MM(ps[:, bd, 256 * half:256 * half + 256], wd[b][:, i, 128 * m:128 * m + 128], at[:, i, :],
                                   start=(i == 0), stop=(i == nh - 1)),
                          reads=[("wff", b), ("actT", t % 2, i)], writes=[("ps", bd)])
            if 'pooladd' not in KX:
                S.add("dve", TT(y2[:, j, 2 * mp:2 * mp + 2, :], y2[:, j, 2 * mp:2 * mp + 2, :],
                                ps[:, bd, :].rearrange("p (a w) -> p a w", a=2), ALU.add),
                      reads=[("ps", bd), ("y2", j, 2 * mp), ("y2", j, 2 * mp + 1)], writes=[("y2", j, 2 * mp), ("y2", j, 2 * mp + 1)])
            else:
                di = dcnt[0] % 3
                dcnt[0] += 1
                S.add("act", ACT(dsb[di][:], ps[:, bd, :], AF.Identity), reads=[("ps", bd)], writes=[("dsb", di)])
                S.add("pool", TT(y2[:, j, 2 * mp:2 * mp + 2, :], y2[:, j, 2 * mp:2 * mp + 2, :],
                                 dsb[di][:].rearrange("p (a w) -> p a w", a=2), ALU.add),
                      reads=[("dsb", di), ("y2", j, 2 * mp), ("y2", j, 2 * mp + 1)], writes=[("y2", j, 2 * mp), ("y2", j, 2 * mp + 1)])

    def stage_ln2(j):
        def ln2_out(m, t, tkey, j=j):
            S.add("act", ACT(y2[:, j, m, :], t, AF.Identity, scale=pv[:, C_LN2G + m:C_LN2G + m + 1], bias=pv[:, C_LN2B + m:C_LN2B + m + 1]),
                  reads=[tkey, "pv"], writes=[("y2", j, m)])
        flush()
        layer_norm(lambda m, j=j: y2[:, j, m, :], lambda m, j=j: ("y2", j, m), CW, C_LN2G, C_LN2B, ln2_out)
        todo.append(lambda j=j: S.add("sp", DMA(outd[j].rearrange("(m p) w -> p m w", p=128), y2[:, j, :, :]),
                                      reads=[("y2", j, m) for m in range(KC)], writes=[("out", j)], dsem=ds_out))

    seq = [(g, j) for g in range(ngrp) for j in range(nj)]
    pend_ln = []
    for t, (g, j) in enumerate(seq):
        stage_up(g, j, t)
        flush()
        if t >= 1:
            gp, jp = seq[t - 1]
            stage_down(gp, jp, t - 1)
            if jp == nj - 1 and gp + 2 < ngrp:
                load_group(gp + 2)
            if gp == ngrp - 1:
                pend_ln.append(jp)
        if len(pend_ln) >= 2:
            stage_ln2(pend_ln.pop(0))
    gp, jp = seq[-1]
    stage_down(gp, jp, len(seq) - 1)
    pend_ln.append(jp)
    for jj in pend_ln:
        stage_ln2(jj)
    flush()
    fin = [("out", j) for j in range(nj)]
    if debug:
        fin += [("dbgc", j) for j in range(nj)]
    S.add("dve", lambda e: e.memset(dv[:, NV_ - 1:NV_], 0.0), reads=fin, writes=["fin"])

    with nc.Block() as block:
        S.emit(block, sg)
    print("ops", len(S.ops), "waits", S.nwaits, "sems", nsem[0])
    return nc


def _host_layout(x, lambda_q1, lambda_k1, lambda_q2, lambda_k2, attn_norm_g, conv_w, ln1_g, ln1_b,
                 ffn_conv_w, ffn_conv_b, ln2_g, ln2_b):
    f = np.float32
    pv_base = np.zeros((128, NP_), f)

    def pcols(v):
        return np.ascontiguousarray(np.asarray(v, f).reshape(-1, 128).T)

    pv_base[:, C_LN1G:C_LN1G + 8] = pcols(ln1_g[0])
    pv_base[:, C_LN1B:C_LN1B + 8] = pcols(ln1_b[0])
    pv_base[:, C_LN2G:C_LN2G + 8] = pcols(ln2_g[0])
    pv_base[:, C_LN2B:C_LN2B + 8] = pcols(ln2_b[0])
    pv_base[:, C_ANG] = np.asarray(attn_norm_g[0], f)
    cw = np.asarray(conv_w[0], f)
    for fc in range(4):
        for tap in range(3):
            pv_base[:, C_CW + 3 * fc + tap] = cw[tap, 128 * fc:128 * fc + 128]
    fw = np.asarray(ffn_conv_w[0], f)
    for hc in range(NHC):
        for tap in range(3):
            pv_base[:, C_FCW + 3 * hc + tap] = fw[tap, 128 * hc:128 * hc + 128]
    pv_base[:, C_FCB:C_FCB + NHC] = pcols(ffn_conv_b[0])
    lam = np.concatenate([np.asarray(v[0], f) for v in (lambda_q1, lambda_k1, lambda_q2, lambda_k2)])
    pv_base[:, C_LAM:C_LAM + 256] = lam[None, :]

    in_maps_part = []
    for core in range(8):
        bi, c = core // 2, core % 2
        xT = np.zeros((D, SEQ + 4), f)
        xT[:, 4:] = np.asarray(x[bi], f).T
        xo = np.zeros((NJ, 128, KC * XW), f)
        xt = np.zeros((NJ, 128, KC * CW), f)
        for j in range(NJ):
            t0 = 256 * (2 * j + c)
            t1 = 256 * (2 * j + 1 - c)
            w = xT[:, t0:t0 + XW].reshape(KC, 128, XW).transpose(1, 0, 2)
            xo[j] = w.reshape(128, KC * XW)
            w = xT[:, 4 + t1:4 + t1 + CW].reshape(KC, 128, CW).transpose(1, 0, 2)
            xt[j] = w.reshape(128, KC * CW)
        pvc = pv_base.copy()
        pvc[:, C_HFLAG:C_HFLAG + NJ] = 1.0
        if c == 0:
            pvc[:, C_HFLAG] = 0.0
        cst = np.zeros((128, 128 + 9 * QW), f)
        cst[:, 0:128] = np.eye(128, dtype=f)
        p = np.arange(128)[:, None]
        q = np.arange(QW)[None, :]
        for ms in range(2):
            for r in range(4):
                if r < 2:
                    key_rel = 256 * c + 128 * r + p
                else:
                    key_rel = 256 * (1 - c) + 128 * (r - 2) + p
                qry_rel = 256 * c - 2 + q
                allowed = key_rel <= qry_rel
                if ms == 0 and c == 0:
                    allowed = allowed | (q < 2)
                o = 128 + (ms * 4 + r) * QW
                cst[:, o:o + QW] = np.where(allowed, 0.0, NEG).astype(f)
        in_maps_part.append({"xo": xo, "xt": xt, "pv": pvc, "cst": cst})
    return in_maps_part


_CACHE = {}


def kernel(x, w_in, lambda_q1, lambda_k1, lambda_q2, lambda_k2, attn_norm_g, conv_w, w_out,
           ln1_g, ln1_b, ffn_w_up, ffn_conv_w, ffn_conv_b, ffn_w_down, ln2_g, ln2_b, _debug=False):
    x = np.asarray(x)
    parts = _host_layout(x, lambda_q1, lambda_k1, lambda_q2, lambda_k2, attn_norm_g, conv_w, ln1_g, ln1_b,
                         ffn_conv_w, ffn_conv_b, ln2_g, ln2_b)
    shared = {
        "w_in": np.ascontiguousarray(np.asarray(w_in, np.float32)[0]),
        "w_out": np.ascontiguousarray(np.asarray(w_out, np.float32)[0]),
        "w_up": np.ascontiguousarray(np.asarray(ffn_w_up, np.float32)[0]),
        "w_down": np.ascontiguousarray(np.asarray(ffn_w_down, np.float32)[0]),
    }
    in_maps = [dict(p, **shared) for p in parts]
    key = ("nc", bool(_debug))
    if key not in _CACHE:
        _CACHE[key] = build_program(debug=_debug)
    nc = _CACHE[key]
    res = run_bass_kernel_spmd(nc, in_maps, core_ids=list(range(8)))
    out = np.zeros((NB, SEQ, D), np.float32)
    for core in range(8):
        bi, c = core // 2, core % 2
        for j in range(NJ):
            t0 = 256 * (2 * j + c)
            out[bi, t0:t0 + CW, :] = res.results[core]["out%d" % j].T
    if _debug:
        return out, res
    return out
```

```python
import math
import numpy as np
import concourse.bass as bass
import concourse.mybir as mybir
from concourse.bass_utils import run_bass_kernel_spmd

F32 = mybir.dt.float32
BF16 = mybir.dt.bfloat16
AF = mybir.ActivationFunctionType
ALU = mybir.AluOpType
AX = mybir.AxisListType

D = 1024
SEQ = 4096
NB = 4
KC = 8
NJ = 8
CW = 256
QW = 258
XW = 260
DFF = 2816
NHC = 22
GRP = 4
GROUPS = [(0, 2), (2, 4), (6, 4), (10, 4), (14, 4), (18, 4)]
ALPHA = (2.0 * 1) ** 0.25
LAM_INIT = 0.8 - 0.6 * math.exp(0.0)
LN_EPS = 1e-5
RMS_EPS = 1e-5
NEG = -30000.0

C_LN1G, C_LN1B, C_LN2G, C_LN2B = 0, 8, 16, 24
C_ANG = 32
C_CW = 33
C_FCW = 45
C_FCB = 111
C_LAM = 133
C_HFLAG = 389
NP_ = 397
V_AG1, V_AB1, V_GSC, V_NEGLAM, V_T0, V_T1, V_E0, V_E1 = 0, 8, 16, 17, 18, 19, 20, 21
NV_ = 24

SEM_ROLL = 1000


class DSem:
    def __init__(self, sem):
        self.sem = sem
        self.count = 0


class Op:
    __slots__ = ("eng", "fn", "deps", "dsem", "ticket", "signal", "clock", "idx", "ndma")


class Sched:
    ENGS = ("pe", "act", "dve", "pool", "sp")

    def __init__(self, nc, same_engine_sync=True):
        self.nc = nc
        self.ops = []
        self.lastw = {}
        self.rd_eng = {}
        self.rd_dma = {}
        self.same_engine_sync = same_engine_sync

    def add(self, eng, fn, reads=(), writes=(), dsem=None, ndma=1):
        op = Op()
        op.eng, op.fn, op.dsem, op.ndma = eng, fn, dsem, ndma
        op.idx = len(self.ops)
        op.ticket = None
        op.signal = dsem is not None
        op.clock = None
        deps = set()
        for r in reads:
            w = self.lastw.get(r)
            if w is not None:
                deps.add(w)
        for w_ in writes:
            w = self.lastw.get(w_)
            if w is not None:
                deps.add(w)
            for i in self.rd_eng.get(w_, {}).values():
                deps.add(i)
            for i in self.rd_dma.get(w_, ()):
                deps.add(i)
        for w_ in writes:
            self.lastw[w_] = op.idx
            self.rd_eng[w_] = {}
            self.rd_dma[w_] = []
        for r in reads:
            if r in writes:
                continue
            if dsem is not None:
                self.rd_dma.setdefault(r, []).append(op.idx)
            else:
                self.rd_eng.setdefault(r, {})[eng] = op.idx
        keep = set()
        best = {}
        bestd = {}
        for d in deps:
            p = self.ops[d]
            if p.dsem is not None:
                k_ = id(p.dsem)
                if k_ not in bestd or bestd[k_] < d:
                    bestd[k_] = d
                continue
            if p.eng == eng and dsem is None and (eng == "pe" or not self.same_engine_sync):
                continue
            if p.eng not in best or best[p.eng] < d:
                best[p.eng] = d
        keep.update(best.values())
        keep.update(bestd.values())
        op.deps = keep
        for d in keep:
            self.ops[d].signal = True
        self.ops.append(op)
        return op

    def emit(self, block, sems):
        cnt = {e: 0 for e in self.ENGS}
        semlist = {e: [] for e in self.ENGS}
        for op in self.ops:
            if op.dsem is not None:
                op.dsem.count += 16 * op.ndma
                op.ticket = ("d", op.dsem.sem, op.dsem.count)
            elif op.signal:
                c = cnt[op.eng]
                si, v = divmod(c, SEM_ROLL)
                if si >= len(semlist[op.eng]):
                    semlist[op.eng].append(next(sems))
                op.ticket = ("e", semlist[op.eng][si], v + 1)
                cnt[op.eng] = c + 1
        streams = {e: [] for e in self.ENGS}
        clock = {e: {} for e in self.ENGS}
        nwaits = 0
        for op in self.ops:
            ck = clock[op.eng]
            st = streams[op.eng]
            for d in sorted(op.deps):
                p = self.ops[d]
                t = p.ticket
                key = id(t[1])
                if ck.get(key, 0) >= t[2]:
                    continue
                st.append(("w", t[1], t[2]))
                nwaits += 1
                ck[key] = t[2]
                if p.clock is not None:
                    for k, v in p.clock.items():
                        if ck.get(k, 0) < v:
                            ck[k] = v
            st.append(("o", op))
            if op.signal:
                op.clock = dict(ck)
        self.nwaits = nwaits
        print("signal counts", cnt, "dma max", max([op.ticket[2] for op in self.ops if op.dsem is not None] + [0]))
        self.streams = streams

        def run(e):
            def body(eng):
                for item in streams[e]:
                    if item[0] == "w":
                        eng.wait_ge(item[1], item[2])
                    else:
                        op = item[1]
                        ins = op.fn(eng)
                        if op.dsem is not None:
                            if not isinstance(ins, (list, tuple)):
                                ins = [ins]
                            assert len(ins) == op.ndma
                            for i_ in ins:
                                i_.then_inc(op.dsem.sem, 16)
                        elif op.signal:
                            ins.then_inc(op.ticket[1], 1)
            return body

        block.tensor(run("pe"))
        block.scalar(run("act"))
        block.vector(run("dve"))
        block.gpsimd(run("pool"))
        block.sync(run("sp"))


def MM(out, lhsT, rhs, start=True, stop=True):
    return lambda e: e.matmul(out, lhsT=lhsT, rhs=rhs, start=start, stop=stop)


def ACT(out, in_, func, scale=None, bias=None):
    kw = {}
    if scale is not None:
        kw["scale"] = scale
    if bias is not None:
        kw["bias"] = bias
    return lambda e: e.activation(out=out, in_=in_, func=func, **kw)


def TT(out, a, b, op):
    return lambda e: e.tensor_tensor(out=out, in0=a, in1=b, op=op)


def TS(out, a, s1, op0, s2=None, op1=None):
    if op1 is None:
        return lambda e: e.tensor_scalar(out=out, in0=a, scalar1=s1, scalar2=None, op0=op0)
    return lambda e: e.tensor_scalar(out=out, in0=a, scalar1=s1, scalar2=s2, op0=op0, op1=op1)


def STT(out, a, s, b, op0, op1):
    return lambda e: e.scalar_tensor_tensor(out=out, in0=a, scalar=s, in1=b, op0=op0, op1=op1)


def CP(out, in_):
    return lambda e: e.tensor_copy(out=out, in_=in_)


def RECIP(out, in_):
    return lambda e: e.reciprocal(out=out, in_=in_)


def DMA(out, in_):
    return lambda e: e.dma_start(out=out, in_=in_)


def DMAS(pairs):
    return lambda e: [e.dma_start(out=o, in_=i) for (o, i) in pairs]


class Arena:
    def __init__(self, nc, base, top):
        self.nc, self.cur, self.top = nc, base, top
        self.n = 0

    def alloc(self, name, shape, dtype):
        esz = 4 if dtype == F32 else 2
        size = esz
        for s in shape[1:]:
            size *= s
        off = (self.cur + 31) // 32 * 32
        self.cur = off + size
        self.last_off = off
        assert self.cur <= self.top, (name, self.cur, self.top)
        self.n += 1
        return self.nc.alloc_sbuf_tensor_at("%s_%d" % (name, self.n), list(shape), dtype, offset=off)


def build_program(nj=NJ, debug=False, limit=99):
    import os
    KX = set(os.environ.get('KX', '').split(','))
    nc = bass.Bass("TRN2", target_bir_lowering=False)

    def dram(name, shape, dtype, kind):
        return nc.dram_tensor(name, list(shape), dtype, kind=kind).ap()

    xo = dram("xo", [NJ, 128, KC * XW], F32, "ExternalInput")
    xt = dram("xt", [NJ, 128, KC * CW], F32, "ExternalInput")
    pvd = dram("pv", [128, NP_], F32, "ExternalInput")
    cstd = dram("cst", [128, 128 + 9 * QW], F32, "ExternalInput")
    w_in = dram("w_in", [D, 3072], F32, "ExternalInput")
    w_out = dram("w_out", [D, D], F32, "ExternalInput")
    w_up = dram("w_up", [D, 2 * DFF], F32, "ExternalInput")
    w_down = dram("w_down", [DFF, D], F32, "ExternalInput")
    outd = [dram("out%d" % j, [D, CW], F32, "ExternalOutput") for j in range(NJ)]
    scr_b = [dram("scr_b%d" % j, [128, KC * QW], BF16, "ExternalOutput") for j in range(NJ)]
    scr_y = [dram("scr_y%d" % j, [128, KC * CW], F32, "ExternalOutput") for j in range(NJ)]
    if debug:
        dbg_cat = dram("dbg_cat", [NJ, 128, KC * QW], F32, "ExternalOutput")
        dbg_h1 = dram("dbg_h1", [NJ, 128, KC * CW], F32, "ExternalOutput")

    nsem = [0]

    def semgen():
        while True:
            nsem[0] += 1
            yield nc.alloc_semaphore("sm%d" % nsem[0])

    sg = semgen()

    def dsem():
        return DSem(next(sg))

    S = Sched(nc)
    base0 = (nc.sbuf_base + 63) // 64 * 64
    top0 = nc.sbuf_top - 64
    arena_t = nc.alloc_sbuf_tensor("arena", [128, top0 - base0], mybir.dt.uint8)
    A = Arena(nc, base0, top0)
    ps = nc.alloc_psum_tensor("ps", [128, 8, 512], F32)

    pv = A.alloc("pv", [128, NP_], F32)
    dv = A.alloc("dv", [128, NV_], F32)
    lamt = A.alloc("lamt", [128, 128], F32)
    ones_f = A.alloc("ones_f", [128, 128], F32)
    ones_b = A.alloc("ones_b", [128, 128], BF16)
    cstb = A.alloc("cstb", [128, 128 + 9 * QW], BF16)
    ident_b = cstb[:, 0:128]

    def maskb(ms, r):
        o = 128 + (ms * 4 + r) * QW
        return cstb[:, o:o + QW]

    mark_persist = A.cur

    winb = A.alloc("winb", [128, KC, 3072], BF16)
    winb_off = A.last_off
    woutb = A.alloc("woutb", [128, KC, D], BF16)
    KT = A.alloc("KT", [128, 4, SEQ], BF16)
    Vt = A.alloc("Vt", [128, 32, 512], BF16)
    xof = A.alloc("xof", [128, KC, XW], F32)
    xob = [A.alloc("xob", [128, KC, XW], BF16) for _ in range(2)]
    xtb = [A.alloc("xtb", [128, KC, CW], BF16) for _ in range(2)]
    qTz = A.alloc("qTz", [128, 4, 2, QW], BF16)
    catT = A.alloc("catT", [128, KC, QW], BF16)
    hc_sb = [A.alloc("hc_sb", [128, XW], F32) for _ in range(2)]
    u_sb = [A.alloc("u_sb", [128, XW], F32) for _ in range(2)]
    acc_sb = [A.alloc("acc_sb", [128, QW], F32) for _ in range(2)]
    pbuf = [A.alloc("pbuf", [128, 2, QW], BF16) for _ in range(4)]
    tmpW = [A.alloc("tmpW", [128, 2, QW], F32) for _ in range(6)]
    tmpW_off = A.last_off - 5 * 2 * QW * 4 - 5 * 32

    def TH(k, h):
        return (tmpW[k][:, h, :], ("tmp", k, h))
    y1 = A.alloc("y1", [128, KC, QW], F32)
    lnt = [TH(0, 0), TH(0, 1), TH(1, 0), TH(1, 1), TH(2, 0)]
    tt_ = [TH(3, 0), TH(3, 1)]
    sqr = [TH(4, 0), TH(4, 1)]
    h1bs = A.alloc("h1bs", [128, KC, QW], BF16)
    print("phase1 sbuf end", A.cur, "of", top0)
    p1_end = A.cur
    A.cur = winb_off
    wg, wv, wd = [], [], []
    for _ in range(2):
        wg.append(A.alloc("wg", [128, KC, 512], BF16))
        wv.append(A.alloc("wv", [128, KC, 512], BF16))
        wd.append(A.alloc("wd", [128, GRP, D], BF16))
    wff_end = A.cur
    assert wff_end <= winb_off + KC * 3072 * 2
    A.cur = wff_end
    h1T = A.alloc("h1T", [128, NJ, KC, QW], BF16)
    y2 = A.alloc("y2", [128, NJ, KC, CW], F32)
    actT = [A.alloc("actT", [128, GRP, CW], BF16) for _ in range(2)]
    ca = [A.alloc("ca", [128, CW], F32) for _ in range(3)]
    cs = [A.alloc("cs", [128, CW], F32) for _ in range(3)]
    dcnt = [0]
    sqr2 = [[A.alloc("sqr2", [128, CW], F32) for _ in range(2)]] * 2
    lnt2 = [[A.alloc("lnt2", [128, CW], F32) for _ in range(5)] for _k in range(2)]
    tt2 = [[A.alloc("ttmp2", [128, CW], F32) for _ in range(2)] for _k in range(2)]
    print("phase2 sbuf end", A.cur, "of", top0)
    p2_end = A.cur
    ds_h1T = [dsem() for _ in range(nj)]
    ds_y2 = [dsem() for _ in range(nj)]
    ds_out = dsem()
    p1_done = [("scr_b", nj - 1), ("scr_y", nj - 1)]
    assert A.cur <= tmpW_off, (A.cur, tmpW_off)
    dead_keys = ["wout", "xof"] + [("cat", k) for k in range(KC)] + [("qT", h) for h in range(4)] + [("xob", 0), ("xob", 1), ("xtb", 0), ("xtb", 1)]
    A.cur = p1_end

    ds_pv, ds_cst = dsem(), dsem()
    ds_win = {"k": dsem(), "v": dsem(), "q": dsem(), 1: dsem()}
    ds_wout = dsem()
    ds_xof = dsem()
    ds_xob = [dsem(), dsem()]
    ds_xtb = [dsem(), dsem()]
    ds_sb, ds_sy = dsem(), dsem()
    ds_dbg = dsem()

    ds_wff = [dsem(), dsem()]

    def load_group(g):
        b = g % 2
        h0, nh = GROUPS[g]
        pairs = []
        for kc in range(KC):
            pairs.append((wg[b][:, kc, 0:128 * nh], w_up[128 * kc:128 * kc + 128, 128 * h0:128 * (h0 + nh)]))
            pairs.append((wv[b][:, kc, 0:128 * nh], w_up[128 * kc:128 * kc + 128, DFF + 128 * h0:DFF + 128 * (h0 + nh)]))
        for i in range(nh):
            pairs.append((wd[b][:, i, :], w_down[128 * (h0 + i):128 * (h0 + i) + 128, :]))
        S.add("pool", DMAS(pairs), writes=[("wff", b), ("win", "k"), ("win", "v"), ("win", "q"), ("win", 1)],
              dsem=ds_wff[b], ndma=len(pairs))

    def issue_reload(j):
        S.add("sp", DMA(h1T[:, j, :, :], scr_b[j].rearrange("p (k w) -> p k w", k=KC)),
              reads=[("scr_b", j)], writes=[("h1T", j)] + dead_keys, dsem=ds_h1T[j])
        S.add("sp", DMA(y2[:, j, :, :], scr_y[j].rearrange("p (k w) -> p k w", k=KC)),
              reads=[("scr_y", j)], writes=[("y2", j, m) for m in range(KC)] + dead_keys, dsem=ds_y2[j])


    rb = [0]

    def bank():
        b = rb[0]
        rb[0] = (b + 1) % 8
        return b

    S.add("sp", DMA(pv[:], pvd[:, :]), writes=["pv"], dsem=ds_pv)
    S.add("pool", DMAS([(cstb[:, 0:1024], cstd[:, 0:1024]), (cstb[:, 1024:128 + 9 * QW], cstd[:, 1024:128 + 9 * QW])]),
          writes=["cst"], dsem=ds_cst, ndma=2)
    S.add("dve", lambda e: e.memset(ones_f[:], 1.0), writes=["ones_f"])
    S.add("dve", lambda e: e.memset(ones_b[:], 1.0), writes=["ones_b"])
    S.add("dve", lambda e: e.memset(qTz[:].rearrange("p a b w -> p (a b w)"), 0.0), writes=[("qT", h) for h in range(4)])

    def xo3(j):
        return xo[j].rearrange("p (k w) -> p k w", k=KC)

    def xt3(j):
        return xt[j].rearrange("p (k w) -> p k w", k=KC)

    def issue_loads(j):
        b = j % 2
        S.add("pool", DMA(xob[b][:], xo3(j)), writes=[("xob", b)], dsem=ds_xob[b])
        S.add("pool", DMA(xtb[b][:], xt3(j)), writes=[("xtb", b)], dsem=ds_xtb[b])

    issue_loads(0)
    for grp, c0, cn in (("k", 512, 512), ("v", 1024, 512), ("q", 0, 512), (1, 1536, 1536)):
        S.add("pool", DMAS([(winb[:, kc, c0:c0 + cn], w_in[128 * kc:128 * kc + 128, c0:c0 + cn]) for kc in range(KC)]),
              writes=[("win", grp)], dsem=ds_win[grp], ndma=KC)
    S.add("sp", DMA(xof[:], xo3(0)), writes=["xof"], dsem=ds_xof)
    S.add("pool", DMAS([(woutb[:, kc, :], w_out[128 * kc:128 * kc + 128, :]) for kc in range(KC)]),
          writes=["wout"], dsem=ds_wout, ndma=KC)

    S.add("dve", TT(lamt[:, 0:64], pv[:, C_LAM:C_LAM + 64], pv[:, C_LAM + 64:C_LAM + 128], ALU.mult), reads=["pv"], writes=["lamt0"])
    S.add("dve", TT(lamt[:, 64:128], pv[:, C_LAM + 128:C_LAM + 192], pv[:, C_LAM + 192:C_LAM + 256], ALU.mult), reads=["pv"], writes=["lamt1"])
    S.add("dve", lambda e: e.reduce_sum(out=dv[:, V_T0:V_T0 + 1], in_=lamt[:, 0:64], axis=AX.X), reads=["lamt0"], writes=["dvt0"])
    S.add("dve", lambda e: e.reduce_sum(out=dv[:, V_T1:V_T1 + 1], in_=lamt[:, 64:128], axis=AX.X), reads=["lamt1"], writes=["dvt1"])
    S.add("act", ACT(dv[:, V_E0:V_E0 + 2], dv[:, V_T0:V_T0 + 2], AF.Exp), reads=["dvt0", "dvt1"], writes=["dve01"])
    S.add("dve", TT(dv[:, V_NEGLAM:V_NEGLAM + 1], dv[:, V_E1:V_E1 + 1], dv[:, V_E0:V_E0 + 1], ALU.subtract), reads=["dve01"], writes=["neglam0"])
    S.add("dve", TS(dv[:, V_NEGLAM:V_NEGLAM + 1], dv[:, V_NEGLAM:V_NEGLAM + 1], -LAM_INIT, ALU.add), reads=["neglam0"], writes=["neglam"])
    S.add("dve", TS(dv[:, V_GSC:V_GSC + 1], pv[:, C_ANG:C_ANG + 1], 1.0 - LAM_INIT, ALU.mult), reads=["pv"], writes=["gsc"])
    S.add("dve", TS(dv[:, V_AG1:V_AG1 + 8], pv[:, C_LN1G:C_LN1G + 8], ALPHA, ALU.mult), reads=["pv"], writes=["ag1"])
    S.add("dve", TS(dv[:, V_AB1:V_AB1 + 8], pv[:, C_LN1B:C_LN1B + 8], ALPHA, ALU.mult), reads=["pv"], writes=["ab1"])
    neglam = dv[:, V_NEGLAM:V_NEGLAM + 1]
    gsc = dv[:, V_GSC:V_GSC + 1]

    evac_flip = [0]

    def evac(out_ap, in_ap, reads, writes):
        evac_flip[0] ^= 1
        if evac_flip[0]:
            S.add("dve", CP(out_ap, in_ap), reads=reads, writes=writes)
        else:
            S.add("act", ACT(out_ap, in_ap, AF.Identity), reads=reads, writes=writes)

    todo = []

    def drain(n):
        for _ in range(min(n, len(todo))):
            todo.pop(0)()

    def flush():
        drain(len(todo))

    def layer_norm(src, src_keys, width, gcol, bcol, emit_out):
        b1, b2 = bank(), bank()
        s1 = ps[:, b1, 0:width]
        s2 = ps[:, b2, 0:width]
        for m in range(KC):
            sq, sqk = sqr[m % 2]
            S.add("act", ACT(sq[:, 0:width], src(m), AF.Square), reads=[src_keys(m)], writes=[sqk])
            S.add("pe", MM(s1, ones_f[:], src(m), start=(m == 0), stop=(m == KC - 1)), reads=["ones_f", src_keys(m)], writes=[("ps", b1)])
            S.add("pe", MM(s2, ones_f[:], sq[:, 0:width], start=(m == 0), stop=(m == KC - 1)), reads=["ones_f", sqk], writes=[("ps", b2)])
        (mean, kmean), (msq, kmsq), (var, kvar), (rstd, krstd), (nmr, knmr) = [(t[:, 0:width], k) for (t, k) in lnt]
        S.add("dve", TS(mean, s1, 1.0 / D, ALU.mult), reads=[("ps", b1)], writes=[kmean])
        S.add("dve", TT(msq, mean, mean, ALU.mult), reads=[kmean], writes=[kmsq])
        S.add("dve", STT(var, s2, 1.0 / D, msq, ALU.mult, ALU.subtract), reads=[("ps", b2), kmsq], writes=[kvar])
        S.add("dve", TS(var, var, LN_EPS, ALU.add), reads=[kvar], writes=[kvar])
        S.add("act", ACT(var, var, AF.Ln), reads=[kvar], writes=[kvar])
        S.add("act", ACT(rstd, var, AF.Exp, scale=-0.5), reads=[kvar], writes=[krstd])
        S.add("dve", STT(nmr, mean, -1.0, rstd, ALU.mult, ALU.mult), reads=[kmean, krstd], writes=[knmr])
        def norm_step(m, t, tk):
            def go():
                S.add("dve", TT(t, src(m), rstd, ALU.mult), reads=[src_keys(m), krstd], writes=[tk])
                S.add("dve", TT(t, t, nmr, ALU.add), reads=[tk, knmr], writes=[tk])
                emit_out(m, t, tk)
            return go

        for m in range(KC):
            t, tk = tt_[m % 2]
            todo.append(norm_step(m, t[:, 0:width], tk))

    for j in range(nj if limit >= 1 else 0):
        b = j % 2
        xb, xtb_ = xob[b], xtb[b]
        if j + 1 < nj:
            issue_loads(j + 1)
        for h in range(4):
            bk = bank()
            col = 512 + 128 * h
            for half, (src, skey) in enumerate([(lambda kc: xb[:, kc, 4:XW], ("xob", b)), (lambda kc: xtb_[:, kc, :], ("xtb", b))]):
                for kc in range(KC):
                    S.add("pe", MM(ps[:, bk, 256 * half:256 * half + 256], winb[:, kc, col:col + 128], src(kc),
                                   start=(kc == 0), stop=(kc == KC - 1)),
                          reads=[("win", "k"), skey], writes=[("ps", bk)])
            evac(KT[:, h, 512 * j:512 * j + 512], ps[:, bk, :], [("ps", bk)], [("KT", h, j)])
            drain(1)
        if limit < 1.2:
            continue
        for blk in range(4):
            bk = bank()
            for kc in range(KC):
                if blk < 2:
                    lt = xb[:, kc, 4 + 128 * blk:4 + 128 * blk + 128]
                    skey = ("xob", b)
                else:
                    lt = xtb_[:, kc, 128 * (blk - 2):128 * (blk - 2) + 128]
                    skey = ("xtb", b)
                S.add("pe", MM(ps[:, bk, :], lt, winb[:, kc, 1024:1536], start=(kc == 0), stop=(kc == KC - 1)),
                      reads=[("win", "v"), skey], writes=[("ps", bk)])
            evac(Vt[:, 4 * j + blk, :], ps[:, bk, :], [("ps", bk)], [("V", 4 * j + blk)])
            drain(1)
        if limit < 1.4:
            continue
        for h in range(4):
            bk = bank()
            for kc in range(KC):
                S.add("pe", MM(ps[:, bk, 0:QW], winb[:, kc, 128 * h:128 * h + 128], xb[:, kc, 2:XW],
                               start=(kc == 0), stop=(kc == KC - 1)),
                      reads=[("win", "q"), ("xob", b)], writes=[("ps", bk)])
            S.add("dve", CP(qTz[0:64, h, 0, :], ps[0:64, bk, 0:QW]), reads=[("ps", bk)], writes=[("qT", h)])
            S.add("act", ACT(qTz[64:128, h, 1, :], ps[64:128, bk, 0:QW], AF.Identity), reads=[("ps", bk)], writes=[("qT", h)])
            drain(1)
        if limit < 1.6:
            continue
        def conv_branch(xb=xb, b=b):
            for fc in range(4):
                bB, bC, bH = bank(), bank(), bank()
                for (bk, c0) in [(bH, 2560), (bC, 2048), (bB, 1536)]:
                    col = c0 + 128 * fc
                    for kc in range(KC):
                        S.add("pe", MM(ps[:, bk, 0:XW], winb[:, kc, col:col + 128], xb[:, kc, :],
                                       start=(kc == 0), stop=(kc == KC - 1)),
                              reads=[("win", 1), ("xob", b)], writes=[("ps", bk)])
                t = fc % 2
                hs, us, ac = hc_sb[t], u_sb[t], acc_sb[t]
                S.add("act", ACT(hs[:], ps[:, bH, 0:XW], AF.Identity), reads=[("ps", bH)], writes=[("hc_sb", t)])
                S.add("dve", TT(us[:], ps[:, bC, 0:XW], hs[:], ALU.mult), reads=[("ps", bC), ("hc_sb", t)], writes=[("u_sb", t)])
                cw = C_CW + 3 * fc
                S.add("dve", TS(ac[:], us[:, 2:XW], pv[:, cw + 2:cw + 3], ALU.mult), reads=[("u_sb", t), "pv"], writes=[("acc", t)])
                S.add("dve", STT(ac[:], us[:, 1:XW - 1], pv[:, cw + 1:cw + 2], ac[:], ALU.mult, ALU.add), reads=[("u_sb", t), "pv", ("acc", t)], writes=[("acc", t)])
                S.add("dve", STT(ac[:], us[:, 0:XW - 2], pv[:, cw:cw + 1], ac[:], ALU.mult, ALU.add), reads=[("u_sb", t), "pv", ("acc", t)], writes=[("acc", t)])
                S.add("dve", TT(catT[:, 4 + fc, :], ps[:, bB, 2:XW], ac[:], ALU.mult), reads=[("ps", bB), ("acc", t)], writes=[("cat", 4 + fc)])


        if j > 0:
            conv_branch()
        if j == nj - 1 and limit >= 4:
            load_group(0)
            load_group(1)
        if limit < 2:
            continue
        flush()
        nkb = 4 * j + 4
        if 'noattn' in KX:
            nkb = 0
        ms = 0 if j == 0 else 1
        pcount = [0]
        accb = [4, 5, 6, 7]
        LAG = 2
        NPB = len(pbuf)

        def epilogue_a(h):
            osb, lsb = tmpW[0], tmpW[1]
            ko = [("tmp", 0, 0), ("tmp", 0, 1)]
            kl = [("tmp", 1, 0), ("tmp", 1, 1)]
            o_, ok_ = TH(2 + h // 2, h % 2)
            sq_, sqk_ = TH(4 + h // 2, h % 2)
            S.add("dve", CP(osb[:], ps[:, 4:6, 0:QW]), reads=[("ps", 4), ("ps", 5)], writes=ko)
            S.add("act", ACT(lsb[:], ps[:, 6:8, 0:QW], AF.Identity), reads=[("ps", 6), ("ps", 7)], writes=kl)
            S.add("dve", RECIP(lsb[:], lsb[:]), reads=kl, writes=kl)
            S.add("dve", TT(osb[:], osb[:], lsb[:], ALU.mult), reads=ko + kl, writes=ko)
            S.add("dve", STT(o_, osb[:, 1, :], neglam, osb[:, 0, :], ALU.mult, ALU.add), reads=ko + ["neglam"], writes=[ok_])
            S.add("dve", TT(sq_, o_, o_, ALU.mult), reads=[ok_], writes=[sqk_])

        def do_pv(h, kb, pi):
            pb = pbuf[pi]
            first, last = (kb == 0), (kb == nkb - 1)
            for mp in range(2):
                S.add("pe", MM(ps[:, accb[mp], 0:QW], Vt[:, kb, 128 * h:128 * h + 128], pb[:, mp, :], start=first, stop=last),
                      reads=[("V", kb), ("pbuf", pi)], writes=[("ps", accb[mp])])
                S.add("pe", MM(ps[:, accb[2 + mp], 0:QW], ones_b[:], pb[:, mp, :], start=first, stop=last),
                      reads=["ones_b", ("pbuf", pi)], writes=[("ps", accb[2 + mp])])
            if last:
                epilogue_a(h)

        pendq = []
        items = [(h, kb) for h in range(4 if nkb else 0) for kb in range(nkb)]
        for idx, (h, kb) in enumerate(items):
            sp_ = (idx % 2) * 2
            masked = (kb >= 4 * j) or ('allmask' in KX)
            kj = kb // 4
            for mp in range(2):
                sdst = ps[:, sp_ + mp, 0:QW]
                if masked:
                    S.add("pe", MM(sdst, ident_b, (maskb(ms, kb - 4 * j) if kb >= 4 * j else maskb(2, 0)), start=True, stop=False),
                          reads=["cst"], writes=[("ps", sp_ + mp)])
                S.add("pe", MM(sdst, KT[:, h, 128 * kb:128 * kb + 128], qTz[:, h, mp, :],
                               start=(not masked), stop=True),
                      reads=[("KT", h, kj), ("qT", h)], writes=[("ps", sp_ + mp)])
            pi = pcount[0] % NPB
            pcount[0] += 1
            S.add("act", ACT(pbuf[pi][:], ps[:, sp_:sp_ + 2, 0:QW], AF.Exp, scale=0.125),
                  reads=[("ps", sp_), ("ps", sp_ + 1)], writes=[("pbuf", pi)])
            pendq.append((h, kb, pi))
            if len(pendq) > LAG:
                do_pv(*pendq.pop(0))
        while pendq:
            do_pv(*pendq.pop(0))

        if j == 0:
            conv_branch()
        nh_ = 4 if nkb else 0

        def rms_stats():
            for h in range(nh_):
                sq_, sqk_ = TH(4 + h // 2, h % 2)
                S.add("pe", MM(ps[:, h, 0:QW], ones_f[:], sq_, start=True, stop=True), reads=["ones_f", sqk_], writes=[("ps", h)])

        def rms_chain():
            for h in range(nh_):
                lnv, lk_ = TH(4 + h // 2, h % 2)
                S.add("dve", TS(lnv, ps[:, h, 0:QW], 1.0 / 128.0, ALU.mult, RMS_EPS, ALU.add), reads=[("ps", h)], writes=[lk_])
                S.add("act", ACT(lnv, lnv, AF.Ln), reads=[lk_], writes=[lk_])
                S.add("act", ACT(lnv, lnv, AF.Exp, scale=-0.5), reads=[lk_], writes=[lk_])
            for h in range(nh_):
                o_, ok_ = TH(2 + h // 2, h % 2)
                lnv, lk_ = TH(4 + h // 2, h % 2)
                S.add("dve", STT(catT[:, h, :], o_, gsc, lnv, ALU.mult, ALU.mult), reads=[ok_, "gsc", lk_], writes=[("cat", h)])

        if limit < 3:
            rms_stats()
            rms_chain()
            continue
        ob = [4, 5, 6, 7, 0, 1, 2, 3]

        def op_half(m, kcs, first, last):
            for ki, kc in enumerate(kcs):
                S.add("pe", MM(ps[:, ob[m], 0:QW], woutb[:, kc, 128 * m:128 * m + 128], catT[:, kc, :],
                               start=(first and ki == 0), stop=(last and ki == len(kcs) - 1)),
                      reads=["wout", ("cat", kc)], writes=[("ps", ob[m])])

        def op_evac(m):
            S.add("dve", STT(y1[:, m, :], xof[:, m, 2:XW], ALPHA, ps[:, ob[m], 0:QW], ALU.mult, ALU.add),
                  reads=["xof", ("ps", ob[m])], writes=[("y1", m)])

        for m in range(0, 4):
            op_half(m, [4, 5, 6, 7], True, False)
        rms_stats()
        rms_chain()
        for m in range(4, 8):
            op_half(m, [4, 5, 6, 7], True, False)
        for m in range(0, 4):
            op_half(m, [0, 1, 2, 3], False, True)
            op_evac(m)
        for m in range(4, 8):
            op_half(m, [0, 1, 2, 3], False, True)
            op_evac(m)

        if j + 1 < nj:
            S.add("sp", DMA(xof[:], xo3(j + 1)), writes=["xof"], dsem=ds_xof)
        elif limit >= 4:
            for jj in range(nj - 1):
                issue_reload(jj)

        def ln1_out(m, t, tkey, j=j):
            if True:
                S.add("act", ACT(h1bs[:, m, :], t, AF.Identity, scale=pv[:, C_LN1G + m:C_LN1G + m + 1], bias=pv[:, C_LN1B + m:C_LN1B + m + 1]),
                      reads=[tkey, "pv"], writes=[("h1bs", m)])
            else:
                S.add("pool", TS(h1bs[:, m, :], t, pv[:, C_LN1G + m:C_LN1G + m + 1], ALU.mult, pv[:, C_LN1B + m:C_LN1B + m + 1], ALU.add),
                      reads=[tkey, "pv"], writes=[("h1bs", m)])
            S.add("act", ACT(y1[:, m, 2:QW], t[:, 2:QW], AF.Identity, scale=dv[:, V_AG1 + m:V_AG1 + m + 1], bias=dv[:, V_AB1 + m:V_AB1 + m + 1]),
                  reads=[tkey, "ag1", "ab1"], writes=[("y1", m)])

        layer_norm(lambda m: y1[:, m, :], lambda m: ("y1", m), QW, C_LN1G, C_LN1B, ln1_out)
        def ln1_tail(j=j):
            S.add("dve", TS(h1bs[:, :, 0:2], h1bs[:, :, 0:2], pv[:, C_HFLAG + j:C_HFLAG + j + 1], ALU.mult),
                  reads=[("h1bs", m) for m in range(KC)] + ["pv"], writes=[("h1bs", m) for m in range(KC)])
            S.add("sp", DMA(scr_b[j].rearrange("p (k w) -> p k w", k=KC), h1bs[:]), reads=[("h1bs", m) for m in range(KC)],
                  writes=[("scr_b", j)], dsem=ds_sb)
            S.add("sp", DMA(scr_y[j].rearrange("p (k w) -> p k w", k=KC), y1[:, :, 2:QW]), reads=[("y1", m) for m in range(KC)],
                  writes=[("scr_y", j)], dsem=ds_sy)
        todo.append(ln1_tail)
        flush()

    if limit < 4:
        fin = ["pv", "cst", ("win", "k"), ("win", "v"), ("win", "q"), ("win", 1), "wout", "xof", ("xob", 0), ("xob", 1), ("xtb", 0), ("xtb", 1)]
        fin += [("scr_b", j) for j in range(nj)] + [("scr_y", j) for j in range(nj)]
        if debug:
            fin += [("dbgc", j) for j in range(nj)]
        S.add("dve", lambda e: e.memset(dv[:, NV_ - 1:NV_], 0.0), reads=fin, writes=["fin"])
        with nc.Block() as block:
            S.emit(block, sg)
        print("ops", len(S.ops), "waits", S.nwaits, "sems", nsem[0])
        return nc

    def use_ln2_set(k):
        sqr[:] = [(t[:], ("sqr2", i)) for i, t in enumerate(sqr2[k])]
        lnt[:] = [(t[:], ("lnt2", k, i)) for i, t in enumerate(lnt2[k])]
        tt_[:] = [(t[:], ("tt2", k, i)) for i, t in enumerate(tt2[k])]
    ln2_count = [0]
    ngrp = len(GROUPS)
    issue_reload(nj - 1)

    cidx = [0]

    def stage_up(g, j, t):
        b = g % 2
        h0, nh = GROUPS[g]
        at = actT[t % 2]
        slots = []
        for i in range(nh + 1):
            if i < nh:
                hc = h0 + i
                bg, bv = bank(), bank()
                for kc in range(KC):
                    S.add("pe", MM(ps[:, bg, 0:QW], wg[b][:, kc, 128 * i:128 * i + 128], h1T[:, j, kc, :],
                                   start=(kc == 0), stop=(kc == KC - 1)),
                          reads=[("wff", b), ("h1T", j)], writes=[("ps", bg)])
                for kc in range(KC):
                    S.add("pe", MM(ps[:, bv, 0:CW], wv[b][:, kc, 128 * i:128 * i + 128], h1T[:, j, kc, 2:QW],
                                   start=(kc == 0), stop=(kc == KC - 1)),
                          reads=[("wff", b), ("h1T", j)], writes=[("ps", bv)])
                ci = cidx[0] % 3
                cidx[0] += 1
                a_ = ca[ci]
                fw_ = C_FCW + 3 * hc
                S.add("act", ACT(a_[:], ps[:, bg, 0:CW], AF.Identity, scale=pv[:, fw_:fw_ + 1], bias=pv[:, C_FCB + hc:C_FCB + hc + 1]),
                      reads=[("ps", bg), "pv"], writes=[("ca", ci)])
                S.add("dve", STT(a_[:], ps[:, bg, 1:CW + 1], pv[:, fw_ + 1:fw_ + 2], a_[:], ALU.mult, ALU.add),
                      reads=[("ps", bg), "pv", ("ca", ci)], writes=[("ca", ci)])
                S.add("dve", STT(a_[:], ps[:, bg, 2:CW + 2], pv[:, fw_ + 2:fw_ + 3], a_[:], ALU.mult, ALU.add),
                      reads=[("ps", bg), "pv", ("ca", ci)], writes=[("ca", ci)])
                slots.append((ci, bv))
            if i >= 1:
                ci, bv = slots[i - 1]
                S.add("act", ACT(cs[ci][:], ca[ci][:], AF.Silu), reads=[("ca", ci)], writes=[("cs", ci)])
                S.add("dve", TT(at[:, i - 1, :], ps[:, bv, 0:CW], cs[ci][:], ALU.mult), reads=[("ps", bv), ("cs", ci)], writes=[("actT", t % 2, i - 1)])

    def stage_down(g, j, t):
        b = g % 2
        h0, nh = GROUPS[g]
        at = actT[t % 2]
        for mp in range(4):
            bd = bank()
            for half in range(2):
                m = 2 * mp + half
                for i in range(nh):
                    S.add("pe", MM(ps[:, bd, 256 * half:256 * half + 256], wd[b][:, i, 128 * m:128 * m + 128], at[:, i, :],
                                   start=(i == 0), stop=(i == nh - 1)),
                          reads=[("wff", b), ("actT", t % 2, i)], writes=[("ps", bd)])
            if 'pooladd' not in KX:
                S.add("dve", TT(y2[:, j, 2 * mp:2 * mp + 2, :], y2[:, j, 2 * mp:2 * mp + 2, :],
                                ps[:, bd, :].rearrange("p (a w) -> p a w", a=2), ALU.add),
                      reads=[("ps", bd), ("y2", j, 2 * mp), ("y2", j, 2 * mp + 1)], writes=[("y2", j, 2 * mp), ("y2", j, 2 * mp + 1)])
            else:
                di = dcnt[0] % 3
                dcnt[0] += 1
                S.add("act", ACT(dsb[di][:], ps[:, bd, :], AF.Identity), reads=[("ps", bd)], writes=[("dsb", di)])
                S.add("pool", TT(y2[:, j, 2 * mp:2 * mp + 2, :], y2[:, j, 2 * mp:2 * mp + 2, :],
                                 dsb[di][:].rearrange("p (a w) -> p a w", a=2), ALU.add),
                      reads=[("dsb", di), ("y2", j, 2 * mp), ("y2", j, 2 * mp + 1)], writes=[("y2", j, 2 * mp), ("y2", j, 2 * mp + 1)])

    def stage_ln2(j):
        def ln2_out(m, t, tkey, j=j):
            S.add("act", ACT(y2[:, j, m, :], t, AF.Identity, scale=pv[:, C_LN2G + m:C_LN2G + m + 1], bias=pv[:, C_LN2B + m:C_LN2B + m + 1]),
                  reads=[tkey, "pv"], writes=[("y2", j, m)])
        use_ln2_set(ln2_count[0] % 2)
        ln2_count[0] += 1
        layer_norm(lambda m, j=j: y2[:, j, m, :], lambda m, j=j: ("y2", j, m), CW, C_LN2G, C_LN2B, ln2_out)
        todo.append(lambda j=j: S.add("sp", DMA(outd[j].rearrange("(m p) w -> p m w", p=128), y2[:, j, :, :]),
                                      reads=[("y2", j, m) for m in range(KC)], writes=[("out", j)], dsem=ds_out))

    seq = [(g, j) for g in range(ngrp) for j in range(nj)]
    pend_ln = []
    for t, (g, j) in enumerate(seq):
        stage_up(g, j, t)
        if t >= 1:
            gp, jp = seq[t - 1]
            stage_down(gp, jp, t - 1)
            if jp == nj - 1 and gp + 2 < ngrp:
                load_group(gp + 2)
            if gp == ngrp - 1:
                pend_ln.append(jp)
        n_prev = len(todo)
        if len(pend_ln) >= 2:
            stage_ln2(pend_ln.pop(0))
        drain(n_prev)
    gp, jp = seq[-1]
    stage_down(gp, jp, len(seq) - 1)
    pend_ln.append(jp)
    for jj in pend_ln:
        n_prev = len(todo)
        stage_ln2(jj)
        drain(n_prev)
    flush()
    fin = [("out", j) for j in range(nj)]
    if debug:
        fin += [("dbgc", j) for j in range(nj)]
    S.add("dve", lambda e: e.memset(dv[:, NV_ - 1:NV_], 0.0), reads=fin, writes=["fin"])

    with nc.Block() as block:
        S.emit(block, sg)
    print("ops", len(S.ops), "waits", S.nwaits, "sems", nsem[0])
    return nc


def _host_layout(x, lambda_q1, lambda_k1, lambda_q2, lambda_k2, attn_norm_g, conv_w, ln1_g, ln1_b,
                 ffn_conv_w, ffn_conv_b, ln2_g, ln2_b):
    f = np.float32
    pv_base = np.zeros((128, NP_), f)

    def pcols(v):
        return np.ascontiguousarray(np.asarray(v, f).reshape(-1, 128).T)

    pv_base[:, C_LN1G:C_LN1G + 8] = pcols(ln1_g[0])
    pv_base[:, C_LN1B:C_LN1B + 8] = pcols(ln1_b[0])
    pv_base[:, C_LN2G:C_LN2G + 8] = pcols(ln2_g[0])
    pv_base[:, C_LN2B:C_LN2B + 8] = pcols(ln2_b[0])
    pv_base[:, C_ANG] = np.asarray(attn_norm_g[0], f)
    cw = np.asarray(conv_w[0], f)
    for fc in range(4):
        for tap in range(3):
            pv_base[:, C_CW + 3 * fc + tap] = cw[tap, 128 * fc:128 * fc + 128]
    fw = np.asarray(ffn_conv_w[0], f)
    for hc in range(NHC):
        for tap in range(3):
            pv_base[:, C_FCW + 3 * hc + tap] = fw[tap, 128 * hc:128 * hc + 128]
    pv_base[:, C_FCB:C_FCB + NHC] = pcols(ffn_conv_b[0])
    lam = np.concatenate([np.asarray(v[0], f) for v in (lambda_q1, lambda_k1, lambda_q2, lambda_k2)])
    pv_base[:, C_LAM:C_LAM + 256] = lam[None, :]

    in_maps_part = []
    for core in range(8):
        bi, c = core // 2, core % 2
        xT = np.zeros((D, SEQ + 4), f)
        xT[:, 4:] = np.asarray(x[bi], f).T
        xo = np.zeros((NJ, 128, KC * XW), f)
        xt = np.zeros((NJ, 128, KC * CW), f)
        for j in range(NJ):
            t0 = 256 * (2 * j + c)
            t1 = 256 * (2 * j + 1 - c)
            w = xT[:, t0:t0 + XW].reshape(KC, 128, XW).transpose(1, 0, 2)
            xo[j] = w.reshape(128, KC * XW)
            w = xT[:, 4 + t1:4 + t1 + CW].reshape(KC, 128, CW).transpose(1, 0, 2)
            xt[j] = w.reshape(128, KC * CW)
        pvc = pv_base.copy()
        pvc[:, C_HFLAG:C_HFLAG + NJ] = 1.0
        if c == 0:
            pvc[:, C_HFLAG] = 0.0
        cst = np.zeros((128, 128 + 9 * QW), f)
        cst[:, 0:128] = np.eye(128, dtype=f)
        p = np.arange(128)[:, None]
        q = np.arange(QW)[None, :]
        for ms in range(2):
            for r in range(4):
                if r < 2:
                    key_rel = 256 * c + 128 * r + p
                else:
                    key_rel = 256 * (1 - c) + 128 * (r - 2) + p
                qry_rel = 256 * c - 2 + q
                allowed = key_rel <= qry_rel
                if ms == 0 and c == 0:
                    allowed = allowed | (q < 2)
                o = 128 + (ms * 4 + r) * QW
                cst[:, o:o + QW] = np.where(allowed, 0.0, NEG).astype(f)
        in_maps_part.append({"xo": xo, "xt": xt, "pv": pvc, "cst": cst})
    return in_maps_part


_CACHE = {}


def kernel(x, w_in, lambda_q1, lambda_k1, lambda_q2, lambda_k2, attn_norm_g, conv_w, w_out,
           ln1_g, ln1_b, ffn_w_up, ffn_conv_w, ffn_conv_b, ffn_w_down, ln2_g, ln2_b, _debug=False):
    x = np.asarray(x)
    parts = _host_layout(x, lambda_q1, lambda_k1, lambda_q2, lambda_k2, attn_norm_g, conv_w, ln1_g, ln1_b,
                         ffn_conv_w, ffn_conv_b, ln2_g, ln2_b)
    shared = {
        "w_in": np.ascontiguousarray(np.asarray(w_in, np.float32)[0]),
        "w_out": np.ascontiguousarray(np.asarray(w_out, np.float32)[0]),
        "w_up": np.ascontiguousarray(np.asarray(ffn_w_up, np.float32)[0]),
        "w_down": np.ascontiguousarray(np.asarray(ffn_w_down, np.float32)[0]),
    }
    in_maps = [dict(p, **shared) for p in parts]
    key = ("nc", bool(_debug))
    if key not in _CACHE:
        _CACHE[key] = build_program(debug=_debug)
    nc = _CACHE[key]
    res = run_bass_kernel_spmd(nc, in_maps, core_ids=list(range(8)))
    out = np.zeros((NB, SEQ, D), np.float32)
    for core in range(8):
        bi, c = core // 2, core % 2
        for j in range(NJ):
            t0 = 256 * (2 * j + c)
            out[bi, t0:t0 + CW, :] = res.results[core]["out%d" % j].T
    if _debug:
        return out, res
    return out
```

```python
import math
import numpy as np
import concourse.bass as bass
import concourse.mybir as mybir
from concourse.bass_utils import run_bass_kernel_spmd

F32 = mybir.dt.float32
BF16 = mybir.dt.bfloat16
AF = mybir.ActivationFunctionType
ALU = mybir.AluOpType
AX = mybir.AxisListType

D = 1024
SEQ = 4096
NB = 4
KC = 8
NJ = 8
CW = 256
QW = 258
XW = 260
DFF = 2816
NHC = 22
GRP = 4
GROUPS = [(0, 2), (2, 4), (6, 4), (10, 4), (14, 4), (18, 4)]
ALPHA = (2.0 * 1) ** 0.25
LAM_INIT = 0.8 - 0.6 * math.exp(0.0)
LN_EPS = 1e-5
RMS_EPS = 1e-5
NEG = -30000.0

C_LN1G, C_LN1B, C_LN2G, C_LN2B = 0, 8, 16, 24
C_ANG = 32
C_CW = 33
C_FCW = 45
C_FCB = 111
C_LAM = 133
C_HFLAG = 389
NP_ = 397
V_AG1, V_AB1, V_GSC, V_NEGLAM, V_T0, V_T1, V_E0, V_E1 = 0, 8, 16, 17, 18, 19, 20, 21
NV_ = 24

SEM_ROLL = 1000


class DSem:
    def __init__(self, sem):
        self.sem = sem
        self.count = 0


class Op:
    __slots__ = ("eng", "fn", "deps", "dsem", "ticket", "signal", "clock", "idx", "ndma")


class Sched:
    ENGS = ("pe", "act", "dve", "pool", "sp")

    def __init__(self, nc, same_engine_sync=True):
        self.nc = nc
        self.ops = []
        self.lastw = {}
        self.rd_eng = {}
        self.rd_dma = {}
        self.same_engine_sync = same_engine_sync

    def add(self, eng, fn, reads=(), writes=(), dsem=None, ndma=1):
        op = Op()
        op.eng, op.fn, op.dsem, op.ndma = eng, fn, dsem, ndma
        op.idx = len(self.ops)
        op.ticket = None
        op.signal = dsem is not None
        op.clock = None
        deps = set()
        for r in reads:
            w = self.lastw.get(r)
            if w is not None:
                deps.add(w)
        for w_ in writes:
            w = self.lastw.get(w_)
            if w is not None:
                deps.add(w)
            for i in self.rd_eng.get(w_, {}).values():
                deps.add(i)
            for i in self.rd_dma.get(w_, ()):
                deps.add(i)
        for w_ in writes:
            self.lastw[w_] = op.idx
            self.rd_eng[w_] = {}
            self.rd_dma[w_] = []
        for r in reads:
            if r in writes:
                continue
            if dsem is not None:
                self.rd_dma.setdefault(r, []).append(op.idx)
            else:
                self.rd_eng.setdefault(r, {})[eng] = op.idx
        keep = set()
        best = {}
        bestd = {}
        for d in deps:
            p = self.ops[d]
            if p.dsem is not None:
                k_ = id(p.dsem)
                if k_ not in bestd or bestd[k_] < d:
                    bestd[k_] = d
                continue
            if p.eng == eng and dsem is None and (eng == "pe" or not self.same_engine_sync):
                continue
            if p.eng not in best or best[p.eng] < d:
                best[p.eng] = d
        keep.update(best.values())
        keep.update(bestd.values())
        op.deps = keep
        for d in keep:
            self.ops[d].signal = True
        self.ops.append(op)
        return op

    def emit(self, block, sems):
        cnt = {e: 0 for e in self.ENGS}
        semlist = {e: [] for e in self.ENGS}
        for op in self.ops:
            if op.dsem is not None:
                op.dsem.count += 16 * op.ndma
                op.ticket = ("d", op.dsem.sem, op.dsem.count)
            elif op.signal:
                c = cnt[op.eng]
                si, v = divmod(c, SEM_ROLL)
                if si >= len(semlist[op.eng]):
                    semlist[op.eng].append(next(sems))
                op.ticket = ("e", semlist[op.eng][si], v + 1)
                cnt[op.eng] = c + 1
        streams = {e: [] for e in self.ENGS}
        clock = {e: {} for e in self.ENGS}
        nwaits = 0
        for op in self.ops:
            ck = clock[op.eng]
            st = streams[op.eng]
            for d in sorted(op.deps):
                p = self.ops[d]
                t = p.ticket
                key = id(t[1])
                if ck.get(key, 0) >= t[2]:
                    continue
                st.append(("w", t[1], t[2]))
                nwaits += 1
                ck[key] = t[2]
                if p.clock is not None:
                    for k, v in p.clock.items():
                        if ck.get(k, 0) < v:
                            ck[k] = v
            st.append(("o", op))
            if op.signal:
                op.clock = dict(ck)
        self.nwaits = nwaits
        print("signal counts", cnt, "dma max", max([op.ticket[2] for op in self.ops if op.dsem is not None] + [0]))
        self.streams = streams

        def run(e):
            def body(eng):
                for item in streams[e]:
                    if item[0] == "w":
                        eng.wait_ge(item[1], item[2])
                    else:
                        op = item[1]
                        ins = op.fn(eng)
                        if op.dsem is not None:
                            if not isinstance(ins, (list, tuple)):
                                ins = [ins]
                            assert len(ins) == op.ndma
                            for i_ in ins:
                                i_.then_inc(op.dsem.sem, 16)
                        elif op.signal:
                            ins.then_inc(op.ticket[1], 1)
            return body

        block.tensor(run("pe"))
        block.scalar(run("act"))
        block.vector(run("dve"))
        block.gpsimd(run("pool"))
        block.sync(run("sp"))


def MM(out, lhsT, rhs, start=True, stop=True):
    return lambda e: e.matmul(out, lhsT=lhsT, rhs=rhs, start=start, stop=stop)


def ACT(out, in_, func, scale=None, bias=None):
    kw = {}
    if scale is not None:
        kw["scale"] = scale
    if bias is not None:
        kw["bias"] = bias
    return lambda e: e.activation(out=out, in_=in_, func=func, **kw)


def TT(out, a, b, op):
    return lambda e: e.tensor_tensor(out=out, in0=a, in1=b, op=op)


def TS(out, a, s1, op0, s2=None, op1=None):
    if op1 is None:
        return lambda e: e.tensor_scalar(out=out, in0=a, scalar1=s1, scalar2=None, op0=op0)
    return lambda e: e.tensor_scalar(out=out, in0=a, scalar1=s1, scalar2=s2, op0=op0, op1=op1)


def STT(out, a, s, b, op0, op1):
    return lambda e: e.scalar_tensor_tensor(out=out, in0=a, scalar=s, in1=b, op0=op0, op1=op1)


def CP(out, in_):
    return lambda e: e.tensor_copy(out=out, in_=in_)


def RECIP(out, in_):
    return lambda e: e.reciprocal(out=out, in_=in_)


def DMA(out, in_):
    return lambda e: e.dma_start(out=out, in_=in_)


def DMAS(pairs):
    return lambda e: [e.dma_start(out=o, in_=i) for (o, i) in pairs]


class Arena:
    def __init__(self, nc, base, top):
        self.nc, self.cur, self.top = nc, base, top
        self.n = 0

    def alloc(self, name, shape, dtype):
        esz = 4 if dtype == F32 else 2
        size = esz
        for s in shape[1:]:
            size *= s
        off = (self.cur + 31) // 32 * 32
        self.cur = off + size
        self.last_off = off
        assert self.cur <= self.top, (name, self.cur, self.top)
        self.n += 1
        return self.nc.alloc_sbuf_tensor_at("%s_%d" % (name, self.n), list(shape), dtype, offset=off)


def build_program(nj=NJ, debug=False, limit=99):
    import os
    KX = set(os.environ.get('KX', '').split(','))
    nc = bass.Bass("TRN2", target_bir_lowering=False)

    def dram(name, shape, dtype, kind):
        return nc.dram_tensor(name, list(shape), dtype, kind=kind).ap()

    xo = dram("xo", [NJ, 128, KC * XW], F32, "ExternalInput")
    xt = dram("xt", [NJ, 128, KC * CW], F32, "ExternalInput")
    pvd = dram("pv", [128, NP_], F32, "ExternalInput")
    cstd = dram("cst", [128, 128 + 9 * QW], F32, "ExternalInput")
    w_in = dram("w_in", [D, 3072], F32, "ExternalInput")
    w_out = dram("w_out", [D, D], F32, "ExternalInput")
    w_up = dram("w_up", [D, 2 * DFF], F32, "ExternalInput")
    w_down = dram("w_down", [DFF, D], F32, "ExternalInput")
    outd = [dram("out%d" % j, [D, CW], F32, "ExternalOutput") for j in range(NJ)]
    scr_b = [dram("scr_b%d" % j, [128, KC * QW], BF16, "ExternalOutput") for j in range(NJ)]
    scr_y = [dram("scr_y%d" % j, [128, KC * CW], F32, "ExternalOutput") for j in range(NJ)]
    if debug:
        dbg_cat = dram("dbg_cat", [NJ, 128, KC * QW], F32, "ExternalOutput")
        dbg_h1 = dram("dbg_h1", [NJ, 128, KC * CW], F32, "ExternalOutput")

    nsem = [0]

    def semgen():
        while True:
            nsem[0] += 1
            yield nc.alloc_semaphore("sm%d" % nsem[0])

    sg = semgen()

    def dsem():
        return DSem(next(sg))

    S = Sched(nc)
    base0 = (nc.sbuf_base + 63) // 64 * 64
    top0 = nc.sbuf_top - 64
    arena_t = nc.alloc_sbuf_tensor("arena", [128, top0 - base0], mybir.dt.uint8)
    A = Arena(nc, base0, top0)
    ps = nc.alloc_psum_tensor("ps", [128, 8, 512], F32)

    pv = A.alloc("pv", [128, NP_], F32)
    dv = A.alloc("dv", [128, NV_], F32)
    lamt = A.alloc("lamt", [128, 128], F32)
    ones_f = A.alloc("ones_f", [128, 128], F32)
    ones_b = A.alloc("ones_b", [128, 128], BF16)
    cstb = A.alloc("cstb", [128, 128 + 9 * QW], BF16)
    ident_b = cstb[:, 0:128]

    def maskb(ms, r):
        o = 128 + (ms * 4 + r) * QW
        return cstb[:, o:o + QW]

    mark_persist = A.cur

    winb = A.alloc("winb", [128, KC, 3072], BF16)
    winb_off = A.last_off
    woutb = A.alloc("woutb", [128, KC, D], BF16)
    KT = A.alloc("KT", [128, 4, SEQ], BF16)
    Vt = A.alloc("Vt", [128, 32, 512], BF16)
    xof = A.alloc("xof", [128, KC, XW], F32)
    xob = [A.alloc("xob", [128, KC, XW], BF16) for _ in range(2)]
    xtb = [A.alloc("xtb", [128, KC, CW], BF16) for _ in range(2)]
    qTz = A.alloc("qTz", [128, 4, 2, QW], BF16)
    catT = A.alloc("catT", [128, KC, QW], BF16)
    hc_sb = [A.alloc("hc_sb", [128, XW], F32) for _ in range(2)]
    u_sb = [A.alloc("u_sb", [128, XW], F32) for _ in range(2)]
    acc_sb = [A.alloc("acc_sb", [128, QW], F32) for _ in range(2)]
    pbuf = [A.alloc("pbuf", [128, 2, QW], BF16) for _ in range(4)]
    tmpW = [A.alloc("tmpW", [128, 2, QW], F32) for _ in range(6)]
    tmpW_off = A.last_off - 5 * 2 * QW * 4 - 5 * 32

    def TH(k, h):
        return (tmpW[k][:, h, :], ("tmp", k, h))
    y1 = A.alloc("y1", [128, KC, QW], F32)
    lnt = [TH(0, 0), TH(0, 1), TH(1, 0), TH(1, 1), TH(2, 0)]
    tt_ = [TH(3, 0), TH(3, 1)]
    sqr = [TH(4, 0), TH(4, 1)]
    h1bs = A.alloc("h1bs", [128, KC, QW], BF16)
    print("phase1 sbuf end", A.cur, "of", top0)
    p1_end = A.cur
    A.cur = winb_off
    wg, wv, wd = [], [], []
    for _ in range(2):
        wg.append(A.alloc("wg", [128, KC, 512], BF16))
        wv.append(A.alloc("wv", [128, KC, 512], BF16))
        wd.append(A.alloc("wd", [128, GRP, D], BF16))
    wff_end = A.cur
    assert wff_end <= winb_off + KC * 3072 * 2
    A.cur = wff_end
    h1T = A.alloc("h1T", [128, NJ, KC, QW], BF16)
    y2 = A.alloc("y2", [128, NJ, KC, CW], F32)
    actT = [A.alloc("actT", [128, GRP, CW], BF16) for _ in range(2)]
    ca = [A.alloc("ca", [128, CW], F32) for _ in range(3)]
    cs = [A.alloc("cs", [128, CW], F32) for _ in range(3)]
    dcnt = [0]
    sqr2 = [[A.alloc("sqr2", [128, CW], F32) for _ in range(2)]] * 2
    lnt2 = [[A.alloc("lnt2", [128, CW], F32) for _ in range(5)] for _k in range(2)]
    tt2 = [[A.alloc("ttmp2", [128, CW], F32) for _ in range(2)] for _k in range(2)]
    print("phase2 sbuf end", A.cur, "of", top0)
    p2_end = A.cur
    ds_h1T = [dsem() for _ in range(nj)]
    ds_y2 = [dsem() for _ in range(nj)]
    ds_out = dsem()
    p1_done = [("scr_b", nj - 1), ("scr_y", nj - 1)]
    assert A.cur <= tmpW_off, (A.cur, tmpW_off)
    dead_keys = ["wout", "xof"] + [("cat", k) for k in range(KC)] + [("qT", h) for h in range(4)] + [("xob", 0), ("xob", 1), ("xtb", 0), ("xtb", 1)]
    A.cur = p1_end

    ds_pv, ds_cst = dsem(), dsem()
    ds_win = {"k": dsem(), "v": dsem(), "q": dsem(), 1: dsem()}
    ds_wout = dsem()
    ds_xof = dsem()
    ds_xob = [dsem(), dsem()]
    ds_xtb = [dsem(), dsem()]
    ds_sb, ds_sy = dsem(), dsem()
    ds_dbg = dsem()

    ds_wff = [dsem(), dsem()]

    def load_group(g):
        b = g % 2
        h0, nh = GROUPS[g]
        pairs = []
        for kc in range(KC):
            pairs.append((wg[b][:, kc, 0:128 * nh], w_up[128 * kc:128 * kc + 128, 128 * h0:128 * (h0 + nh)]))
            pairs.append((wv[b][:, kc, 0:128 * nh], w_up[128 * kc:128 * kc + 128, DFF + 128 * h0:DFF + 128 * (h0 + nh)]))
        for i in range(nh):
            pairs.append((wd[b][:, i, :], w_down[128 * (h0 + i):128 * (h0 + i) + 128, :]))
        S.add("pool", DMAS(pairs), writes=[("wff", b), ("win", "k"), ("win", "v"), ("win", "q"), ("win", 1)],
              dsem=ds_wff[b], ndma=len(pairs))

    def issue_reload(j):
        S.add("sp", DMA(h1T[:, j, :, :], scr_b[j].rearrange("p (k w) -> p k w", k=KC)),
              reads=[("scr_b", j)], writes=[("h1T", j)] + dead_keys, dsem=ds_h1T[j])
        S.add("sp", DMA(y2[:, j, :, :], scr_y[j].rearrange("p (k w) -> p k w", k=KC)),
              reads=[("scr_y", j)], writes=[("y2", j, m) for m in range(KC)] + dead_keys, dsem=ds_y2[j])


    rb = [0]

    def bank():
        b = rb[0]
        rb[0] = (b + 1) % 8
        return b

    S.add("sp", DMA(pv[:], pvd[:, :]), writes=["pv"], dsem=ds_pv)
    S.add("pool", DMAS([(cstb[:, 0:1024], cstd[:, 0:1024]), (cstb[:, 1024:128 + 9 * QW], cstd[:, 1024:128 + 9 * QW])]),
          writes=["cst"], dsem=ds_cst, ndma=2)
    S.add("dve", lambda e: e.memset(ones_f[:], 1.0), writes=["ones_f"])
    S.add("dve", lambda e: e.memset(ones_b[:], 1.0), writes=["ones_b"])
    S.add("dve", lambda e: e.memset(qTz[:].rearrange("p a b w -> p (a b w)"), 0.0), writes=[("qT", h) for h in range(4)])

    def xo3(j):
        return xo[j].rearrange("p (k w) -> p k w", k=KC)

    def xt3(j):
        return xt[j].rearrange("p (k w) -> p k w", k=KC)

    def issue_loads(j):
        b = j % 2
        S.add("pool", DMA(xob[b][:], xo3(j)), writes=[("xob", b)], dsem=ds_xob[b])
        S.add("pool", DMA(xtb[b][:], xt3(j)), writes=[("xtb", b)], dsem=ds_xtb[b])

    issue_loads(0)
    for grp, c0, cn in (("k", 512, 512), ("v", 1024, 512), ("q", 0, 512), (1, 1536, 1536)):
        S.add("pool", DMAS([(winb[:, kc, c0:c0 + cn], w_in[128 * kc:128 * kc + 128, c0:c0 + cn]) for kc in range(KC)]),
              writes=[("win", grp)], dsem=ds_win[grp], ndma=KC)
    S.add("sp", DMA(xof[:], xo3(0)), writes=["xof"], dsem=ds_xof)
    S.add("pool", DMAS([(woutb[:, kc, :], w_out[128 * kc:128 * kc + 128, :]) for kc in range(KC)]),
          writes=["wout"], dsem=ds_wout, ndma=KC)

    S.add("dve", TT(lamt[:, 0:64], pv[:, C_LAM:C_LAM + 64], pv[:, C_LAM + 64:C_LAM + 128], ALU.mult), reads=["pv"], writes=["lamt0"])
    S.add("dve", TT(lamt[:, 64:128], pv[:, C_LAM + 128:C_LAM + 192], pv[:, C_LAM + 192:C_LAM + 256], ALU.mult), reads=["pv"], writes=["lamt1"])
    S.add("dve", lambda e: e.reduce_sum(out=dv[:, V_T0:V_T0 + 1], in_=lamt[:, 0:64], axis=AX.X), reads=["lamt0"], writes=["dvt0"])
    S.add("dve", lambda e: e.reduce_sum(out=dv[:, V_T1:V_T1 + 1], in_=lamt[:, 64:128], axis=AX.X), reads=["lamt1"], writes=["dvt1"])
    S.add("act", ACT(dv[:, V_E0:V_E0 + 2], dv[:, V_T0:V_T0 + 2], AF.Exp), reads=["dvt0", "dvt1"], writes=["dve01"])
    S.add("dve", TT(dv[:, V_NEGLAM:V_NEGLAM + 1], dv[:, V_E1:V_E1 + 1], dv[:, V_E0:V_E0 + 1], ALU.subtract), reads=["dve01"], writes=["neglam0"])
    S.add("dve", TS(dv[:, V_NEGLAM:V_NEGLAM + 1], dv[:, V_NEGLAM:V_NEGLAM + 1], -LAM_INIT, ALU.add), reads=["neglam0"], writes=["neglam"])
    S.add("dve", TS(dv[:, V_GSC:V_GSC + 1], pv[:, C_ANG:C_ANG + 1], 1.0 - LAM_INIT, ALU.mult), reads=["pv"], writes=["gsc"])
    S.add("dve", TS(dv[:, V_AG1:V_AG1 + 8], pv[:, C_LN1G:C_LN1G + 8], ALPHA, ALU.mult), reads=["pv"], writes=["ag1"])
    S.add("dve", TS(dv[:, V_AB1:V_AB1 + 8], pv[:, C_LN1B:C_LN1B + 8], ALPHA, ALU.mult), reads=["pv"], writes=["ab1"])
    neglam = dv[:, V_NEGLAM:V_NEGLAM + 1]
    gsc = dv[:, V_GSC:V_GSC + 1]

    evac_flip = [0]

    def evac(out_ap, in_ap, reads, writes):
        evac_flip[0] ^= 1
        if evac_flip[0]:
            S.add("dve", CP(out_ap, in_ap), reads=reads, writes=writes)
        else:
            S.add("act", ACT(out_ap, in_ap, AF.Identity), reads=reads, writes=writes)

    todo = []

    def drain(n):
        for _ in range(min(n, len(todo))):
            todo.pop(0)()

    def flush():
        drain(len(todo))

    def layer_norm(src, src_keys, width, gcol, bcol, emit_out):
        b1, b2 = bank(), bank()
        s1 = ps[:, b1, 0:width]
        s2 = ps[:, b2, 0:width]
        for m in range(KC):
            sq, sqk = sqr[m % 2]
            S.add("act", ACT(sq[:, 0:width], src(m), AF.Square), reads=[src_keys(m)], writes=[sqk])
            S.add("pe", MM(s1, ones_f[:], src(m), start=(m == 0), stop=(m == KC - 1)), reads=["ones_f", src_keys(m)], writes=[("ps", b1)])
            S.add("pe", MM(s2, ones_f[:], sq[:, 0:width], start=(m == 0), stop=(m == KC - 1)), reads=["ones_f", sqk], writes=[("ps", b2)])
        (mean, kmean), (msq, kmsq), (var, kvar), (rstd, krstd), (nmr, knmr) = [(t[:, 0:width], k) for (t, k) in lnt]

        def mean_var():
            S.add("dve", TS(mean, s1, 1.0 / D, ALU.mult), reads=[("ps", b1)], writes=[kmean])
            S.add("dve", TT(msq, mean, mean, ALU.mult), reads=[kmean], writes=[kmsq])
            S.add("dve", STT(var, s2, 1.0 / D, msq, ALU.mult, ALU.subtract), reads=[("ps", b2), kmsq], writes=[kvar])
            S.add("dve", TS(var, var, LN_EPS, ALU.add), reads=[kvar], writes=[kvar])
            S.add("act", ACT(var, var, AF.Ln), reads=[kvar], writes=[kvar])
            S.add("act", ACT(rstd, var, AF.Exp, scale=-0.5), reads=[kvar], writes=[krstd])
            S.add("dve", STT(nmr, mean, -1.0, rstd, ALU.mult, ALU.mult), reads=[kmean, krstd], writes=[knmr])
        todo.append(mean_var)

        def norm_step(m, t, tk):
            def go():
                S.add("dve", TT(t, src(m), rstd, ALU.mult), reads=[src_keys(m), krstd], writes=[tk])
                S.add("dve", TT(t, t, nmr, ALU.add), reads=[tk, knmr], writes=[tk])
                emit_out(m, t, tk)
            return go

        for m in range(KC):
            t, tk = tt_[m % 2]
            todo.append(norm_step(m, t[:, 0:width], tk))

    for j in range(nj if limit >= 1 else 0):
        b = j % 2
        xb, xtb_ = xob[b], xtb[b]
        if j + 1 < nj:
            issue_loads(j + 1)
        for h in range(4):
            bk = bank()
            col = 512 + 128 * h
            for half, (src, skey) in enumerate([(lambda kc: xb[:, kc, 4:XW], ("xob", b)), (lambda kc: xtb_[:, kc, :], ("xtb", b))]):
                for kc in range(KC):
                    S.add("pe", MM(ps[:, bk, 256 * half:256 * half + 256], winb[:, kc, col:col + 128], src(kc),
                                   start=(kc == 0), stop=(kc == KC - 1)),
                          reads=[("win", "k"), skey], writes=[("ps", bk)])
            evac(KT[:, h, 512 * j:512 * j + 512], ps[:, bk, :], [("ps", bk)], [("KT", h, j)])
            drain(1)
        if limit < 1.2:
            continue
        for blk in range(4):
            bk = bank()
            for kc in range(KC):
                if blk < 2:
                    lt = xb[:, kc, 4 + 128 * blk:4 + 128 * blk + 128]
                    skey = ("xob", b)
                else:
                    lt = xtb_[:, kc, 128 * (blk - 2):128 * (blk - 2) + 128]
                    skey = ("xtb", b)
                S.add("pe", MM(ps[:, bk, :], lt, winb[:, kc, 1024:1536], start=(kc == 0), stop=(kc == KC - 1)),
                      reads=[("win", "v"), skey], writes=[("ps", bk)])
            evac(Vt[:, 4 * j + blk, :], ps[:, bk, :], [("ps", bk)], [("V", 4 * j + blk)])
            drain(1)
        if limit < 1.4:
            continue
        for h in range(4):
            bk = bank()
            for kc in range(KC):
                S.add("pe", MM(ps[:, bk, 0:QW], winb[:, kc, 128 * h:128 * h + 128], xb[:, kc, 2:XW],
                               start=(kc == 0), stop=(kc == KC - 1)),
                      reads=[("win", "q"), ("xob", b)], writes=[("ps", bk)])
            S.add("dve", CP(qTz[0:64, h, 0, :], ps[0:64, bk, 0:QW]), reads=[("ps", bk)], writes=[("qT", h)])
            S.add("act", ACT(qTz[64:128, h, 1, :], ps[64:128, bk, 0:QW], AF.Identity), reads=[("ps", bk)], writes=[("qT", h)])
            drain(1)
        if limit < 1.6:
            continue
        def conv_branch(xb=xb, b=b):
            for fc in range(4):
                bB, bC, bH = bank(), bank(), bank()
                for (bk, c0) in [(bH, 2560), (bC, 2048), (bB, 1536)]:
                    col = c0 + 128 * fc
                    for kc in range(KC):
                        S.add("pe", MM(ps[:, bk, 0:XW], winb[:, kc, col:col + 128], xb[:, kc, :],
                                       start=(kc == 0), stop=(kc == KC - 1)),
                              reads=[("win", 1), ("xob", b)], writes=[("ps", bk)])
                t = fc % 2
                hs, us, ac = hc_sb[t], u_sb[t], acc_sb[t]
                S.add("act", ACT(hs[:], ps[:, bH, 0:XW], AF.Identity), reads=[("ps", bH)], writes=[("hc_sb", t)])
                S.add("dve", TT(us[:], ps[:, bC, 0:XW], hs[:], ALU.mult), reads=[("ps", bC), ("hc_sb", t)], writes=[("u_sb", t)])
                cw = C_CW + 3 * fc
                S.add("dve", TS(ac[:], us[:, 2:XW], pv[:, cw + 2:cw + 3], ALU.mult), reads=[("u_sb", t), "pv"], writes=[("acc", t)])
                S.add("dve", STT(ac[:], us[:, 1:XW - 1], pv[:, cw + 1:cw + 2], ac[:], ALU.mult, ALU.add), reads=[("u_sb", t), "pv", ("acc", t)], writes=[("acc", t)])
                S.add("dve", STT(ac[:], us[:, 0:XW - 2], pv[:, cw:cw + 1], ac[:], ALU.mult, ALU.add), reads=[("u_sb", t), "pv", ("acc", t)], writes=[("acc", t)])
                S.add("dve", TT(catT[:, 4 + fc, :], ps[:, bB, 2:XW], ac[:], ALU.mult), reads=[("ps", bB), ("acc", t)], writes=[("cat", 4 + fc)])


        if j > 0:
            conv_branch()
        if j == nj - 1 and limit >= 4:
            load_group(0)
            load_group(1)
        if limit < 2:
            continue
        flush()
        nkb = 4 * j + 4
        if 'noattn' in KX:
            nkb = 0
        ms = 0 if j == 0 else 1
        pcount = [0]
        accb = [4, 5, 6, 7]
        LAG = 2
        NPB = len(pbuf)

        def epilogue_a(h):
            osb, lsb = tmpW[0], tmpW[1]
            ko = [("tmp", 0, 0), ("tmp", 0, 1)]
            kl = [("tmp", 1, 0), ("tmp", 1, 1)]
            o_, ok_ = TH(2 + h // 2, h % 2)
            sq_, sqk_ = TH(4 + h // 2, h % 2)
            S.add("dve", CP(osb[:], ps[:, 4:6, 0:QW]), reads=[("ps", 4), ("ps", 5)], writes=ko)
            S.add("act", ACT(lsb[:], ps[:, 6:8, 0:QW], AF.Identity), reads=[("ps", 6), ("ps", 7)], writes=kl)
            S.add("dve", RECIP(lsb[:], lsb[:]), reads=kl, writes=kl)
            S.add("dve", TT(osb[:], osb[:], lsb[:], ALU.mult), reads=ko + kl, writes=ko)
            S.add("dve", STT(o_, osb[:, 1, :], neglam, osb[:, 0, :], ALU.mult, ALU.add), reads=ko + ["neglam"], writes=[ok_])
            S.add("dve", TT(sq_, o_, o_, ALU.mult), reads=[ok_], writes=[sqk_])

        def do_pv(h, kb, pi):
            pb = pbuf[pi]
            first, last = (kb == 0), (kb == nkb - 1)
            for mp in range(2):
                S.add("pe", MM(ps[:, accb[mp], 0:QW], Vt[:, kb, 128 * h:128 * h + 128], pb[:, mp, :], start=first, stop=last),
                      reads=[("V", kb), ("pbuf", pi)], writes=[("ps", accb[mp])])
                S.add("pe", MM(ps[:, accb[2 + mp], 0:QW], ones_b[:], pb[:, mp, :], start=first, stop=last),
                      reads=["ones_b", ("pbuf", pi)], writes=[("ps", accb[2 + mp])])
            if last:
                epilogue_a(h)

        pendq = []
        items = [(h, kb) for h in range(4 if nkb else 0) for kb in range(nkb)]
        for idx, (h, kb) in enumerate(items):
            sp_ = (idx % 2) * 2
            masked = (kb >= 4 * j) or ('allmask' in KX)
            kj = kb // 4
            for mp in range(2):
                sdst = ps[:, sp_ + mp, 0:QW]
                if masked:
                    S.add("pe", MM(sdst, ident_b, (maskb(ms, kb - 4 * j) if kb >= 4 * j else maskb(2, 0)), start=True, stop=False),
                          reads=["cst"], writes=[("ps", sp_ + mp)])
                S.add("pe", MM(sdst, KT[:, h, 128 * kb:128 * kb + 128], qTz[:, h, mp, :],
                               start=(not masked), stop=True),
                      reads=[("KT", h, kj), ("qT", h)], writes=[("ps", sp_ + mp)])
            pi = pcount[0] % NPB
            pcount[0] += 1
            S.add("act", ACT(pbuf[pi][:], ps[:, sp_:sp_ + 2, 0:QW], AF.Exp, scale=0.125),
                  reads=[("ps", sp_), ("ps", sp_ + 1)], writes=[("pbuf", pi)])
            pendq.append((h, kb, pi))
            if len(pendq) > LAG:
                do_pv(*pendq.pop(0))
        while pendq:
            do_pv(*pendq.pop(0))

        if j == 0:
            conv_branch()
        nh_ = 4 if nkb else 0

        def rms_stats():
            for h in range(nh_):
                sq_, sqk_ = TH(4 + h // 2, h % 2)
                S.add("pe", MM(ps[:, h, 0:QW], ones_f[:], sq_, start=True, stop=True), reads=["ones_f", sqk_], writes=[("ps", h)])

        def rms_chain():
            for h in range(nh_):
                lnv, lk_ = TH(4 + h // 2, h % 2)
                S.add("dve", TS(lnv, ps[:, h, 0:QW], 1.0 / 128.0, ALU.mult, RMS_EPS, ALU.add), reads=[("ps", h)], writes=[lk_])
                S.add("act", ACT(lnv, lnv, AF.Ln), reads=[lk_], writes=[lk_])
                S.add("act", ACT(lnv, lnv, AF.Exp, scale=-0.5), reads=[lk_], writes=[lk_])
            for h in range(nh_):
                o_, ok_ = TH(2 + h // 2, h % 2)
                lnv, lk_ = TH(4 + h // 2, h % 2)
                S.add("dve", STT(catT[:, h, :], o_, gsc, lnv, ALU.mult, ALU.mult), reads=[ok_, "gsc", lk_], writes=[("cat", h)])

        if limit < 3:
            rms_stats()
            rms_chain()
            continue
        ob = [4, 5, 6, 7, 0, 1, 2, 3]

        def op_half(m, kcs, first, last):
            for ki, kc in enumerate(kcs):
                S.add("pe", MM(ps[:, ob[m], 0:QW], woutb[:, kc, 128 * m:128 * m + 128], catT[:, kc, :],
                               start=(first and ki == 0), stop=(last and ki == len(kcs) - 1)),
                      reads=["wout", ("cat", kc)], writes=[("ps", ob[m])])

        def op_evac(m):
            S.add("dve", STT(y1[:, m, :], xof[:, m, 2:XW], ALPHA, ps[:, ob[m], 0:QW], ALU.mult, ALU.add),
                  reads=["xof", ("ps", ob[m])], writes=[("y1", m)])

        for m in range(0, 4):
            op_half(m, [4, 5, 6, 7], True, False)
        rms_stats()
        rms_chain()
        for m in range(4, 8):
            op_half(m, [4, 5, 6, 7], True, False)
        for m in range(0, 4):
            op_half(m, [0, 1, 2, 3], False, True)
            op_evac(m)
        for m in range(4, 8):
            op_half(m, [0, 1, 2, 3], False, True)
            op_evac(m)

        if j + 1 < nj:
            S.add("sp", DMA(xof[:], xo3(j + 1)), writes=["xof"], dsem=ds_xof)
        elif limit >= 4:
            for jj in range(nj - 1):
                issue_reload(jj)

        def ln1_out(m, t, tkey, j=j):
            if True:
                S.add("act", ACT(h1bs[:, m, :], t, AF.Identity, scale=pv[:, C_LN1G + m:C_LN1G + m + 1], bias=pv[:, C_LN1B + m:C_LN1B + m + 1]),
                      reads=[tkey, "pv"], writes=[("h1bs", m)])
            else:
                S.add("pool", TS(h1bs[:, m, :], t, pv[:, C_LN1G + m:C_LN1G + m + 1], ALU.mult, pv[:, C_LN1B + m:C_LN1B + m + 1], ALU.add),
                      reads=[tkey, "pv"], writes=[("h1bs", m)])
            S.add("act", ACT(y1[:, m, 2:QW], t[:, 2:QW], AF.Identity, scale=dv[:, V_AG1 + m:V_AG1 + m + 1], bias=dv[:, V_AB1 + m:V_AB1 + m + 1]),
                  reads=[tkey, "ag1", "ab1"], writes=[("y1", m)])

        layer_norm(lambda m: y1[:, m, :], lambda m: ("y1", m), QW, C_LN1G, C_LN1B, ln1_out)
        def ln1_tail(j=j):
            S.add("dve", TS(h1bs[:, :, 0:2], h1bs[:, :, 0:2], pv[:, C_HFLAG + j:C_HFLAG + j + 1], ALU.mult),
                  reads=[("h1bs", m) for m in range(KC)] + ["pv"], writes=[("h1bs", m) for m in range(KC)])
            S.add("sp", DMA(scr_b[j].rearrange("p (k w) -> p k w", k=KC), h1bs[:]), reads=[("h1bs", m) for m in range(KC)],
                  writes=[("scr_b", j)], dsem=ds_sb)
            S.add("sp", DMA(scr_y[j].rearrange("p (k w) -> p k w", k=KC), y1[:, :, 2:QW]), reads=[("y1", m) for m in range(KC)],
                  writes=[("scr_y", j)], dsem=ds_sy)
        todo.append(ln1_tail)
        flush()

    if limit < 4:
        fin = ["pv", "cst", ("win", "k"), ("win", "v"), ("win", "q"), ("win", 1), "wout", "xof", ("xob", 0), ("xob", 1), ("xtb", 0), ("xtb", 1)]
        fin += [("scr_b", j) for j in range(nj)] + [("scr_y", j) for j in range(nj)]
        if debug:
            fin += [("dbgc", j) for j in range(nj)]
        S.add("dve", lambda e: e.memset(dv[:, NV_ - 1:NV_], 0.0), reads=fin, writes=["fin"])
        with nc.Block() as block:
            S.emit(block, sg)
        print("ops", len(S.ops), "waits", S.nwaits, "sems", nsem[0])
        return nc

    def use_ln2_set(k):
        sqr[:] = [(t[:], ("sqr2", i)) for i, t in enumerate(sqr2[k])]
        lnt[:] = [(t[:], ("lnt2", k, i)) for i, t in enumerate(lnt2[k])]
        tt_[:] = [(t[:], ("tt2", k, i)) for i, t in enumerate(tt2[k])]
    ln2_count = [0]
    ngrp = len(GROUPS)
    issue_reload(nj - 1)

    cidx = [0]

    def stage_up(g, j, t):
        b = g % 2
        h0, nh = GROUPS[g]
        at = actT[t % 2]
        slots = []
        for i in range(nh + 1):
            if i < nh:
                hc = h0 + i
                bg, bv = bank(), bank()
                for kc in range(KC):
                    S.add("pe", MM(ps[:, bg, 0:QW], wg[b][:, kc, 128 * i:128 * i + 128], h1T[:, j, kc, :],
                                   start=(kc == 0), stop=(kc == KC - 1)),
                          reads=[("wff", b), ("h1T", j)], writes=[("ps", bg)])
                for kc in range(KC):
                    S.add("pe", MM(ps[:, bv, 0:CW], wv[b][:, kc, 128 * i:128 * i + 128], h1T[:, j, kc, 2:QW],
                                   start=(kc == 0), stop=(kc == KC - 1)),
                          reads=[("wff", b), ("h1T", j)], writes=[("ps", bv)])
                ci = cidx[0] % 3
                cidx[0] += 1
                a_ = ca[ci]
                fw_ = C_FCW + 3 * hc
                S.add("act", ACT(a_[:], ps[:, bg, 0:CW], AF.Identity, scale=pv[:, fw_:fw_ + 1], bias=pv[:, C_FCB + hc:C_FCB + hc + 1]),
                      reads=[("ps", bg), "pv"], writes=[("ca", ci)])
                S.add("dve", STT(a_[:], ps[:, bg, 1:CW + 1], pv[:, fw_ + 1:fw_ + 2], a_[:], ALU.mult, ALU.add),
                      reads=[("ps", bg), "pv", ("ca", ci)], writes=[("ca", ci)])
                S.add("dve", STT(a_[:], ps[:, bg, 2:CW + 2], pv[:, fw_ + 2:fw_ + 3], a_[:], ALU.mult, ALU.add),
                      reads=[("ps", bg), "pv", ("ca", ci)], writes=[("ca", ci)])
                slots.append((ci, bv))
            if i >= 1:
                ci, bv = slots[i - 1]
                S.add("act", ACT(cs[ci][:], ca[ci][:], AF.Silu), reads=[("ca", ci)], writes=[("cs", ci)])
                S.add("dve", TT(at[:, i - 1, :], ps[:, bv, 0:CW], cs[ci][:], ALU.mult), reads=[("ps", bv), ("cs", ci)], writes=[("actT", t % 2, i - 1)])

    def stage_down(g, j, t):
        b = g % 2
        h0, nh = GROUPS[g]
        at = actT[t % 2]
        for mp in range(4):
            bd = bank()
            for half in range(2):
                m = 2 * mp + half
                for i in range(nh):
                    S.add("pe", MM(ps[:, bd, 256 * half:256 * half + 256], wd[b][:, i, 128 * m:128 * m + 128], at[:, i, :],
                                   start=(i == 0), stop=(i == nh - 1)),
                          reads=[("wff", b), ("actT", t % 2, i)], writes=[("ps", bd)])
            if 'pooladd' not in KX:
                S.add("dve", TT(y2[:, j, 2 * mp:2 * mp + 2, :], y2[:, j, 2 * mp:2 * mp + 2, :],
                                ps[:, bd, :].rearrange("p (a w) -> p a w", a=2), ALU.add),
                      reads=[("ps", bd), ("y2", j, 2 * mp), ("y2", j, 2 * mp + 1)], writes=[("y2", j, 2 * mp), ("y2", j, 2 * mp + 1)])
            else:
                di = dcnt[0] % 3
                dcnt[0] += 1
                S.add("act", ACT(dsb[di][:], ps[:, bd, :], AF.Identity), reads=[("ps", bd)], writes=[("dsb", di)])
                S.add("pool", TT(y2[:, j, 2 * mp:2 * mp + 2, :], y2[:, j, 2 * mp:2 * mp + 2, :],
                                 dsb[di][:].rearrange("p (a w) -> p a w", a=2), ALU.add),
                      reads=[("dsb", di), ("y2", j, 2 * mp), ("y2", j, 2 * mp + 1)], writes=[("y2", j, 2 * mp), ("y2", j, 2 * mp + 1)])

    def stage_ln2(j):
        def ln2_out(m, t, tkey, j=j):
            S.add("act", ACT(y2[:, j, m, :], t, AF.Identity, scale=pv[:, C_LN2G + m:C_LN2G + m + 1], bias=pv[:, C_LN2B + m:C_LN2B + m + 1]),
                  reads=[tkey, "pv"], writes=[("y2", j, m)])
        use_ln2_set(ln2_count[0] % 2)
        ln2_count[0] += 1
        layer_norm(lambda m, j=j: y2[:, j, m, :], lambda m, j=j: ("y2", j, m), CW, C_LN2G, C_LN2B, ln2_out)
        todo.append(lambda j=j: S.add("sp", DMA(outd[j].rearrange("(m p) w -> p m w", p=128), y2[:, j, :, :]),
                                      reads=[("y2", j, m) for m in range(KC)], writes=[("out", j)], dsem=ds_out))

    seq = [(g, j) for g in range(ngrp) for j in range(nj)]
    pend_ln = []
    for t, (g, j) in enumerate(seq):
        stage_up(g, j, t)
        if t >= 1:
            gp, jp = seq[t - 1]
            stage_down(gp, jp, t - 1)
            if jp == nj - 1 and gp + 2 < ngrp:
                load_group(gp + 2)
            if gp == ngrp - 1:
                pend_ln.append(jp)
        n_prev = len(todo)
        if len(pend_ln) >= 2:
            stage_ln2(pend_ln.pop(0))
            n_prev += 1
        drain(n_prev)
    gp, jp = seq[-1]
    stage_down(gp, jp, len(seq) - 1)
    pend_ln.append(jp)
    for jj in pend_ln:
        n_prev = len(todo)
        stage_ln2(jj)
        drain(n_prev + 1)
    flush()
    fin = [("out", j) for j in range(nj)]
    if debug:
        fin += [("dbgc", j) for j in range(nj)]
    S.add("dve", lambda e: e.memset(dv[:, NV_ - 1:NV_], 0.0), reads=fin, writes=["fin"])

    with nc.Block() as block:
        S.emit(block, sg)
    print("ops", len(S.ops), "waits", S.nwaits, "sems", nsem[0])
    return nc


def _host_layout(x, lambda_q1, lambda_k1, lambda_q2, lambda_k2, attn_norm_g, conv_w, ln1_g, ln1_b,
                 ffn_conv_w, ffn_conv_b, ln2_g, ln2_b):
    f = np.float32
    pv_base = np.zeros((128, NP_), f)

    def pcols(v):
        return np.ascontiguousarray(np.asarray(v, f).reshape(-1, 128).T)

    pv_base[:, C_LN1G:C_LN1G + 8] = pcols(ln1_g[0])
    pv_base[:, C_LN1B:C_LN1B + 8] = pcols(ln1_b[0])
    pv_base[:, C_LN2G:C_LN2G + 8] = pcols(ln2_g[0])
    pv_base[:, C_LN2B:C_LN2B + 8] = pcols(ln2_b[0])
    pv_base[:, C_ANG] = np.asarray(attn_norm_g[0], f)
    cw = np.asarray(conv_w[0], f)
    for fc in range(4):
        for tap in range(3):
            pv_base[:, C_CW + 3 * fc + tap] = cw[tap, 128 * fc:128 * fc + 128]
    fw = np.asarray(ffn_conv_w[0], f)
    for hc in range(NHC):
        for tap in range(3):
            pv_base[:, C_FCW + 3 * hc + tap] = fw[tap, 128 * hc:128 * hc + 128]
    pv_base[:, C_FCB:C_FCB + NHC] = pcols(ffn_conv_b[0])
    lam = np.concatenate([np.asarray(v[0], f) for v in (lambda_q1, lambda_k1, lambda_q2, lambda_k2)])
    pv_base[:, C_LAM:C_LAM + 256] = lam[None, :]

    in_maps_part = []
    for core in range(8):
        bi, c = core // 2, core % 2
        xT = np.zeros((D, SEQ + 4), f)
        xT[:, 4:] = np.asarray(x[bi], f).T
        xo = np.zeros((NJ, 128, KC * XW), f)
        xt = np.zeros((NJ, 128, KC * CW), f)
        for j in range(NJ):
            t0 = 256 * (2 * j + c)
            t1 = 256 * (2 * j + 1 - c)
            w = xT[:, t0:t0 + XW].reshape(KC, 128, XW).transpose(1, 0, 2)
            xo[j] = w.reshape(128, KC * XW)
            w = xT[:, 4 + t1:4 + t1 + CW].reshape(KC, 128, CW).transpose(1, 0, 2)
            xt[j] = w.reshape(128, KC * CW)
        pvc = pv_base.copy()
        pvc[:, C_HFLAG:C_HFLAG + NJ] = 1.0
        if c == 0:
            pvc[:, C_HFLAG] = 0.0
        cst = np.zeros((128, 128 + 9 * QW), f)
        cst[:, 0:128] = np.eye(128, dtype=f)
        p = np.arange(128)[:, None]
        q = np.arange(QW)[None, :]
        for ms in range(2):
            for r in range(4):
                if r < 2:
                    key_rel = 256 * c + 128 * r + p
                else:
                    key_rel = 256 * (1 - c) + 128 * (r - 2) + p
                qry_rel = 256 * c - 2 + q
                allowed = key_rel <= qry_rel
                if ms == 0 and c == 0:
                    allowed = allowed | (q < 2)
                o = 128 + (ms * 4 + r) * QW
                cst[:, o:o + QW] = np.where(allowed, 0.0, NEG).astype(f)
        in_maps_part.append({"xo": xo, "xt": xt, "pv": pvc, "cst": cst})
    return in_maps_part


_CACHE = {}


def kernel(x, w_in, lambda_q1, lambda_k1, lambda_q2, lambda_k2, attn_norm_g, conv_w, w_out,
           ln1_g, ln1_b, ffn_w_up, ffn_conv_w, ffn_conv_b, ffn_w_down, ln2_g, ln2_b, _debug=False):
    x = np.asarray(x)
    parts = _host_layout(x, lambda_q1, lambda_k1, lambda_q2, lambda_k2, attn_norm_g, conv_w, ln1_g, ln1_b,
                         ffn_conv_w, ffn_conv_b, ln2_g, ln2_b)
    shared = {
        "w_in": np.ascontiguousarray(np.asarray(w_in, np.float32)[0]),
        "w_out": np.ascontiguousarray(np.asarray(w_out, np.float32)[0]),
        "w_up": np.ascontiguousarray(np.asarray(ffn_w_up, np.float32)[0]),
        "w_down": np.ascontiguousarray(np.asarray(ffn_w_down, np.float32)[0]),
    }
    in_maps = [dict(p, **shared) for p in parts]
    key = ("nc", bool(_debug))
    if key not in _CACHE:
        _CACHE[key] = build_program(debug=_debug)
    nc = _CACHE[key]
    res = run_bass_kernel_spmd(nc, in_maps, core_ids=list(range(8)))
    out = np.zeros((NB, SEQ, D), np.float32)
    for core in range(8):
        bi, c = core // 2, core % 2
        for j in range(NJ):
            t0 = 256 * (2 * j + c)
            out[bi, t0:t0 + CW, :] = res.results[core]["out%d" % j].T
    if _debug:
        return out, res
    return out
```
